# Optimizing a Trainium2 kernel written in Bass

```python
import jax
import jax.numpy as jnp
from jax import lax
import numpy as np

D_MODEL = 1024
BATCH = 1
SEQ = 16384
DEPTH = 4

GRID_W = 64
CTX_LEN = 256
N_HEADS = 8
N_KV_HEADS = 2
HEAD_DIM = 64
Q_GROUP = N_HEADS // N_KV_HEADS
Q_W = N_HEADS * HEAD_DIM
KV_W = N_KV_HEADS * HEAD_DIM
ATTN_BLOCK = 128
ROPE_THETA = 10000.0
FNET_GROUPS = 4
FNET_GROUP_DIM = 64
FNET_WIDTH = FNET_GROUPS * FNET_GROUP_DIM
CONV_WIDTH = 256
CONV_KERNEL = 31
SC_WIDTH = 256
SC_KERNEL = 3
N_BRANCHES = 4
D_FF = 2816
N_MOD = 9
EPS = 1e-6
IN_SPLITS = (Q_W, KV_W, KV_W, FNET_WIDTH, 2 * CONV_WIDTH, 3 * SC_WIDTH, N_BRANCHES * D_MODEL)
IN_W = sum(IN_SPLITS)

kernel_name = "hybrid_parallel_branch_diffusion_block"


def rmsnorm(x, g):
    x32 = x.astype(jnp.float32)
    y = x32 * lax.rsqrt(jnp.mean(x32 * x32, axis=-1, keepdims=True) + EPS)
    return (y * g.astype(jnp.float32)).astype(x.dtype)


def layernorm(x, g, b):
    x32 = x.astype(jnp.float32)
    mu = jnp.mean(x32, axis=-1, keepdims=True)
    var = jnp.mean(jnp.square(x32 - mu), axis=-1, keepdims=True)
    y = (x32 - mu) * lax.rsqrt(var + EPS)
    return (y * g.astype(jnp.float32) + b.astype(jnp.float32)).astype(x.dtype)


def split_in(p):
    idx = [int(i) for i in np.cumsum(IN_SPLITS)[:-1]]
    return jnp.split(p, idx, axis=-1)


def axial_rope_tables(n_tokens):
    n_rows = n_tokens // GRID_W
    row = jnp.broadcast_to(jnp.arange(n_rows)[:, None], (n_rows, GRID_W)).reshape(-1)
    col = jnp.broadcast_to(jnp.arange(GRID_W)[None, :], (n_rows, GRID_W)).reshape(-1)
    axis_dim = HEAD_DIM // 2
    inv_freq = ROPE_THETA ** (-jnp.arange(0, axis_dim, 2, dtype=jnp.float32) / axis_dim)
    pos = jnp.stack([row, col], axis=-1).astype(jnp.float32)
    ang = pos[:, :, None] * inv_freq
    return jnp.cos(ang), jnp.sin(ang)


def apply_axial_rope(x, cos, sin):
    B, L, H, _ = x.shape
    xr = x.astype(jnp.float32).reshape(B, L, H, 2, 2, HEAD_DIM // 4)
    x1, x2 = xr[..., 0, :], xr[..., 1, :]
    cs = cos[None, :, None]
    sn = sin[None, :, None]
    out = jnp.stack([x1 * cs - x2 * sn, x1 * sn + x2 * cs], axis=-2)
    return out.reshape(x.shape).astype(x.dtype)


def to_heads(t, n_heads):
    B, L, _ = t.shape
    return t.reshape(B, L, n_heads, HEAD_DIM)


def attend(q, k, v):
    s = jnp.einsum('bqkgd,bskd->bkgqs', q, k).astype(jnp.float32) * (HEAD_DIM ** -0.5)
    p = jax.nn.softmax(s, axis=-1).astype(v.dtype)
    return jnp.einsum('bkgqs,bskd->bqkgd', p, v)


def latent_attention(q, k, v, k_ctx, v_ctx):
    B, L = q.shape[:2]
    keys = jnp.concatenate([k_ctx, k], axis=1)
    vals = jnp.concatenate([v_ctx, v], axis=1)
    n_blk = L // ATTN_BLOCK
    qb = q.reshape(B, n_blk, ATTN_BLOCK, N_KV_HEADS, Q_GROUP, HEAD_DIM).transpose(1, 0, 2, 3, 4, 5)
    ob = lax.map(lambda qq: attend(qq, keys, vals), qb)
    return ob.transpose(1, 0, 2, 3, 4, 5).reshape(B, L, Q_W)


def context_attention(q, k, v):
    B, L = q.shape[:2]
    o = attend(q.reshape(B, L, N_KV_HEADS, Q_GROUP, HEAD_DIM), k, v)
    return o.reshape(B, L, Q_W)


def depthwise_conv(x, w):
    k_w = w.shape[0]
    pad = (k_w - 1) // 2
    return lax.conv_general_dilated(
        x, w[:, None, :].astype(x.dtype), window_strides=(1,), padding=[(pad, pad)],
        dimension_numbers=('NWC', 'WIO', 'NWC'), feature_group_count=x.shape[-1])


def fourier_mix(f):
    B, L, _ = f.shape
    fg = f.astype(jnp.float32).reshape(B, L, FNET_GROUPS, FNET_GROUP_DIM)
    y = jnp.fft.fft2(fg, axes=(1, 3), norm='ortho').real
    return y.reshape(B, L, FNET_WIDTH).astype(f.dtype)


def conformer_conv(a, dw_w, dw_b, ln_g, ln_b):
    a1, a2 = jnp.split(a, 2, axis=-1)
    h = a1 * jax.nn.sigmoid(a2)
    h = depthwise_conv(h, dw_w) + dw_b
    h = layernorm(h, ln_g, ln_b)
    return jax.nn.silu(h)


def short_gated_conv(s, w):
    gb, gc, h = jnp.split(s, 3, axis=-1)
    return gb * depthwise_conv(gc * h, w)


def merge_branches(attn, f, cg, sc, g, lw):
    B, L, _ = attn.shape
    y_a = attn @ lw['w_attn_out']
    y_b = fourier_mix(f) @ lw['w_fnet']
    y_c = conformer_conv(cg, lw['conv_dw_w'], lw['conv_dw_b'], lw['conv_ln_g'], lw['conv_ln_b']) @ lw['w_conv_out']
    y_d = short_gated_conv(sc, lw['sc_conv_w']) @ lw['w_sc_out']
    gates = jax.nn.sigmoid(g + lw['b_gate']).reshape(B, L, N_BRANCHES, D_MODEL)
    merged = gates[:, :, 0] * y_a + gates[:, :, 1] * y_b + gates[:, :, 2] * y_c + gates[:, :, 3] * y_d
    return merged @ lw['w_o']


def modulated_norm(x, mod, slot, g):
    return rmsnorm(x, g) * (1 + mod[:, :, 3 * slot + 1]) + mod[:, :, 3 * slot]


def ffn_half(x, mod, slot, g, w13, w2):
    h = modulated_norm(x, mod, slot, g)
    a, b = jnp.split(h @ w13, 2, axis=-1)
    return x + 0.5 * mod[:, :, 3 * slot + 2] * ((jax.nn.silu(a) * b) @ w2)


def setup_inputs(seed: int = 0) -> dict:
    key = jax.random.key(seed)
    ks = jax.random.split(key, 24)
    nrm = jax.random.normal
    f32 = jnp.float32
    D = D_MODEL
    return {
        'x': nrm(ks[0], (BATCH, SEQ, D), f32),
        'c': nrm(ks[1], (BATCH, D), f32),
        'ctx': nrm(ks[2], (BATCH, CTX_LEN, D), f32),
        'c_ctx': nrm(ks[3], (D,), f32),
        'w_ada': nrm(ks[4], (DEPTH, D, N_MOD * D), f32) * (0.5 * D ** -0.5),
        'b_ada': nrm(ks[5], (DEPTH, N_MOD * D), f32) * 0.01,
        'norm_g': 1.0 + 0.01 * nrm(ks[6], (DEPTH, 3, D), f32),
        'ffn_w13': nrm(ks[7], (DEPTH, 2, D, 2 * D_FF), f32) * D ** -0.5,
        'ffn_w2': nrm(ks[8], (DEPTH, 2, D_FF, D), f32) * D_FF ** -0.5,
        'w_in': nrm(ks[9], (DEPTH, D, IN_W), f32) * D ** -0.5,
        'b_gate': nrm(ks[10], (DEPTH, N_BRANCHES * D), f32) * 0.01,
        'q_norm_g': 1.0 + 0.01 * nrm(ks[11], (DEPTH, HEAD_DIM), f32),
        'k_norm_g': 1.0 + 0.01 * nrm(ks[12], (DEPTH, HEAD_DIM), f32),
        'w_attn_out': nrm(ks[13], (DEPTH, Q_W, D), f32) * Q_W ** -0.5,
        'w_fnet': nrm(ks[14], (DEPTH, FNET_WIDTH, D), f32) * FNET_WIDTH ** -0.5,
        'conv_dw_w': nrm(ks[15], (DEPTH, CONV_KERNEL, CONV_WIDTH), f32) * CONV_KERNEL ** -0.5,
        'conv_dw_b': nrm(ks[16], (DEPTH, CONV_WIDTH), f32) * 0.01,
        'conv_ln_g': 1.0 + 0.01 * nrm(ks[17], (DEPTH, CONV_WIDTH), f32),
        'conv_ln_b': nrm(ks[18], (DEPTH, CONV_WIDTH), f32) * 0.01,
        'w_conv_out': nrm(ks[19], (DEPTH, CONV_WIDTH, D), f32) * CONV_WIDTH ** -0.5,
        'sc_conv_w': nrm(ks[20], (DEPTH, SC_KERNEL, SC_WIDTH), f32) * SC_KERNEL ** -0.5,
        'w_sc_out': nrm(ks[21], (DEPTH, SC_WIDTH, D), f32) * SC_WIDTH ** -0.5,
        'w_o': nrm(ks[22], (DEPTH, D, D), f32) * D ** -0.5,
        'final_norm_g': 1.0 + 0.01 * nrm(ks[23], (D,), f32),
    }


def reference(x, c, ctx, c_ctx, w_ada, b_ada, norm_g, ffn_w13, ffn_w2, w_in, b_gate,
              q_norm_g, k_norm_g, w_attn_out, w_fnet, conv_dw_w, conv_dw_b, conv_ln_g,
              conv_ln_b, w_conv_out, sc_conv_w, w_sc_out, w_o, final_norm_g):
    B, L, D = x.shape
    cos, sin = axial_rope_tables(L)
    cx = ctx
    for l in range(DEPTH):
        last = l == DEPTH - 1
        lw = {
            'w_attn_out': w_attn_out[l], 'w_fnet': w_fnet[l], 'conv_dw_w': conv_dw_w[l],
            'conv_dw_b': conv_dw_b[l], 'conv_ln_g': conv_ln_g[l], 'conv_ln_b': conv_ln_b[l],
            'w_conv_out': w_conv_out[l], 'sc_conv_w': sc_conv_w[l], 'w_sc_out': w_sc_out[l],
            'b_gate': b_gate[l], 'w_o': w_o[l],
        }
        mod_x = (jax.nn.silu(c) @ w_ada[l] + b_ada[l]).reshape(B, 1, N_MOD, D)
        mod_c = (jax.nn.silu(c_ctx) @ w_ada[l] + b_ada[l]).reshape(1, 1, N_MOD, D)

        x = ffn_half(x, mod_x, 0, norm_g[l, 0], ffn_w13[l, 0], ffn_w2[l, 0])
        cx = ffn_half(cx, mod_c, 0, norm_g[l, 0], ffn_w13[l, 0], ffn_w2[l, 0])

        u_x = modulated_norm(x, mod_x, 1, norm_g[l, 1])
        u_c = modulated_norm(cx, mod_c, 1, norm_g[l, 1])
        q_x, k_x, v_x, f_x, cg_x, sc_x, g_x = split_in(u_x @ w_in[l])
        if last:
            k_c, v_c = jnp.split(u_c @ w_in[l, :, Q_W:Q_W + 2 * KV_W], 2, axis=-1)
        else:
            q_c, k_c, v_c, f_c, cg_c, sc_c, g_c = split_in(u_c @ w_in[l])
        k_c = rmsnorm(to_heads(k_c, N_KV_HEADS), k_norm_g[l])
        v_c = to_heads(v_c, N_KV_HEADS)

        qh = apply_axial_rope(rmsnorm(to_heads(q_x, N_HEADS), q_norm_g[l]), cos, sin)
        kh = apply_axial_rope(rmsnorm(to_heads(k_x, N_KV_HEADS), k_norm_g[l]), cos, sin)
        attn_x = latent_attention(qh, kh, to_heads(v_x, N_KV_HEADS), k_c, v_c)
        x = x + mod_x[:, :, 5] * merge_branches(attn_x, f_x, cg_x, sc_x, g_x, lw)

        if not last:
            qc = rmsnorm(to_heads(q_c, N_HEADS), q_norm_g[l])
            attn_c = context_attention(qc, k_c, v_c)
            cx = cx + mod_c[:, :, 5] * merge_branches(attn_c, f_c, cg_c, sc_c, g_c, lw)

        x = ffn_half(x, mod_x, 2, norm_g[l, 2], ffn_w13[l, 1], ffn_w2[l, 1])
        if not last:
            cx = ffn_half(cx, mod_c, 2, norm_g[l, 2], ffn_w13[l, 1], ffn_w2[l, 1])

    return rmsnorm(x, final_norm_g)
```

```python
import numpy as np
from contextlib import ExitStack
import concourse.bass as bass
import concourse.mybir as mybir
from concourse.bass_utils import run_bass_kernel_spmd

F32 = mybir.dt.float32
BF16 = mybir.dt.bfloat16
AF = mybir.ActivationFunctionType
ALU = mybir.AluOpType

NCORE = 8
D = 1024
L = 16384
LT = 2048
CT = 256
NT = LT + CT
NH = 32
DEPTH = 4
DFF = 2816
NJ = 22
EPS = 1e-6
BLK = [(0, 512, 'x'), (512, 512, 'x'), (1024, 512, 'x'), (1536, 512, 'x'), (2048, 256, 'c')]
LBLK = BLK[:4]
HBLK = (NT, NH, 'x')
JPARTS = [(0, 6), (6, 6), (12, 5), (17, 5)]

TO = {}
_o = 0
for _n, _w in [('modx', 72), ('modc', 72), ('normg', 24), ('bg', 32), ('cw', 62), ('cb', 2), ('lg', 2),
               ('lb', 2), ('scw', 6), ('qg', 1), ('kg', 1)]:
    TO[_n] = _o
    _o += _w
NTAB = _o


class KB:
    def __init__(self, nc, es):
        self.nc = nc
        self.es = es
        self.E = dict(pe=nc.tensor, act=nc.scalar, dve=nc.vector, pool=nc.gpsimd, sp=nc.sync)
        self.sem = {e: es.enter_context(nc.semaphore("s_" + e)) for e in self.E}
        self.cnt = {e: 0 for e in self.E}
        self.seen = {e: {} for e in self.E}
        self.prog = {e: [] for e in self.E}
        self.W = {}
        self.R = {}
        self.pend = {e: [[], []] for e in self.E}
        self.dsems = {}

    def _semh(self, sk):
        if isinstance(sk, tuple):
            return self.dsems[sk[1]][0]
        return self.sem[sk]

    def _need(self, e, reads, writes, skip=None):
        toks = []
        for k in reads:
            t = self.W.get(k)
            if t:
                toks.append(t)
        for k in writes:
            t = self.W.get(k)
            if t:
                toks.append(t)
            toks.extend(self.R.get(k, {}).items())
        d = {}
        for sk, v in toks:
            if sk == skip:
                continue
            if sk == e and e == 'pe':
                continue
            if self.seen[e].get(sk, 0) >= v:
                continue
            d[sk] = max(d.get(sk, 0), v)
        for sk, v in d.items():
            self.seen[e][sk] = v
        return list(d.items())

    def _reg(self, tok, reads, writes):
        sk, v = tok
        for k in reads:
            r = self.R.setdefault(k, {})
            r[sk] = max(r.get(sk, 0), v)
        for k in writes:
            self.W[k] = tok
            self.R[k] = {}

    def op(self, e, fn, reads=(), writes=(), inc=True):
        for e2 in self.E:
            if e2 != e:
                for k in list(reads) + list(writes):
                    assert k not in self.pend[e2][1], ("pending write conflict", k)
                for k in writes:
                    assert k not in self.pend[e2][0], ("pending read conflict", k)
        waits = self._need(e, reads, writes)
        if inc:
            self.cnt[e] += 1
            tok = (e, self.cnt[e])
            pr, pw = self.pend[e]
            self._reg(tok, list(reads) + pr, list(writes) + pw)
            self.pend[e] = [[], []]
        else:
            self.pend[e][0] += list(reads)
            self.pend[e][1] += list(writes)
        self.prog[e].append((waits, fn, inc, None))

    def dma(self, q, out, in_, reads=(), writes=(), sem=None, group=False):
        if sem not in self.dsems:
            self.dsems[sem] = [self.es.enter_context(self.nc.semaphore("d_" + sem)), 0]
        skip = ('d', sem) if group else None
        waits = self._need(q, reads, writes, skip=skip)
        self.dsems[sem][1] += 16
        tok = (('d', sem), self.dsems[sem][1])
        self._reg(tok, reads, writes)
        self.prog[q].append((waits, lambda e, o=out, i=in_: e.dma_start(out=o, in_=i), 'dma', sem))

    def cc(self, in_t, out_t, reads=(), writes=()):
        sem = "cc"
        if sem not in self.dsems:
            self.dsems[sem] = [self.es.enter_context(self.nc.semaphore("d_" + sem)), 0]
        waits = self._need('pool', reads, writes)
        self.dsems[sem][1] += 1
        tok = (('d', sem), self.dsems[sem][1])
        self._reg(tok, reads, writes)
        self.prog['pool'].append((waits, lambda e, i=in_t, o=out_t: e.collective_compute(
            "AllGather", ALU.bypass, replica_groups=[list(range(NCORE))], ins=[i.ap().opt()], outs=[o.ap().opt()]),
            'cc', sem))

    def barrier(self, full=True):
        for e in self.E:
            waits = []
            for e2 in self.E:
                if e2 != e and self.cnt[e2] > self.seen[e].get(e2, 0):
                    waits.append((e2, self.cnt[e2]))
                    self.seen[e][e2] = self.cnt[e2]
            for name, (s, c) in self.dsems.items():
                sk = ('d', name)
                if name == "cc" and not full:
                    continue
                if c > self.seen[e].get(sk, 0):
                    waits.append((sk, c))
                    self.seen[e][sk] = c
            if waits:
                self.prog[e].append((waits, None, False, None))

    def emit(self):
        self.barrier(full=True)
        with self.nc.Block() as block:
            for e, dec in (('sp', block.sync), ('act', block.scalar), ('dve', block.vector),
                           ('pool', block.gpsimd), ('pe', block.tensor)):
                def body(eng, e=e):
                    for waits, fn, inc, sem in self.prog[e]:
                        for sk, v in waits:
                            eng.wait_ge(self._semh(sk), v)
                        if fn is None:
                            continue
                        ins = fn(eng)
                        if inc is True:
                            ins.then_inc(self.sem[e], 1)
                        elif inc == 'dma':
                            ins.then_inc(self.dsems[sem][0], 16)
                        elif inc == 'cc':
                            ins.then_inc(self.dsems[sem][0], 1)
                dec(body)


class Arena:
    def __init__(self, nc, es, words):
        self.t = es.enter_context(nc.sbuf_tensor("arena", [128, words], F32))
        self.words = words
        self.top = 0

    def mark(self):
        return self.top

    def release(self, m):
        self.top = m

    def f32(self, n):
        o = self.top
        self.top += n
        assert self.top <= self.words, ("arena overflow", self.top)
        return self.t[:, o:o + n]

    def bf(self, n):
        w = (n + 1) // 2
        o = self.top
        self.top += w
        assert self.top <= self.words, ("arena overflow", self.top)
        return self.t[:, o:o + w].bitcast(BF16)[:, 0:n]


class Prog:
    def __init__(self, kind, dbg=()):
        self.kind = kind
        self.dbg = dbg
        self.es = ExitStack()
        nc = self.nc = bass.Bass("TRN2", target_bir_lowering=False)
        self.kb = KB(nc, self.es)
        self.ar = Arena(nc, self.es, 51712)
        self.pst = self.es.enter_context(nc.psum_tensor("ps", [128, 4096], F32))
        self.uid = 0
        self.outs = []
        self.dins = {}
        self.build()
        self.kb.emit()
        self.es.close()

    def ps(self, b, n=512, p=128):
        return self.pst[0:p, b * 512:b * 512 + n]

    def din(self, name, shape, dt=F32):
        if name not in self.dins:
            self.dins[name] = self.nc.dram_tensor(name, list(shape), dt, kind="ExternalInput").ap()
        return self.dins[name]

    def dout(self, name, shape, dt=F32):
        self.outs.append(name)
        return self.nc.dram_tensor(name, list(shape), dt, kind="ExternalOutput").ap()

    def key(self, s):
        self.uid += 1
        return "%s#%d" % (s, self.uid)

    def mm(self, out, lhsT, rhs, start, stop, reads, writes, inc):
        self.kb.op('pe', lambda e: e.matmul(out=out, lhsT=lhsT, rhs=rhs, start=start, stop=stop),
                   reads, writes, inc)

    def act(self, out, in_, func, reads, writes, bias=None, scale=None):
        kw = {}
        if bias is not None:
            kw['bias'] = bias
        if scale is not None:
            kw['scale'] = scale
        self.kb.op('act', lambda e: e.activation(out=out, in_=in_, func=func, **kw), reads, writes)

    def tt(self, out, in0, in1, op, reads, writes, eng='dve'):
        self.kb.op(eng, lambda e: e.tensor_tensor(out=out, in0=in0, in1=in1, op=op), reads, writes)

    def ts(self, out, in0, s1, s2, op0, op1, reads, writes, eng='dve'):
        if s2 is None:
            self.kb.op(eng, lambda e: e.tensor_scalar(out=out, in0=in0, scalar1=s1, scalar2=None, op0=op0),
                       reads, writes)
        else:
            self.kb.op(eng, lambda e: e.tensor_scalar(out=out, in0=in0, scalar1=s1, scalar2=s2, op0=op0, op1=op1),
                       reads, writes)

    def stt(self, out, in0, scalar, in1, op0, op1, reads, writes):
        self.kb.op('dve', lambda e: e.scalar_tensor_tensor(out=out, in0=in0, scalar=scalar, in1=in1, op0=op0, op1=op1),
                   reads, writes)

    def cp(self, out, in_, reads, writes, eng='dve'):
        self.kb.op(eng, lambda e: e.tensor_copy(out=out, in_=in_), reads, writes)

    def recip(self, out, in_, reads, writes):
        self.kb.op('dve', lambda e: e.reciprocal(out=out, in_=in_), reads, writes)

    def memset(self, ap, val, writes, eng='dve'):
        self.kb.op(eng, lambda e: e.memset(ap, val), (), writes)

    def debug(self, name, ap, reads, shape, dt=F32):
        if name in self.dbg:
            d = self.dout("dbg_" + name, shape, dt)
            self.kb.dma('sp', d, ap, reads=reads, writes=(), sem=self.key("dbg"))

    def load_tab(self, l):
        name = "tab%d" % l
        tab_d = self.din(name, [128, NTAB])
        tab = self.ar.f32(NTAB)
        self.kb.dma('sp', tab[:, 144:NTAB], tab_d[:, 144:NTAB], writes=[name], sem="tab", group=True)
        gm = self.g_mod.ap().rearrange("(r p) (w l c) -> p w l r c", p=128, w=2, l=DEPTH)
        for w in range(2):
            self.kb.dma('sp', tab[:, 72 * w:72 * w + 72].rearrange("p (r c) -> p r c", r=8), gm[:, w, l],
                        reads=["g_mod"], writes=[name], sem="tab", group=True)
        der = self.ar.f32(96)
        derv = der.rearrange("p (t s a o) -> p t s a o", t=2, s=3, a=2)
        for ti, mn in enumerate(('modx', 'modc')):
            for s in range(3):
                mo = TO[mn]
                sc = tab[:, mo + (3 * s + 1) * 8: mo + (3 * s + 1) * 8 + 8]
                gt = tab[:, mo + (3 * s + 2) * 8: mo + (3 * s + 2) * 8 + 8]
                ng = tab[:, TO['normg'] + s * 8: TO['normg'] + s * 8 + 8]
                self.stt(derv[:, ti, s, 0, :], sc, 1.0, ng, ALU.add, ALU.mult, [name], [name + "_der"])
                self.ts(derv[:, ti, s, 1, :], gt, 0.5 if s != 1 else 1.0, None, ALU.mult, None, [name], [name + "_der"])
        return dict(tab=tab, der=derv, name=name)

    def tA(self, T, ty, s, o):
        return T['der'][:, 0 if ty == 'x' else 1, s, 0, o:o + 1]

    def tG(self, T, ty, s, o):
        return T['der'][:, 0 if ty == 'x' else 1, s, 1, o:o + 1]

    def tB(self, T, ty, s, o):
        mo = TO['modx' if ty == 'x' else 'modc']
        return T['tab'][:, mo + 3 * s * 8 + o: mo + 3 * s * 8 + o + 1]

    def tcol(self, T, n, i=0):
        return T['tab'][:, TO[n] + i: TO[n] + i + 1]

    def modnorm_block(self, T, s, ty, xsrc, xkeys, dst, dkeys, n, tmp):
        kb = self.kb
        sq, rstd, t32 = tmp['sq'], tmp['rstd'], tmp['t32']
        pb = 7
        for k in range(8):
            i = k % 2
            self.act(sq[i][:, 0:n], xsrc(k), AF.Square, xkeys, ["sq%d" % i])
            self.mm(self.ps(pb, n), self.c_ones, sq[i][:, 0:n], k == 0, k == 7, ["sq%d" % i, "const"], ["ps%d" % pb], True)
        self.act(rstd[:, 0:n], self.ps(pb, n), AF.Sqrt, ["ps%d" % pb], ["rstd"], bias=self.c_eps, scale=1.0 / D)
        self.recip(rstd[:, 0:n], rstd[:, 0:n], ["rstd"], ["rstd"])
        for k in range(8):
            i = k % 2
            self.stt(t32[i][:, 0:n], xsrc(k), self.tA(T, ty, s, k), rstd[:, 0:n], ALU.mult, ALU.mult,
                     list(xkeys) + ["rstd", T['name'] + "_der"], ["t32_%d" % i])
            self.act(dst(k), t32[i][:, 0:n], AF.Identity, ["t32_%d" % i, T['name']], dkeys, bias=self.tB(T, ty, s, k))

    def norm_tmps(self):
        return dict(sq=[self.ar.bf(512), self.ar.bf(512)], rstd=self.ar.f32(512),
                    t32=[self.ar.f32(512), self.ar.f32(512)])

    def ffn(self, T, s, w13_d, w2_d, tag):
        kb = self.kb
        ar = self.ar
        m = ar.mark()
        h = ar.bf(8 * NT).rearrange("p (k t) -> p k t", k=8)
        g = ar.bf(6 * NT).rearrange("p (j t) -> p j t", j=6)
        w13b = [ar.bf(8 * 256).rearrange("p (k c) -> p k c", k=8) for _ in range(2)]
        w2b = [ar.bf(6 * 128).rearrange("p (j c) -> p j c", j=6) for _ in range(2)]
        sa = [ar.f32(512), ar.f32(512)]
        tmp = self.norm_tmps()
        xT = self.xT
        for bi, (t0, n, ty) in enumerate(BLK):
            self.modnorm_block(T, s, ty, lambda k: xT[:, k, t0:t0 + n], ["x%d" % bi],
                               lambda k: h[:, k, t0:t0 + n], ["h%d" % bi], n, tmp)
        it = 0
        w2it = 0
        for (j0, nj) in JPARTS:
            for jj in range(nj):
                j = j0 + jj
                sl = it % 2
                kb.dma('pool', w13b[sl], w13_d[j], writes=["w13b%d" % sl], sem="w13b%d" % sl)
                for bi, (t0, n, ty) in enumerate(BLK):
                    pa, pb_ = (0, 1) if (it * 5 + bi) % 2 == 0 else (2, 3)
                    for k in range(8):
                        self.mm(self.ps(pa, n), w13b[sl][:, k, 0:128], h[:, k, t0:t0 + n], k == 0, k == 7,
                                ["w13b%d" % sl, "h%d" % bi], ["ps%d" % pa], k == 7)
                    for k in range(8):
                        self.mm(self.ps(pb_, n), w13b[sl][:, k, 128:256], h[:, k, t0:t0 + n], k == 0, k == 7,
                                ["w13b%d" % sl, "h%d" % bi], ["ps%d" % pb_], k == 7)
                    si = (it * 5 + bi) % 2
                    self.act(sa[si][:, 0:n], self.ps(pa, n), AF.Silu, ["ps%d" % pa], ["sa%d" % si])
                    self.tt(g[:, jj, t0:t0 + n], sa[si][:, 0:n], self.ps(pb_, n), ALU.mult,
                            ["sa%d" % si, "ps%d" % pb_], ["g%d_%d" % (jj, bi)])
                it += 1
            for o in range(8):
                sl = w2it % 2
                kb.dma('pool', w2b[sl][:, 0:nj, :], w2_d[j0:j0 + nj, :, o, :].rearrange("j p c -> p j c"),
                       writes=["w2b%d" % sl], sem="w2b%d" % sl)
                for bi, (t0, n, ty) in enumerate(BLK):
                    py = 4 + (w2it * 5 + bi) % 2
                    for jj in range(nj):
                        self.mm(self.ps(py, n), w2b[sl][:, jj, :], g[:, jj, t0:t0 + n], jj == 0, jj == nj - 1,
                                ["w2b%d" % sl, "g%d_%d" % (jj, bi)], ["ps%d" % py], jj == nj - 1)
                    self.stt(xT[:, o, t0:t0 + n], self.ps(py, n), self.tG(T, ty, s, o), xT[:, o, t0:t0 + n],
                             ALU.mult, ALU.add, ["ps%d" % py, "x%d" % bi, T['name'] + "_der"], ["x%d" % bi])
                w2it += 1
        kb.barrier()
        ar.release(m)

    def qknorm(self, pb, n, gcol, dst, dkeys, rope_t0, tmp, T):
        qf, sq, rstd, qn, qb, t1 = tmp['qf'], tmp['sq'], tmp['rstd'], tmp['qn'], tmp['qb'], tmp['t1']
        pk = "ps%d" % pb
        self.act(qf[:, 0:n], self.ps(pb, n), AF.Copy, [pk], ["qf"])
        self.act(sq[:, 0:n], self.ps(pb, n), AF.Square, [pk], ["qsq"])
        self.mm(self.ps(6, n), self.c_bd, sq[:, 0:n], True, True, ["qsq", "const"], ["ps6"], True)
        self.act(rstd[:, 0:n], self.ps(6, n), AF.Sqrt, ["ps6"], ["qrstd"], bias=self.c_eps, scale=1.0 / 64)
        self.recip(rstd[:, 0:n], rstd[:, 0:n], ["qrstd"], ["qrstd"])
        if rope_t0 is None:
            self.stt(dst, qf[:, 0:n], gcol, rstd[:, 0:n], ALU.mult, ALU.mult, ["qf", "qrstd", T['name']], dkeys)
            return
        self.stt(qn[:, 0:n], qf[:, 0:n], gcol, rstd[:, 0:n], ALU.mult, ALU.mult, ["qf", "qrstd", T['name']], ["qn"])
        self.cp(qb[:, 0:n], qn[:, 0:n], ["qn"], ["qb"])
        self.mm(self.ps(6, n), self.c_perm, qb[:, 0:n], True, True, ["qb", "const"], ["ps6"], True)
        self.tt(t1[:, 0:n], qn[:, 0:n], self.cosT[:, rope_t0:rope_t0 + n], ALU.mult, ["qn", "rope"], ["qt1"])
        self.tt(qn[:, 0:n], self.ps(6, n), self.sinT[:, rope_t0:rope_t0 + n], ALU.mult, ["ps6", "rope"], ["qn"])
        self.tt(dst, t1[:, 0:n], qn[:, 0:n], ALU.add, ["qt1", "qn"], dkeys)

    def qk_tmps(self):
        ar = self.ar
        return dict(qf=ar.f32(512), sq=ar.bf(512), rstd=ar.f32(512), qn=ar.f32(512), qb=ar.bf(512), t1=ar.f32(512))

    def load_rope(self):
        ar = self.ar
        self.cosT = ar.f32(LT)
        self.sinT = ar.f32(LT)
        self.kb.dma('sp', self.cosT, self.rope_d[0], writes=["rope"], sem="rope", group=True)
        self.kb.dma('sp', self.sinT, self.rope_d[1], writes=["rope"], sem="rope", group=True)

    def load_consts(self):
        ar = self.ar
        kb = self.kb
        cb_d = self.din("cbf", [128, 3, 128])
        cb = ar.bf(3 * 128).rearrange("p (a c) -> p a c", a=3)
        kb.dma('pool', cb, cb_d, writes=["const"], sem="const", group=True)
        self.c_ones, self.c_bd, self.c_perm = cb[:, 0, :], cb[:, 1, :], cb[:, 2, :]
        self.c_eps = ar.f32(1)
        self.memset(self.c_eps, EPS, ["const"])
        self.c_onesf = ar.f32(128)
        self.memset(self.c_onesf, 1.0 / 256.0, ["const"])

    def part_a(self, l):
        kb, ar = self.kb, self.ar
        m0 = ar.mark()
        T = self.load_tab(l)
        w13_d = self.din("w13_%d_0" % l, [NJ, 128, 8, 256])
        w2_d = self.din("w2_%d_0" % l, [NJ, 128, 8, 128])
        wk_d = self.din("wk_%d" % l, [128, 8, 128])
        wvf_d = self.din("wvf_%d" % l, [128, 8, 384])
        self.ffn(T, 0, w13_d, w2_d, "f1")
        xk = ["x%d" % i for i in range(5)]
        kb.dma('sp', self.xdram.ap(), self.xT, reads=xk, writes=["xdram"], sem="xout")
        xe = self.b_xe.ap().rearrange("p (k t) -> p k t", k=8)
        m = ar.mark()
        u = ar.bf(8 * NT).rearrange("p (k t) -> p k t", k=8)
        wk = ar.bf(8 * 128).rearrange("p (k c) -> p k c", k=8)
        wvf = ar.bf(8 * 384).rearrange("p (k c) -> p k c", k=8)
        kb.dma('pool', wk, wk_d, writes=["wk"], sem="wk")
        kb.dma('pool', wvf, wvf_d, writes=["wvf"], sem="wvf")
        self.load_rope()
        tmp = self.norm_tmps()
        qt = self.qk_tmps()
        ko = [ar.f32(512), ar.f32(512)]
        vfo = [ar.f32(384), ar.f32(384)]
        xT = self.xT
        kT_d = self.b_k.ap()
        vf_d = self.b_vf.ap()
        for bi, (t0, n, ty) in enumerate(BLK):
            self.modnorm_block(T, 1, ty, lambda k: xT[:, k, t0:t0 + n], ["x%d" % bi],
                               lambda k: u[:, k, t0:t0 + n], ["u%d" % bi], n, tmp)
        kb.dma('sp', xe[:, :, 0:16], u[:, :, 0:16], reads=["u0"], writes=["b_xe"], sem="xe", group=True)
        kb.dma('sp', xe[:, :, 16:32], u[:, :, LT - 16:LT], reads=["u3"], writes=["b_xe"], sem="xe", group=True)
        kb.dma('sp', self.udram.ap(), u, reads=["u%d" % i for i in range(5)], writes=["udram"], sem="uout")
        for bi, (t0, n, ty) in enumerate(LBLK):
            pb = bi % 2
            for k in range(8):
                self.mm(self.ps(pb, n), wk[:, k, :], u[:, k, t0:t0 + n], k == 0, k == 7, ["wk", "u%d" % bi],
                        ["ps%d" % pb], k == 7)
            self.qknorm(pb, n, self.tcol(T, 'kg'), ko[bi % 2][:, 0:n], ["ko%d" % (bi % 2)], t0, qt, T)
            kb.dma('sp', kT_d[:, t0:t0 + n], ko[bi % 2][:, 0:n], reads=["ko%d" % (bi % 2)], writes=["b_k%d" % (bi % 2)],
                   sem="kout%d" % (bi % 2))
        for tt_ in range(LT // 128):
            pb = 2 + tt_ % 2
            bi = tt_ // 4
            for k in range(8):
                self.mm(self.ps(pb, 384), u[:, k, tt_ * 128:(tt_ + 1) * 128], wvf[:, k, :], k == 0, k == 7,
                        ["wvf", "u%d" % bi], ["ps%d" % pb], k == 7)
            self.act(vfo[tt_ % 2], self.ps(pb, 384), AF.Copy, ["ps%d" % pb], ["vfo%d" % (tt_ % 2)])
            kb.dma('sp', vf_d[tt_ * 128:(tt_ + 1) * 128, :], vfo[tt_ % 2], reads=["vfo%d" % (tt_ % 2)],
                   writes=["b_vf%d" % (tt_ % 2)], sem="vfout%d" % (tt_ % 2))
        kb.cc(self.b_xe, self.g_xe, reads=["b_xe"], writes=["g_xe"])
        kb.cc(self.b_k, self.g_k, reads=["b_k0", "b_k1"], writes=["g_k"])
        kb.cc(self.b_vf, self.g_vf, reads=["b_vf0", "b_vf1"], writes=["g_vf"])
        kb.barrier()
        ar.release(m0)

    def mod_phase(self):
        kb, ar = self.kb, self.ar
        m = ar.mark()
        cT_d = self.din("cT", [128, 8, 2])
        w_d = self.din("wada", [36, 128, 8, 128])
        b_d = self.din("bada", [128, 36])
        cT = ar.f32(16).rearrange("p (k w) -> p k w", k=8)
        sc = ar.f32(16).rearrange("p (k w) -> p k w", k=8)
        bs = ar.f32(36)
        os_ = ar.f32(72).rearrange("p (w g) -> p w g", w=2)
        wb = [ar.f32(1024).rearrange("p (k n) -> p k n", k=8) for _ in range(3)]
        kb.dma('sp', cT, cT_d, writes=["cT"], sem="cT")
        kb.dma('sp', bs, b_d, writes=["bs"], sem="bs")
        self.act(sc, cT, AF.Silu, ["cT"], ["sc"])
        for g in range(36):
            sl = g % 3
            kb.dma('sp', wb[sl], w_d[g], writes=["wb%d" % sl], sem="wb%d" % sl)
            for k in range(8):
                self.mm(self.pst[:, 2 * g:2 * g + 2], wb[sl][:, k, :], sc[:, k, :], k == 0, k == 7,
                        ["wb%d" % sl, "sc"], ["ps0"], k == 7)
        psv = self.pst[:, 0:72].rearrange("p (g w) -> p g w", w=2)
        for w in range(2):
            self.tt(os_[:, w, :], psv[:, :, w], bs, ALU.add, ["ps0", "bs"], ["os"])
        kb.dma('sp', self.b_mod.ap(), os_.rearrange("p w g -> p (w g)"), reads=["os"], writes=["b_mod"], sem="modout")
        kb.cc(self.b_mod, self.g_mod, reads=["b_mod"], writes=["g_mod"])
        kb.barrier()
        ar.release(m)

    def part_b(self, l):
        kb, ar = self.kb, self.ar
        mB = ar.mark()
        T = self.load_tab(l)
        wfm_d = self.din("wfm_%d" % l, [128, 15, 8, 128])
        wvf_d = self.din("wvf_%d" % l, [128, 8, 384])
        wg_d = self.din("wg_%d" % l, [8, 128, 4, 8, 128])
        wbr_d = self.din("wbr_%d" % l, [8, 128, 10, 128])
        wo_d = self.din("wo_%d" % l, [8, 128, 8, 128])
        w13_d = self.din("w13_%d_1" % l, [NJ, 128, 8, 256])
        w2_d = self.din("w2_%d_1" % l, [NJ, 128, 8, 128])
        xin = self.xdram.ap()
        gk = self.g_k.ap().rearrange("(r p) t -> p r t", p=128)
        vf_d = self.g_vf.ap()
        hm_d, cs_d, mt_d, c256_d, cbsb_d = self.hm_d, self.cs_d, self.mt_d, self.c256_d, self.cbsb_d
        one1 = ar.f32(64)
        self.memset(one1, 1.0, ["const"])
        Xreg = self.xT
        Xflat = Xreg.rearrange("p k t -> p (k t)")
        xoff = [0]

        def xf32(n):
            o = xoff[0]
            xoff[0] += n
            assert xoff[0] <= 8 * NT, ("X region overflow", xoff[0])
            return Xflat[:, o:o + n]

        def xbf(n):
            w = (n + 1) // 2
            return xf32(w).bitcast(BF16)[:, 0:n]

        m0 = ar.mark()
        attnT = ar.bf(4 * NT).rearrange("p (k t) -> p k t", k=4)
        convo = ar.bf(2 * NT).rearrange("p (k t) -> p k t", k=2)
        sco = ar.bf(2 * NT).rearrange("p (k t) -> p k t", k=2)
        Y = ar.bf(2 * NT).rearrange("p (k t) -> p k t", k=2)
        KTc = ar.bf(CT)
        Vc = ar.bf(2 * 2 * 65).rearrange("p (a b c) -> p a b c", a=2, b=2)
        Fc = ar.bf(2 * 256).rearrange("p (a c) -> p a c", a=2)
        mP = ar.mark()
        qT = ar.bf(4 * NT).rearrange("p (k t) -> p k t", k=4)
        NU = NT + NH
        ABLK = BLK + [HBLK]
        xoff[0] = 0
        u = xbf(8 * NU).rearrange("p (k t) -> p k t", k=8)
        m1 = ar.mark()
        ud = self.udram.ap()
        for bi, (t0, n, ty) in enumerate(BLK):
            kb.dma('sp', u[:, :, t0:t0 + n], ud[:, :, t0:t0 + n], reads=["udram"], writes=["u%d" % bi], sem="uin%d" % bi)
        E = ar.bf(8 * 256).rearrange("p (r k t) -> p r k t", r=8, k=8)
        kb.dma('sp', E.rearrange("p r k t -> p r (k t)"), self.g_xe.ap().rearrange("(r p) n -> p r n", p=128),
               reads=["g_xe"], writes=["E"], sem="E")
        hs = ar.f32(16)
        kb.dma('sp', hs, self.hsel_d, writes=["hs"], sem="hs")
        xhal = ar.f32(8 * NH).rearrange("p (k t) -> p k t", k=8)
        for side, (d0, s0) in enumerate(((0, 16), (16, 0))):
            dst = xhal[:, :, d0:d0 + 16]
            for r in range(8):
                sc_ = hs[:, side * 8 + r:side * 8 + r + 1]
                if r == 0:
                    self.ts(dst, E[:, r, :, s0:s0 + 16], sc_, None, ALU.mult, None, ["E", "hs"], ["xhal"])
                else:
                    self.stt(dst, E[:, r, :, s0:s0 + 16], sc_, dst, ALU.mult, ALU.add, ["E", "hs", "xhal"], ["xhal"])
        self.cp(u[:, :, NT:NT + NH], xhal, ["xhal"], ["u5"])
        kb.barrier()
        ar.release(m1)
        self.debug("u", u, ["u%d" % i for i in range(6)], [128, 8, NU], BF16)
        m1 = ar.mark()
        xm = xoff[0]
        KT = None
        wq = ar.bf(5 * 8 * 128).rearrange("p (c k n) -> p c k n", c=5, k=8)
        kb.dma('pool', wq, wfm_d[:, 0:5], writes=["wq"], sem="wq")
        wvf = ar.bf(8 * 384).rearrange("p (k c) -> p k c", k=8)
        kb.dma('pool', wvf, wvf_d, writes=["wvf"], sem="wvf")
        self.load_rope()
        qt = self.qk_tmps()
        self.memset(Vc[:, :, :, 64:65], 1.0, ["Vc1"])
        it = 0
        for c in range(4):
            for bi, (t0, n, ty) in enumerate(BLK):
                pb = it % 4
                it += 1
                for k in range(8):
                    self.mm(self.ps(pb, n), wq[:, c, k, :], u[:, k, t0:t0 + n], k == 0, k == 7, ["wq", "u%d" % bi],
                            ["ps%d" % pb], k == 7)
                self.qknorm(pb, n, self.tcol(T, 'qg'), qT[:, c, t0:t0 + n], ["qT%d_%d" % (c, bi)],
                            t0 if ty == 'x' else None, qt, T)
        t0, n, ty = BLK[4]
        for k in range(8):
            self.mm(self.ps(0, n), wq[:, 4, k, :], u[:, k, t0:t0 + n], k == 0, k == 7, ["wq", "u4"], ["ps0"], k == 7)
        self.qknorm(0, n, self.tcol(T, 'kg'), KTc[:, 0:n], ["KTc"], None, qt, T)
        for tt_ in range(2):
            pb = 1 + tt_
            for k in range(8):
                self.mm(self.ps(pb, 384), u[:, k, LT + tt_ * 128:LT + (tt_ + 1) * 128], wvf[:, k, :], k == 0, k == 7,
                        ["wvf", "u4"], ["ps%d" % pb], k == 7)
            self.act(Vc[:, tt_, :, 0:64], self.ps(pb, 128).rearrange("p (a d) -> p a d", a=2), AF.Copy,
                     ["ps%d" % pb], ["Vc"])
            self.act(Fc[:, tt_, :], self.pst[:, pb * 512 + 128:pb * 512 + 384], AF.Copy, ["ps%d" % pb], ["Fc"])
        kb.barrier()
        ar.release(m1)
        self.debug("qT", qT, [], [128, 4, NT], BF16)
        m1 = ar.mark()
        wcb = ar.bf(6 * 8 * 128).rearrange("p (c k n) -> p c k n", c=6, k=8)
        HB = LT + 30
        PBW = LT + 2
        hb = xf32(2 * HB).rearrange("p (i t) -> p i t", i=2)
        pbf = xf32(2 * PBW).rearrange("p (i t) -> p i t", i=2)
        hbc = xf32(2 * (CT + 30)).rearrange("p (i t) -> p i t", i=2)
        pbc = ar.f32(2 * (CT + 2)).rearrange("p (i t) -> p i t", i=2)
        gb = ar.bf(2 * NT).rearrange("p (i t) -> p i t", i=2)
        acc = ar.f32(2 * NT).rearrange("p (i t) -> p i t", i=2)
        sg = [ar.f32(512), ar.f32(512)]
        hht = ar.f32(2 * NH).rearrange("p (i t) -> p i t", i=2)
        pht = ar.f32(2 * NH).rearrange("p (i t) -> p i t", i=2)
        hm = ar.f32(NH)
        kb.dma('sp', hm, hm_d, writes=["hm"], sem="hm")
        self.memset(hbc, 0.0, ["hbc"])
        self.memset(pbc, 0.0, ["pbc"])
        it = 0
        kb.dma('pool', wcb[:, 0:4], wfm_d[:, 5:9], writes=["wc"], sem="wc")
        for bi, (t0, n, ty) in enumerate(ABLK):
            uk = "u%d" % bi
            for i in range(2):
                for k in range(8):
                    self.mm(self.ps(0, n), wcb[:, i, k, :], u[:, k, t0:t0 + n], k == 0, k == 7, ["wc", uk], ["ps0"], k == 7)
                for k in range(8):
                    self.mm(self.ps(1, n), wcb[:, 2 + i, k, :], u[:, k, t0:t0 + n], k == 0, k == 7, ["wc", uk], ["ps1"], k == 7)
                si = it % 2
                it += 1
                self.act(sg[si][:, 0:n], self.ps(1, n), AF.Sigmoid, ["ps1"], ["sg%d" % si])
                if bi < 4:
                    dst = hb[:, i, 15 + t0:15 + t0 + n]
                    dk = "hb"
                elif bi == 4:
                    dst = hbc[:, i, 15:15 + CT]
                    dk = "hbc"
                else:
                    dst = hht[:, i, :]
                    dk = "hht"
                self.tt(dst, sg[si][:, 0:n], self.ps(0, n), ALU.mult, ["sg%d" % si, "ps0"], [dk])
        kb.dma('pool', wcb[:, 0:6], wfm_d[:, 9:15], writes=["wc"], sem="wc")
        for bi, (t0, n, ty) in enumerate(ABLK):
            uk = "u%d" % bi
            for i in range(2):
                for k in range(8):
                    self.mm(self.ps(2, n), wcb[:, 2 + i, k, :], u[:, k, t0:t0 + n], k == 0, k == 7, ["wc", uk], ["ps2"], k == 7)
                for k in range(8):
                    self.mm(self.ps(3, n), wcb[:, 4 + i, k, :], u[:, k, t0:t0 + n], k == 0, k == 7, ["wc", uk], ["ps3"], k == 7)
                si = it % 2
                it += 1
                self.act(sg[si][:, 0:n], self.ps(2, n), AF.Copy, ["ps2"], ["sg%d" % si])
                if bi < 4:
                    dst = pbf[:, i, 1 + t0:1 + t0 + n]
                    dk = "pb"
                elif bi == 4:
                    dst = pbc[:, i, 1:1 + CT]
                    dk = "pbc"
                else:
                    dst = pht[:, i, :]
                    dk = "pht"
                self.tt(dst, sg[si][:, 0:n], self.ps(3, n), ALU.mult, ["sg%d" % si, "ps3"], [dk])
                if bi < 5:
                    for k in range(8):
                        self.mm(self.ps(5, n), wcb[:, i, k, :], u[:, k, t0:t0 + n], k == 0, k == 7, ["wc", uk], ["ps5"], k == 7)
                    self.act(gb[:, i, t0:t0 + n], self.ps(5, n), AF.Copy, ["ps5"], ["gb"])
        for i in range(2):
            self.tt(hht[:, i, :], hht[:, i, :], hm, ALU.mult, ["hht", "hm"], ["hht"])
            self.tt(pht[:, i, :], pht[:, i, :], hm, ALU.mult, ["pht", "hm"], ["pht"])
            self.cp(hb[:, i, 0:15], hht[:, i, 1:16], ["hht"], ["hb"])
            self.cp(hb[:, i, 15 + LT:30 + LT], hht[:, i, 16:31], ["hht"], ["hb"])
            self.cp(pbf[:, i, 0:1], pht[:, i, 15:16], ["pht"], ["pb"])
            self.cp(pbf[:, i, 1 + LT:2 + LT], pht[:, i, 16:17], ["pht"], ["pb"])
        for i in range(2):
            for (src, skey, n0, d0) in ((pbf, "pb", LT, 0), (pbc, "pbc", CT, LT)):
                a_ = acc[:, i, d0:d0 + n0]
                self.ts(a_, src[:, i, 0:n0], self.tcol(T, 'scw', i * 3), None, ALU.mult, None, [skey, T['name']], ["acc"])
                for tap in range(1, 3):
                    self.stt(a_, src[:, i, tap:tap + n0], self.tcol(T, 'scw', i * 3 + tap), a_, ALU.mult, ALU.add,
                             [skey, "acc", T['name']], ["acc"])
            self.tt(sco[:, i, :], acc[:, i, :], gb[:, i, :], ALU.mult, ["acc", "gb"], ["sco"])
        for i in range(2):
            for (src, skey, n0, d0) in ((hb, "hb", LT, 0), (hbc, "hbc", CT, LT)):
                a_ = acc[:, i, d0:d0 + n0]
                self.ts(a_, src[:, i, 0:n0], self.tcol(T, 'cw', i * 31), self.tcol(T, 'cb', i), ALU.mult, ALU.add,
                        [skey, T['name']], ["acc"])
                for tap in range(1, 31):
                    self.stt(a_, src[:, i, tap:tap + n0], self.tcol(T, 'cw', i * 31 + tap), a_, ALU.mult, ALU.add,
                             [skey, "acc", T['name']], ["acc"])
        lt_ = [ar.f32(512) for _ in range(4)]
        for bi, (t0, n, ty) in enumerate(BLK):
            for i in range(2):
                self.mm(self.ps(0, n), self.c_onesf, acc[:, i, t0:t0 + n], i == 0, i == 1, ["acc", "const"], ["ps0"], i == 1)
            for i in range(2):
                self.act(lt_[i][:, 0:n], acc[:, i, t0:t0 + n], AF.Square, ["acc"], ["lt%d" % i])
                self.mm(self.ps(1, n), self.c_onesf, lt_[i][:, 0:n], i == 0, i == 1, ["lt%d" % i, "const"], ["ps1"], i == 1)
            self.act(lt_[2][:, 0:n], self.ps(0, n), AF.Square, ["ps0"], ["lt2"])
            self.tt(lt_[3][:, 0:n], self.ps(1, n), lt_[2][:, 0:n], ALU.subtract, ["ps1", "lt2"], ["lt3"])
            self.act(lt_[3][:, 0:n], lt_[3][:, 0:n], AF.Sqrt, ["lt3"], ["lt3"], bias=self.c_eps, scale=1.0)
            self.recip(lt_[3][:, 0:n], lt_[3][:, 0:n], ["lt3"], ["lt3"])
            for i in range(2):
                self.tt(lt_[i][:, 0:n], acc[:, i, t0:t0 + n], self.ps(0, n), ALU.subtract, ["acc", "ps0"], ["lt%d" % i])
                self.tt(lt_[i][:, 0:n], lt_[i][:, 0:n], lt_[3][:, 0:n], ALU.mult, ["lt%d" % i, "lt3"], ["lt%d" % i])
                self.act(convo[:, i, t0:t0 + n], lt_[i][:, 0:n], AF.Silu, ["lt%d" % i, T['name']], ["convo"],
                         bias=self.tcol(T, 'lb', i), scale=self.tcol(T, 'lg', i))
        kb.barrier()
        ar.release(m1)
        self.debug("convo", convo, [], [128, 2, NT], BF16)
        self.debug("sco", sco, [], [128, 2, NT], BF16)
        m1 = ar.mark()
        xoff[0] = 0
        NKC = 2 + L // 128
        KT = xbf(128 * NKC)
        V = xbf(NKC * 2 * 65).rearrange("p (n a c) -> p n a c", n=NKC, a=2)
        self.memset(V[:, 2:, :, 64:65], 1.0, ["V"])
        self.cp(KT[:, 0:CT], KTc, ["KTc"], ["KT"])
        self.cp(V[:, 0:2], Vc, ["Vc", "Vc1"], ["V"])
        for r in range(8):
            kb.dma('pool', KT[:, CT + r * LT:CT + (r + 1) * LT], gk[:, r, :], reads=["g_k"],
                   writes=["KT"], sem="KTd", group=True)
        vsrc = vf_d.rearrange("(n p) c -> p n c", p=128)
        for qd in range(4):
            for a in range(2):
                kb.dma('pool', V[:, 2 + qd * 32:2 + (qd + 1) * 32, a, 0:64], vsrc[:, qd * 32:(qd + 1) * 32, a * 64:(a + 1) * 64],
                       reads=["g_vf"], writes=["V"], sem="Vd", group=True)
        pt = [ar.bf(1024).rearrange("p (a n) -> p a n", a=2) for _ in range(3)]
        ou = [ar.f32(512), ar.f32(512)]
        rden = ar.f32(512)
        obB = [ar.bf(512), ar.bf(512)]
        itp = 0
        fin = 0
        for (t0, n, ty) in BLK:
            chunks = list(range(NKC)) if ty == 'x' else [0, 1]
            bi = t0 // 512
            for c in range(4):
                def qk(kc, sb):
                    self.mm(self.ps(2 * sb, n), KT[0:64, kc * 128:(kc + 1) * 128], qT[0:64, c, t0:t0 + n], True, True,
                            ["KT", "qT%d_%d" % (c, bi)], ["S%d" % sb], False)
                    self.mm(self.ps(2 * sb + 1, n), KT[64:128, kc * 128:(kc + 1) * 128], qT[64:128, c, t0:t0 + n], True, True,
                            ["KT", "qT%d_%d" % (c, bi)], ["S%d" % sb], True)
                LA = 2
                for i_ in range(min(LA, len(chunks))):
                    qk(chunks[i_], i_ % 3)
                for ci, kc in enumerate(chunks):
                    sb = ci % 3
                    if ci + LA < len(chunks):
                        qk(chunks[ci + LA], (ci + LA) % 3)
                    P = pt[itp % 3]
                    pk = "P%d" % (itp % 3)
                    itp += 1
                    sin_ = self.pst[:, 2 * sb * 512:2 * sb * 512 + 1024].rearrange("p (a n) -> p a n", a=2)[:, :, 0:n]
                    self.act(P[:, :, 0:n], sin_, AF.Exp, ["S%d" % sb], [pk], scale=0.125)
                    first, last = ci == 0, ci == len(chunks) - 1
                    self.mm(self.ps(6, n, 65), V[:, kc, 0, :], P[:, 0, 0:n], first, last, ["V", pk], ["acc4"], False)
                    self.mm(self.ps(7, n, 65), V[:, kc, 1, :], P[:, 1, 0:n], first, last, ["V", pk], ["acc4"], True)
                for hh in range(2):
                    pa = 6 + hh
                    f = fin % 2
                    fin += 1
                    self.act(ou[f][0:65, 0:n], self.ps(pa, n, 65), AF.Copy, ["acc4"], ["ou%d" % f])
                    self.act(rden[64:65, 0:n], ou[f][64:65, 0:n], AF.Ln, ["ou%d" % f], ["rden"])
                    self.act(rden[64:65, 0:n], rden[64:65, 0:n], AF.Exp, ["rden"], ["rden"], scale=-1.0)
                    self.mm(self.ps(4, n, 64), one1[64:65, 0:64], rden[64:65, 0:n], True, True, ["rden", "const"], ["S2"], True)
                    if hh == 0:
                        self.tt(attnT[0:64, c, t0:t0 + n], ou[f][0:64, 0:n], self.ps(4, n, 64), ALU.mult,
                                ["ou%d" % f, "S2"], ["attnT"])
                    else:
                        self.tt(obB[f][0:64, 0:n], ou[f][0:64, 0:n], self.ps(4, n, 64), ALU.mult,
                                ["ou%d" % f, "S2"], ["obB%d" % f])
                        kb.dma('sp', attnT[64:128, c, t0:t0 + n], obB[f][0:64, 0:n], reads=["obB%d" % f], writes=["attnT"],
                               sem="obB%d" % f)
        kb.barrier()
        ar.release(mP)
        self.debug("attnT", attnT, [], [128, 4, NT], BF16)
        m1 = ar.mark()
        xoff[0] = 0
        Fg = xbf(128 * 64).rearrange("p (n c) -> p n c", n=128)
        Asb = xbf(2 * 128 * 64).rearrange("p (r k c) -> p r k c", r=2, k=128)
        Asbf = Asb.rearrange("p r k c -> p (r k) c")
        MT = ar.bf(128 * 2 * 32).rearrange("p (k s n) -> p k s n", k=128, s=2)
        CS = xbf(256)
        Xs = xbf(2 * 2048).rearrange("p (r n) -> p r n", r=2)
        C256 = xbf(2 * 2 * 256).rearrange("p (h r n) -> p h r n", h=2, r=2)
        CBSB = xbf(2 * 64).rearrange("p (r n) -> p r n", r=2)
        kb.dma('pool', MT, mt_d, writes=["ftab"], sem="ftab", group=True)
        kb.dma('pool', CS, cs_d, writes=["ftab"], sem="ftab", group=True)
        kb.dma('pool', C256, c256_d, writes=["ftab"], sem="ftab", group=True)
        kb.dma('pool', CBSB[0:64], cbsb_d, writes=["ftab"], sem="ftab", group=True)
        fsrc = vf_d.rearrange("(a b) c -> a b c", b=128)
        ev = 0
        for g in range(4):
            for qd in range(4):
                kb.dma('pool', Fg[:, qd * 32:(qd + 1) * 32, :], fsrc[:, qd * 32:(qd + 1) * 32, 128 + g * 64:128 + (g + 1) * 64],
                       reads=["g_vf"], writes=["Fg"], sem="Fg", group=True)
            for c in range(64):
                pb = c % 4
                self.mm(self.ps(pb, 256), Fg[:, :, c], CS, True, True, ["Fg", "ftab"], ["ps%d" % pb], True)
                if c % 2 == 0:
                    self.act(Asbf[:, :, c], self.ps(pb, 256), AF.Copy, ["ps%d" % pb], ["Asb"])
                else:
                    self.cp(Asbf[:, :, c], self.ps(pb, 256), ["ps%d" % pb], ["Asb"])
            for k1h in range(2):
                for k1l in range(64):
                    k1 = k1h * 64 + k1l
                    col = k1l * 32
                    o_ = self.pst[0:64, col:col + 32]
                    lastk = k1l == 63
                    self.mm(o_, Asb[:, 0, k1, :], MT[:, k1, 0, :], True, False, ["Asb", "ftab"], ["ps0", "ps1", "ps2", "ps3"], False)
                    self.mm(o_, Asb[:, 1, k1, :], MT[:, k1, 1, :], False, True, ["Asb", "ftab"], ["ps0", "ps1", "ps2", "ps3"], lastk)
                srcv = self.pst[0:64, 0:2048].rearrange("p (k r a) -> p k r a", r=2, a=16)
                for r in range(2):
                    src = srcv[:, :, r, :]
                    dst = Xs[0:64, r, :].rearrange("p (a k) -> p k a", k=128)[:, k1h * 64:(k1h + 1) * 64, :]
                    if r == 0:
                        self.act(dst, src, AF.Copy, ["ps0", "ps1", "ps2", "ps3"], ["Xs"])
                    else:
                        self.cp(dst, src, ["ps0", "ps1", "ps2", "ps3"], ["Xs"])
            ro = (g % 2) * 64
            for tb in range(4):
                pb = 4 + tb % 2
                o_ = self.pst[ro:ro + 64, pb * 512:pb * 512 + 512]
                self.mm(o_, CBSB[0:64, 0, :], Xs[0:64, 0, tb * 512:(tb + 1) * 512], True, False, ["Xs", "ftab"], ["ps%d" % pb], False)
                self.mm(o_, CBSB[0:64, 1, :], Xs[0:64, 1, tb * 512:(tb + 1) * 512], False, True, ["Xs", "ftab"], ["ps%d" % pb], True)
                self.act(Y[ro:ro + 64, g // 2, tb * 512:(tb + 1) * 512], o_, AF.Copy, ["ps%d" % pb], ["Y"])
            xc = [self.pst[0:64, 6 * 512:6 * 512 + 256], self.pst[0:64, 7 * 512:7 * 512 + 256]]
            for r in range(2):
                for nh in range(2):
                    self.mm(xc[r], Fc[:, nh, g * 64:(g + 1) * 64], C256[:, nh, r, :], nh == 0, nh == 1, ["Fc", "ftab"],
                            ["ps%d" % (6 + r)], nh == 1)
                self.cp(Xs[0:64, r, 0:256], xc[r], ["ps%d" % (6 + r)], ["Xs"])
            o_ = self.pst[ro:ro + 64, 4 * 512:4 * 512 + 256]
            self.mm(o_, CBSB[0:64, 0, :], Xs[0:64, 0, 0:256], True, False, ["Xs", "ftab"], ["ps4"], False)
            self.mm(o_, CBSB[0:64, 1, :], Xs[0:64, 1, 0:256], False, True, ["Xs", "ftab"], ["ps4"], True)
            self.act(Y[ro:ro + 64, g // 2, LT:NT], o_, AF.Copy, ["ps4"], ["Y"])
        kb.barrier()
        ar.release(m1)
        self.debug("Y", Y, [], [128, 2, NT], BF16)
        m1 = ar.mark()
        xoff[0] = 0
        u = xbf(8 * NT).rearrange("p (k t) -> p k t", k=8)
        wgb = [xbf(4 * 8 * 128).rearrange("p (b k n) -> p b k n", b=4, k=8) for _ in range(2)]
        wbrb = [xbf(10 * 128).rearrange("p (k n) -> p k n", k=10) for _ in range(2)]
        merged = ar.bf(8 * NT).rearrange("p (k t) -> p k t", k=8)
        for bi, (t0, n, ty) in enumerate(BLK):
            kb.dma('sp', u[:, :, t0:t0 + n], self.udram.ap()[:, :, t0:t0 + n], reads=["udram"], writes=["u%d" % bi],
                   sem="uin%d" % bi)
        sg = [ar.f32(512), ar.f32(512)]
        macc = [ar.f32(512), ar.f32(512)]
        mt2 = [ar.f32(512), ar.f32(512)]
        brs = [(attnT, 4, 0, "attnT"), (Y, 2, 4, "Y"), (convo, 2, 6, "convo"), (sco, 2, 8, "sco")]
        it = 0
        for o in range(8):
            sl = o % 2
            kb.dma('pool', wgb[sl], wg_d[o], writes=["wg%d" % sl], sem="wg%d" % sl)
            kb.dma('pool', wbrb[sl], wbr_d[o], writes=["wbr%d" % sl], sem="wbr%d" % sl)
            for bi, (t0, n, ty) in enumerate(BLK):
                ma = macc[bi % 2]
                mk = "macc%d" % (bi % 2)
                for br, (src, nk, k0, skey) in enumerate(brs):
                    pg = it % 2
                    py = 2 + it % 2
                    si = it % 2
                    it += 1
                    for k in range(8):
                        self.mm(self.ps(pg, n), wgb[sl][:, br, k, :], u[:, k, t0:t0 + n], k == 0, k == 7,
                                ["wg%d" % sl, "u%d" % bi], ["ps%d" % pg], k == 7)
                    for k in range(nk):
                        self.mm(self.ps(py, n), wbrb[sl][:, k0 + k, :], src[:, k, t0:t0 + n], k == 0, k == nk - 1,
                                ["wbr%d" % sl, skey], ["ps%d" % py], k == nk - 1)
                    self.act(sg[si][:, 0:n], self.ps(pg, n), AF.Sigmoid, ["ps%d" % pg, T['name']], ["sg%d" % si],
                             bias=self.tcol(T, 'bg', br * 8 + o))
                    if br == 0:
                        self.tt(ma[:, 0:n], sg[si][:, 0:n], self.ps(py, n), ALU.mult, ["sg%d" % si, "ps%d" % py], [mk])
                    else:
                        self.tt(mt2[si][:, 0:n], sg[si][:, 0:n], self.ps(py, n), ALU.mult, ["sg%d" % si, "ps%d" % py], ["mt%d" % si])
                        if br < 3:
                            self.tt(ma[:, 0:n], ma[:, 0:n], mt2[si][:, 0:n], ALU.add, [mk, "mt%d" % si], [mk])
                        else:
                            self.tt(merged[:, o, t0:t0 + n], ma[:, 0:n], mt2[si][:, 0:n], ALU.add, [mk, "mt%d" % si], ["merged%d" % bi])
        kb.barrier()
        self.debug("merged", merged, [], [128, 8, NT], BF16)
        xT = self.xT
        for bi, (t0, n, ty) in enumerate(BLK):
            kb.dma('sp', xT[:, :, t0:t0 + n], xin[:, :, t0:t0 + n], reads=["xdram"], writes=["x%d" % bi], sem="xin%d" % bi)
        wob = [ar.bf(8 * 128).rearrange("p (k n) -> p k n", k=8) for _ in range(2)]
        it = 0
        for o in range(8):
            sl = o % 2
            kb.dma('pool', wob[sl], wo_d[o], writes=["wo%d" % sl], sem="wo%d" % sl)
            for bi, (t0, n, ty) in enumerate(BLK):
                py = 4 + it % 2
                it += 1
                for k in range(8):
                    self.mm(self.ps(py, n), wob[sl][:, k, :], merged[:, k, t0:t0 + n], k == 0, k == 7,
                            ["wo%d" % sl, "merged%d" % bi], ["ps%d" % py], k == 7)
                self.stt(xT[:, o, t0:t0 + n], self.ps(py, n), self.tG(T, ty, 1, o), xT[:, o, t0:t0 + n],
                         ALU.mult, ALU.add, ["ps%d" % py, "x%d" % bi, T['name'] + "_der"], ["x%d" % bi])
        kb.barrier()
        ar.release(m0)
        self.debug("x_mix", self.xT, ["x%d" % i for i in range(5)], [128, 8, NT])
        self.ffn(T, 2, w13_d, w2_d, "f2")
        self.debug("x_ffn2", self.xT, ["x%d" % i for i in range(5)], [128, 8, NT])
        kb.barrier()
        ar.release(mB)

    def final_norm(self):
        kb, ar = self.kb, self.ar
        g_d = self.din("fng", [128, 8])
        gt = ar.f32(8)
        kb.dma('sp', gt, g_d, writes=["fng"], sem="fng")
        y_d = self.dout("yT_out", [128, 8, LT])
        tmp = self.norm_tmps()
        yo = [ar.f32(512), ar.f32(512)]
        xT = self.xT
        it = 0
        for bi, (t0, n, ty) in enumerate(LBLK):
            sq, rstd = tmp['sq'], tmp['rstd']
            for k in range(8):
                i = k % 2
                self.act(sq[i][:, 0:n], xT[:, k, t0:t0 + n], AF.Square, ["x%d" % bi], ["sq%d" % i])
                self.mm(self.ps(7, n), self.c_ones, sq[i][:, 0:n], k == 0, k == 7, ["sq%d" % i, "const"], ["ps7"], True)
            self.act(rstd[:, 0:n], self.ps(7, n), AF.Sqrt, ["ps7"], ["rstd"], bias=self.c_eps, scale=1.0 / D)
            self.recip(rstd[:, 0:n], rstd[:, 0:n], ["rstd"], ["rstd"])
            for k in range(8):
                i = it % 2
                it += 1
                self.stt(yo[i][:, 0:n], xT[:, k, t0:t0 + n], gt[:, k:k + 1], rstd[:, 0:n], ALU.mult, ALU.mult,
                         ["x%d" % bi, "rstd", "fng"], ["yo%d" % i])
                kb.dma('sp', y_d[:, k, t0:t0 + n], yo[i][:, 0:n], reads=["yo%d" % i], sem="yout%d" % i)

    def build(self):
        kb, ar, nc = self.kb, self.ar, self.nc
        self.load_consts()
        self.rope_d = self.din("rope", [2, 128, LT])
        self.hm_d = self.din("hmask", [128, NH])
        self.hsel_d = self.din("hsel", [128, 16])
        self.cs_d = self.din("cs128", [128, 256])
        self.mt_d = self.din("mtab", [128, 128, 2, 32])
        self.c256_d = self.din("c256", [128, 2, 2, 256])
        self.cbsb_d = self.din("cbsb", [64, 2, 64])
        self.xdram = nc.dram_tensor("xdram", [128, 8, NT], F32)
        self.b_k = nc.dram_tensor("b_k", [128, LT], F32)
        self.g_k = nc.dram_tensor("g_k", [NCORE * 128, LT], F32)
        self.b_vf = nc.dram_tensor("b_vf", [LT, 384], F32)
        self.g_vf = nc.dram_tensor("g_vf", [NCORE * LT, 384], F32)
        self.b_xe = nc.dram_tensor("b_xe", [128, 8 * NH], BF16)
        self.g_xe = nc.dram_tensor("g_xe", [NCORE * 128, 8 * NH], BF16)
        self.udram = nc.dram_tensor("udram", [128, 8, NT], BF16)
        self.b_mod = nc.dram_tensor("b_mod", [128, 72], F32)
        self.g_mod = nc.dram_tensor("g_mod", [NCORE * 128, 72], F32)
        self.xT = ar.f32(8 * NT).rearrange("p (k t) -> p k t", k=8)
        xin = self.din("xT_in", [128, 8, NT])
        self.mod_phase()
        for bi, (t0, n, ty) in enumerate(BLK):
            kb.dma('sp', self.xT[:, :, t0:t0 + n], xin[:, :, t0:t0 + n], writes=["x%d" % bi], sem="xin%d" % bi)
        self.part_a(0)
        for l in range(DEPTH):
            self.part_b(l)
            if l < DEPTH - 1:
                self.part_a(l + 1)
            else:
                self.final_norm()


def _bf_consts():
    ones = np.ones((128, 128), np.float32)
    bd = np.zeros((128, 128), np.float32)
    bd[:64, :64] = 1
    bd[64:, 64:] = 1
    perm = np.zeros((128, 128), np.float32)
    for d in range(128):
        hd = d % 64
        half = (hd % 32) // 16
        partner = d + 16 if half == 0 else d - 16
        perm[partner, d] = 1.0
    return np.ascontiguousarray(np.stack([ones, bd, perm], 1))


def _rope_tables(core):
    pos = np.arange(core * LT, (core + 1) * LT)
    row = (pos // 64).astype(np.float64)
    col = (pos % 64).astype(np.float64)
    inv = 10000.0 ** (-np.arange(0, 32, 2, dtype=np.float64) / 32)
    cos = np.zeros((128, LT), np.float32)
    sin = np.zeros((128, LT), np.float32)
    for d in range(128):
        hd = d % 64
        axis = hd // 32
        half = (hd % 32) // 16
        f = hd % 16
        ang = (row if axis == 0 else col) * inv[f]
        cos[d] = np.cos(ang)
        sin[d] = np.sin(ang) * (-1.0 if half == 0 else 1.0)
    return np.stack([cos, sin], 0)


def _fm(w, cols):
    return np.ascontiguousarray(w.reshape(8, 128, w.shape[1])[:, :, cols].transpose(1, 0, 2))


def _layer_tab(l, modx, modc, I):
    tab = np.zeros((128, NTAB), np.float32)
    tab[:, TO['modx']:TO['modx'] + 72] = modx
    tab[:, TO['modc']:TO['modc'] + 72] = modc
    tab[:, TO['normg']:TO['normg'] + 24] = I['norm_g'][l].reshape(3, 8, 128).transpose(2, 0, 1).reshape(128, 24)
    tab[:, TO['bg']:TO['bg'] + 32] = I['b_gate'][l].reshape(4, 8, 128).transpose(2, 0, 1).reshape(128, 32)
    tab[:, TO['cw']:TO['cw'] + 62] = I['conv_dw_w'][l].reshape(31, 2, 128).transpose(2, 1, 0).reshape(128, 62)
    tab[:, TO['cb']:TO['cb'] + 2] = I['conv_dw_b'][l].reshape(2, 128).T
    tab[:, TO['lg']:TO['lg'] + 2] = I['conv_ln_g'][l].reshape(2, 128).T
    tab[:, TO['lb']:TO['lb'] + 2] = I['conv_ln_b'][l].reshape(2, 128).T
    tab[:, TO['scw']:TO['scw'] + 6] = I['sc_conv_w'][l].reshape(3, 2, 128).transpose(2, 1, 0).reshape(128, 6)
    tab[:, TO['qg']] = np.tile(I['q_norm_g'][l], 2)
    tab[:, TO['kg']] = np.tile(I['k_norm_g'][l], 2)
    return tab


def _w13p(w13):
    a = w13[:, :DFF].reshape(8, 128, NJ, 128)
    b = w13[:, DFF:].reshape(8, 128, NJ, 128)
    ab = np.concatenate([a, b], axis=3)
    return np.ascontiguousarray(ab.transpose(2, 1, 0, 3))


def _a_inputs(l, I):
    w_in = I['w_in'][l]
    return dict(
        a_w13=_w13p(I['ffn_w13'][l, 0]),
        a_w2=np.ascontiguousarray(I['ffn_w2'][l, 0].reshape(NJ, 128, 8, 128)),
        a_wk=_fm(w_in, np.arange(512, 640)),
        a_wvf=_fm(w_in, np.arange(640, 1024)),
    )


_CACHE = {}


def _get_prog(kind, dbg=()):
    key = (kind, tuple(dbg))
    if key not in _CACHE:
        _CACHE[key] = Prog(kind, dbg)
    return _CACHE[key]


def _xT_from(x_core, ctx):
    a = np.concatenate([x_core, ctx], axis=0)
    return np.ascontiguousarray(a.T.reshape(8, 128, NT).transpose(1, 0, 2))


def _b_inputs(l, I):
    w_in = I['w_in'][l]
    pair = lambda c: np.concatenate([np.arange(c * 64, c * 64 + 64), np.arange((4 + c) * 64, (4 + c) * 64 + 64)])
    chunks = [pair(c) for c in range(4)] + [np.arange(512, 640)]
    chunks += [np.arange(1024 + i * 128, 1024 + (i + 1) * 128) for i in range(4)]
    chunks += [np.arange(1536 + i * 128, 1536 + (i + 1) * 128) for i in range(6)]
    wfm = np.ascontiguousarray(np.stack([_fm(w_in, c) for c in chunks], axis=1))
    g = w_in[:, 2304:].reshape(8, 128, 4, 8, 128)
    wg = np.ascontiguousarray(g.transpose(3, 1, 2, 0, 4))
    rows = np.stack([pair(c) for c in range(4)])
    a = I['w_attn_out'][l][rows]
    allw = np.concatenate([a, I['w_fnet'][l].reshape(2, 128, 1024), I['w_conv_out'][l].reshape(2, 128, 1024),
                           I['w_sc_out'][l].reshape(2, 128, 1024)], 0)
    wbr = np.ascontiguousarray(allw.reshape(10, 128, 8, 128).transpose(2, 1, 0, 3))
    wo = np.ascontiguousarray(I['w_o'][l].reshape(8, 128, 8, 128).transpose(2, 1, 0, 3))
    return dict(b_wfm=wfm, b_wvf=_fm(w_in, np.arange(640, 1024)), b_wg=wg, b_wbr=wbr, b_wo=wo,
                b_w13=_w13p(I['ffn_w13'][l, 1]),
                b_w2=np.ascontiguousarray(I['ffn_w2'][l, 1].reshape(NJ, 128, 8, 128)))


def _fnet_tables(core):
    n = np.arange(128, dtype=np.float64)
    a1 = 2 * np.pi * np.outer(n, n) / 128
    cs128 = np.concatenate([np.cos(a1), -np.sin(a1)], 1).astype(np.float32)
    k1 = np.arange(128)[:, None]
    k2 = np.arange(16)[None, :] + 16 * core
    kk = (k1 + 128 * k2).reshape(-1).astype(np.float64)
    a2 = 2 * np.pi * np.outer(n, kk) / L
    s = 1.0 / 128.0
    mc = (np.cos(a2) * s).reshape(128, 128, 16)
    ms = (np.sin(a2) * s).reshape(128, 128, 16)
    mtab = np.zeros((128, 128, 2, 32), np.float32)
    mtab[:, :, 0, 0:16] = mc
    mtab[:, :, 0, 16:32] = -ms
    mtab[:, :, 1, 0:16] = ms
    mtab[:, :, 1, 16:32] = mc
    nn = (np.arange(2)[None, :] * 128 + np.arange(128)[:, None]).astype(np.float64)
    a3 = 2 * np.pi * nn[:, :, None] * np.arange(256)[None, None, :] / 256
    c256 = (np.stack([np.cos(a3), -np.sin(a3)], 2) / 16.0).astype(np.float32)
    j = np.arange(64, dtype=np.float64)
    a4 = 2 * np.pi * np.outer(j, j) / 64
    cbsb = (np.stack([np.cos(a4), np.sin(a4)], 1) / 8.0).astype(np.float32)
    return dict(cs128=cs128, mtab=np.ascontiguousarray(mtab), c256=np.ascontiguousarray(c256),
                cbsb=np.ascontiguousarray(cbsb))


def make_in_maps(I):
    I = {k: np.asarray(v) for k, v in I.items()}
    cbf = _bf_consts()
    x = I['x'][0]
    ctx = I['ctx'][0]
    cT = np.ascontiguousarray(np.stack([I['c'].reshape(8, 128).T, I['c_ctx'].reshape(8, 128).T], axis=2).astype(np.float32))
    zero = np.zeros((128, 72), np.float32)
    shared = dict(cbf=cbf, cT=cT, fng=np.ascontiguousarray(I['final_norm_g'].reshape(8, 128).T))
    for l in range(DEPTH):
        shared["tab%d" % l] = _layer_tab(l, zero, zero, I)
        a = _a_inputs(l, I)
        bw = _b_inputs(l, I)
        shared["w13_%d_0" % l] = a['a_w13']
        shared["w2_%d_0" % l] = a['a_w2']
        shared["wk_%d" % l] = a['a_wk']
        shared["wvf_%d" % l] = a['a_wvf']
        shared["w13_%d_1" % l] = bw['b_w13']
        shared["w2_%d_1" % l] = bw['b_w2']
        shared["wfm_%d" % l] = bw['b_wfm']
        shared["wg_%d" % l] = bw['b_wg']
        shared["wbr_%d" % l] = bw['b_wbr']
        shared["wo_%d" % l] = bw['b_wo']
    in_maps = []
    for i in range(NCORE):
        m = dict(shared)
        ft = _fnet_tables(i)
        m.update(ft)
        m["rope"] = _rope_tables(i)
        m["xT_in"] = _xT_from(x[i * LT:(i + 1) * LT], ctx)
        cols = np.arange(i * 1152, (i + 1) * 1152)
        w = I['w_ada'][:, :, cols].reshape(4, 8, 128, 9, 128).transpose(0, 3, 2, 1, 4).reshape(36, 128, 8, 128)
        m["wada"] = np.ascontiguousarray(w)
        m["bada"] = np.ascontiguousarray(I['b_ada'][:, cols].reshape(4, 9, 128).transpose(2, 0, 1).reshape(128, 36))
        hm = np.zeros((128, NH), np.float32)
        hs = np.zeros((128, 16), np.float32)
        if i > 0:
            hm[:, 0:16] = 1.0
            hs[:, i - 1] = 1.0
        if i < NCORE - 1:
            hm[:, 16:32] = 1.0
            hs[:, 8 + i + 1] = 1.0
        m["hmask"] = hm
        m["hsel"] = hs
        in_maps.append(m)
    return in_maps


def kernel(**I):
    prog = _get_prog('F')
    in_maps = make_in_maps(I)
    res = run_bass_kernel_spmd(prog.nc, in_maps, core_ids=list(range(NCORE)))
    outs = res.results
    y = np.concatenate([o["yT_out"].transpose(2, 1, 0).reshape(LT, D) for o in outs], axis=0)
    return np.ascontiguousarray(y[None].astype(np.float32))
```

```python
import numpy as np
from contextlib import ExitStack
import concourse.bass as bass
import concourse.mybir as mybir
from concourse.bass_utils import run_bass_kernel_spmd

F32 = mybir.dt.float32
BF16 = mybir.dt.bfloat16
AF = mybir.ActivationFunctionType
ALU = mybir.AluOpType

NCORE = 8
D = 1024
L = 16384
LT = 2048
CT = 256
NT = LT + CT
NH = 32
DEPTH = 4
DFF = 2816
NJ = 22
EPS = 1e-6
BLK = [(0, 512, 'x'), (512, 512, 'x'), (1024, 512, 'x'), (1536, 512, 'x'), (2048, 256, 'c')]
LBLK = BLK[:4]
HBLK = (NT, NH, 'x')
JPARTS = [(0, 6), (6, 6), (12, 5), (17, 5)]

TO = {}
_o = 0
for _n, _w in [('modx', 72), ('modc', 72), ('normg', 24), ('bg', 32), ('cw', 62), ('cb', 2), ('lg', 2),
               ('lb', 2), ('scw', 6), ('qg', 1), ('kg', 1)]:
    TO[_n] = _o
    _o += _w
NTAB = _o


class KB:
    def __init__(self, nc, es):
        self.nc = nc
        self.es = es
        self.E = dict(pe=nc.tensor, act=nc.scalar, dve=nc.vector, pool=nc.gpsimd, sp=nc.sync)
        self.sem = {e: es.enter_context(nc.semaphore("s_" + e)) for e in self.E}
        self.cnt = {e: 0 for e in self.E}
        self.seen = {e: {} for e in self.E}
        self.prog = {e: [] for e in self.E}
        self.W = {}
        self.R = {}
        self.pend = {e: [[], []] for e in self.E}
        self.dsems = {}

    def _semh(self, sk):
        if isinstance(sk, tuple):
            return self.dsems[sk[1]][0]
        return self.sem[sk]

    def _need(self, e, reads, writes, skip=None):
        toks = []
        for k in reads:
            t = self.W.get(k)
            if t:
                toks.append(t)
        for k in writes:
            t = self.W.get(k)
            if t:
                toks.append(t)
            toks.extend(self.R.get(k, {}).items())
        d = {}
        for sk, v in toks:
            if sk == skip:
                continue
            if sk == e and e == 'pe':
                continue
            if self.seen[e].get(sk, 0) >= v:
                continue
            d[sk] = max(d.get(sk, 0), v)
        for sk, v in d.items():
            self.seen[e][sk] = v
        return list(d.items())

    def _reg(self, tok, reads, writes):
        sk, v = tok
        for k in reads:
            r = self.R.setdefault(k, {})
            r[sk] = max(r.get(sk, 0), v)
        for k in writes:
            self.W[k] = tok
            self.R[k] = {}

    def op(self, e, fn, reads=(), writes=(), inc=True):
        for e2 in self.E:
            if e2 != e:
                for k in list(reads) + list(writes):
                    assert k not in self.pend[e2][1], ("pending write conflict", k)
                for k in writes:
                    assert k not in self.pend[e2][0], ("pending read conflict", k)
        waits = self._need(e, reads, writes)
        if inc:
            self.cnt[e] += 1
            tok = (e, self.cnt[e])
            pr, pw = self.pend[e]
            self._reg(tok, list(reads) + pr, list(writes) + pw)
            self.pend[e] = [[], []]
        else:
            self.pend[e][0] += list(reads)
            self.pend[e][1] += list(writes)
        self.prog[e].append((waits, fn, inc, None))

    def dma(self, q, out, in_, reads=(), writes=(), sem=None, group=False):
        if sem not in self.dsems:
            self.dsems[sem] = [self.es.enter_context(self.nc.semaphore("d_" + sem)), 0]
        skip = ('d', sem) if group else None
        waits = self._need(q, reads, writes, skip=skip)
        self.dsems[sem][1] += 16
        tok = (('d', sem), self.dsems[sem][1])
        self._reg(tok, reads, writes)
        self.prog[q].append((waits, lambda e, o=out, i=in_: e.dma_start(out=o, in_=i), 'dma', sem))

    def cc(self, in_t, out_t, reads=(), writes=()):
        sem = "cc"
        if sem not in self.dsems:
            self.dsems[sem] = [self.es.enter_context(self.nc.semaphore("d_" + sem)), 0]
        waits = self._need('pool', reads, writes)
        self.dsems[sem][1] += 1
        tok = (('d', sem), self.dsems[sem][1])
        self._reg(tok, reads, writes)
        self.prog['pool'].append((waits, lambda e, i=in_t, o=out_t: e.collective_compute(
            "AllGather", ALU.bypass, replica_groups=[list(range(NCORE))], ins=[i.ap().opt()], outs=[o.ap().opt()]),
            'cc', sem))

    def barrier(self, full=True):
        for e in self.E:
            waits = []
            for e2 in self.E:
                if e2 != e and self.cnt[e2] > self.seen[e].get(e2, 0):
                    waits.append((e2, self.cnt[e2]))
                    self.seen[e][e2] = self.cnt[e2]
            for name, (s, c) in self.dsems.items():
                sk = ('d', name)
                if name == "cc" and not full:
                    continue
                if c > self.seen[e].get(sk, 0):
                    waits.append((sk, c))
                    self.seen[e][sk] = c
            if waits:
                self.prog[e].append((waits, None, False, None))

    def emit(self):
        self.barrier(full=True)
        with self.nc.Block() as block:
            for e, dec in (('sp', block.sync), ('act', block.scalar), ('dve', block.vector),
                           ('pool', block.gpsimd), ('pe', block.tensor)):
                def body(eng, e=e):
                    for waits, fn, inc, sem in self.prog[e]:
                        for sk, v in waits:
                            eng.wait_ge(self._semh(sk), v)
                        if fn is None:
                            continue
                        ins = fn(eng)
                        if inc is True:
                            ins.then_inc(self.sem[e], 1)
                        elif inc == 'dma':
                            ins.then_inc(self.dsems[sem][0], 16)
                        elif inc == 'cc':
                            ins.then_inc(self.dsems[sem][0], 1)
                dec(body)


class Arena:
    def __init__(self, nc, es, words):
        self.t = es.enter_context(nc.sbuf_tensor("arena", [128, words], F32))
        self.words = words
        self.top = 0

    def mark(self):
        return self.top

    def release(self, m):
        self.top = m

    def f32(self, n):
        o = self.top
        self.top += n
        assert self.top <= self.words, ("arena overflow", self.top)
        return self.t[:, o:o + n]

    def bf(self, n):
        w = (n + 1) // 2
        o = self.top
        self.top += w
        assert self.top <= self.words, ("arena overflow", self.top)
        return self.t[:, o:o + w].bitcast(BF16)[:, 0:n]


class Prog:
    def __init__(self, kind, dbg=()):
        self.kind = kind
        self.dbg = dbg
        self.es = ExitStack()
        nc = self.nc = bass.Bass("TRN2", target_bir_lowering=False)
        self.kb = KB(nc, self.es)
        self.ar = Arena(nc, self.es, 51712)
        self.pst = self.es.enter_context(nc.psum_tensor("ps", [128, 4096], F32))
        self.uid = 0
        self.outs = []
        self.dins = {}
        self.build()
        self.kb.emit()
        self.es.close()

    def ps(self, b, n=512, p=128):
        return self.pst[0:p, b * 512:b * 512 + n]

    def din(self, name, shape, dt=F32):
        if name not in self.dins:
            self.dins[name] = self.nc.dram_tensor(name, list(shape), dt, kind="ExternalInput").ap()
        return self.dins[name]

    def dout(self, name, shape, dt=F32):
        self.outs.append(name)
        return self.nc.dram_tensor(name, list(shape), dt, kind="ExternalOutput").ap()

    def key(self, s):
        self.uid += 1
        return "%s#%d" % (s, self.uid)

    def mm(self, out, lhsT, rhs, start, stop, reads, writes, inc):
        self.kb.op('pe', lambda e: e.matmul(out=out, lhsT=lhsT, rhs=rhs, start=start, stop=stop),
                   reads, writes, inc)

    def act(self, out, in_, func, reads, writes, bias=None, scale=None):
        kw = {}
        if bias is not None:
            kw['bias'] = bias
        if scale is not None:
            kw['scale'] = scale
        self.kb.op('act', lambda e: e.activation(out=out, in_=in_, func=func, **kw), reads, writes)

    def tt(self, out, in0, in1, op, reads, writes, eng='dve'):
        self.kb.op(eng, lambda e: e.tensor_tensor(out=out, in0=in0, in1=in1, op=op), reads, writes)

    def ts(self, out, in0, s1, s2, op0, op1, reads, writes, eng='dve'):
        if s2 is None:
            self.kb.op(eng, lambda e: e.tensor_scalar(out=out, in0=in0, scalar1=s1, scalar2=None, op0=op0),
                       reads, writes)
        else:
            self.kb.op(eng, lambda e: e.tensor_scalar(out=out, in0=in0, scalar1=s1, scalar2=s2, op0=op0, op1=op1),
                       reads, writes)

    def stt(self, out, in0, scalar, in1, op0, op1, reads, writes):
        self.kb.op('dve', lambda e: e.scalar_tensor_tensor(out=out, in0=in0, scalar=scalar, in1=in1, op0=op0, op1=op1),
                   reads, writes)

    def cp(self, out, in_, reads, writes, eng='dve'):
        self.kb.op(eng, lambda e: e.tensor_copy(out=out, in_=in_), reads, writes)

    def recip(self, out, in_, reads, writes):
        self.kb.op('dve', lambda e: e.reciprocal(out=out, in_=in_), reads, writes)

    def memset(self, ap, val, writes, eng='dve'):
        self.kb.op(eng, lambda e: e.memset(ap, val), (), writes)

    def debug(self, name, ap, reads, shape, dt=F32):
        if name in self.dbg:
            d = self.dout("dbg_" + name, shape, dt)
            self.kb.dma('sp', d, ap, reads=reads, writes=(), sem=self.key("dbg"))

    def load_tab(self, l):
        name = "tab%d" % l
        tab_d = self.din(name, [128, NTAB])
        tab = self.ar.f32(NTAB)
        self.kb.dma('sp', tab[:, 144:NTAB], tab_d[:, 144:NTAB], writes=[name], sem="tab", group=True)
        gm = self.g_mod.ap().rearrange("(r p) (w l c) -> p w l r c", p=128, w=2, l=DEPTH)
        for w in range(2):
            self.kb.dma('sp', tab[:, 72 * w:72 * w + 72].rearrange("p (r c) -> p r c", r=8), gm[:, w, l],
                        reads=["g_mod"], writes=[name], sem="tab", group=True)
        der = self.ar.f32(96)
        derv = der.rearrange("p (t s a o) -> p t s a o", t=2, s=3, a=2)
        for ti, mn in enumerate(('modx', 'modc')):
            for s in range(3):
                mo = TO[mn]
                sc = tab[:, mo + (3 * s + 1) * 8: mo + (3 * s + 1) * 8 + 8]
                gt = tab[:, mo + (3 * s + 2) * 8: mo + (3 * s + 2) * 8 + 8]
                ng = tab[:, TO['normg'] + s * 8: TO['normg'] + s * 8 + 8]
                self.stt(derv[:, ti, s, 0, :], sc, 1.0, ng, ALU.add, ALU.mult, [name], [name + "_der"])
                self.ts(derv[:, ti, s, 1, :], gt, 0.5 if s != 1 else 1.0, None, ALU.mult, None, [name], [name + "_der"])
        return dict(tab=tab, der=derv, name=name)

    def tA(self, T, ty, s, o):
        return T['der'][:, 0 if ty == 'x' else 1, s, 0, o:o + 1]

    def tG(self, T, ty, s, o):
        return T['der'][:, 0 if ty == 'x' else 1, s, 1, o:o + 1]

    def tB(self, T, ty, s, o):
        mo = TO['modx' if ty == 'x' else 'modc']
        return T['tab'][:, mo + 3 * s * 8 + o: mo + 3 * s * 8 + o + 1]

    def tcol(self, T, n, i=0):
        return T['tab'][:, TO[n] + i: TO[n] + i + 1]

    def modnorm_block(self, T, s, ty, xsrc, xkeys, dst, dkeys, n, tmp):
        kb = self.kb
        sq, rstd, t32 = tmp['sq'], tmp['rstd'], tmp['t32']
        pb = 7
        for k in range(8):
            i = k % 2
            self.act(sq[i][:, 0:n], xsrc(k), AF.Square, xkeys, ["sq%d" % i])
            self.mm(self.ps(pb, n), self.c_ones, sq[i][:, 0:n], k == 0, k == 7, ["sq%d" % i, "const"], ["ps%d" % pb], True)
        self.act(rstd[:, 0:n], self.ps(pb, n), AF.Sqrt, ["ps%d" % pb], ["rstd"], bias=self.c_eps, scale=1.0 / D)
        self.recip(rstd[:, 0:n], rstd[:, 0:n], ["rstd"], ["rstd"])
        for k in range(8):
            i = k % 2
            self.stt(t32[i][:, 0:n], xsrc(k), self.tA(T, ty, s, k), rstd[:, 0:n], ALU.mult, ALU.mult,
                     list(xkeys) + ["rstd", T['name'] + "_der"], ["t32_%d" % i])
            self.act(dst(k), t32[i][:, 0:n], AF.Identity, ["t32_%d" % i, T['name']], dkeys, bias=self.tB(T, ty, s, k))

    def norm_tmps(self):
        return dict(sq=[self.ar.bf(512), self.ar.bf(512)], rstd=self.ar.f32(512),
                    t32=[self.ar.f32(512), self.ar.f32(512)])

    def ffn(self, T, s, w13_d, w2_d, tag):
        kb = self.kb
        ar = self.ar
        m = ar.mark()
        h = ar.bf(8 * NT).rearrange("p (k t) -> p k t", k=8)
        g = ar.bf(6 * NT).rearrange("p (j t) -> p j t", j=6)
        w13b = [ar.bf(8 * 256).rearrange("p (k c) -> p k c", k=8) for _ in range(2)]
        w2b = [ar.bf(6 * 128).rearrange("p (j c) -> p j c", j=6) for _ in range(2)]
        sa = [ar.f32(512), ar.f32(512)]
        tmp = self.norm_tmps()
        xT = self.xT
        for bi, (t0, n, ty) in enumerate(BLK):
            self.modnorm_block(T, s, ty, lambda k: xT[:, k, t0:t0 + n], ["x%d" % bi],
                               lambda k: h[:, k, t0:t0 + n], ["h%d" % bi], n, tmp)
        it = 0
        w2it = 0
        for (j0, nj) in JPARTS:
            for jj in range(nj):
                j = j0 + jj
                sl = it % 2
                kb.dma('pool', w13b[sl], w13_d[j], writes=["w13b%d" % sl], sem="w13b%d" % sl)
                for bi, (t0, n, ty) in enumerate(BLK):
                    pa, pb_ = (0, 1) if (it * 5 + bi) % 2 == 0 else (2, 3)
                    for k in range(8):
                        self.mm(self.ps(pa, n), w13b[sl][:, k, 0:128], h[:, k, t0:t0 + n], k == 0, k == 7,
                                ["w13b%d" % sl, "h%d" % bi], ["ps%d" % pa], k == 7)
                    for k in range(8):
                        self.mm(self.ps(pb_, n), w13b[sl][:, k, 128:256], h[:, k, t0:t0 + n], k == 0, k == 7,
                                ["w13b%d" % sl, "h%d" % bi], ["ps%d" % pb_], k == 7)
                    si = (it * 5 + bi) % 2
                    self.act(sa[si][:, 0:n], self.ps(pa, n), AF.Silu, ["ps%d" % pa], ["sa%d" % si])
                    self.tt(g[:, jj, t0:t0 + n], sa[si][:, 0:n], self.ps(pb_, n), ALU.mult,
                            ["sa%d" % si, "ps%d" % pb_], ["g%d_%d" % (jj, bi)])
                it += 1
            for o in range(8):
                sl = w2it % 2
                kb.dma('pool', w2b[sl][:, 0:nj, :], w2_d[j0:j0 + nj, :, o, :].rearrange("j p c -> p j c"),
                       writes=["w2b%d" % sl], sem="w2b%d" % sl)
                for bi, (t0, n, ty) in enumerate(BLK):
                    py = 4 + (w2it * 5 + bi) % 2
                    for jj in range(nj):
                        self.mm(self.ps(py, n), w2b[sl][:, jj, :], g[:, jj, t0:t0 + n], jj == 0, jj == nj - 1,
                                ["w2b%d" % sl, "g%d_%d" % (jj, bi)], ["ps%d" % py], jj == nj - 1)
                    self.stt(xT[:, o, t0:t0 + n], self.ps(py, n), self.tG(T, ty, s, o), xT[:, o, t0:t0 + n],
                             ALU.mult, ALU.add, ["ps%d" % py, "x%d" % bi, T['name'] + "_der"], ["x%d" % bi])
                w2it += 1
        kb.barrier()
        ar.release(m)

    def qknorm(self, pb, n, gcol, dst, dkeys, rope_t0, tmp, T):
        qf, sq, rstd, qn, qb, t1 = tmp['qf'], tmp['sq'], tmp['rstd'], tmp['qn'], tmp['qb'], tmp['t1']
        x_ = tmp['sfx']
        p2 = tmp['pb2']
        p2k = "ps%d" % p2
        pk = "ps%d" % pb
        self.act(qf[:, 0:n], self.ps(pb, n), AF.Copy, [pk], ["qf" + x_])
        self.act(sq[:, 0:n], self.ps(pb, n), AF.Square, [pk], ["qsq" + x_])
        self.mm(self.ps(p2, n), self.c_bd, sq[:, 0:n], True, True, ["qsq" + x_, "const"], [p2k], True)
        self.act(rstd[:, 0:n], self.ps(p2, n), AF.Sqrt, [p2k], ["qrstd" + x_], bias=self.c_eps, scale=1.0 / 64)
        self.recip(rstd[:, 0:n], rstd[:, 0:n], ["qrstd" + x_], ["qrstd" + x_])
        if rope_t0 is None:
            self.stt(dst, qf[:, 0:n], gcol, rstd[:, 0:n], ALU.mult, ALU.mult, ["qf" + x_, "qrstd" + x_, T['name']], dkeys)
            return
        self.stt(qn[:, 0:n], qf[:, 0:n], gcol, rstd[:, 0:n], ALU.mult, ALU.mult, ["qf" + x_, "qrstd" + x_, T['name']], ["qn" + x_])
        self.cp(qb[:, 0:n], qn[:, 0:n], ["qn" + x_], ["qb" + x_])
        self.mm(self.ps(p2, n), self.c_perm, qb[:, 0:n], True, True, ["qb" + x_, "const"], [p2k], True)
        self.tt(t1[:, 0:n], qn[:, 0:n], self.cosT[:, rope_t0:rope_t0 + n], ALU.mult, ["qn" + x_, "rope"], ["qt1" + x_])
        self.tt(qn[:, 0:n], self.ps(p2, n), self.sinT[:, rope_t0:rope_t0 + n], ALU.mult, [p2k, "rope"], ["qn" + x_])
        self.tt(dst, t1[:, 0:n], qn[:, 0:n], ALU.add, ["qt1" + x_, "qn" + x_], dkeys)

    def qk_tmps(self, sfx="", pb2=6):
        ar = self.ar
        return dict(qf=ar.f32(512), sq=ar.bf(512), rstd=ar.f32(512), qn=ar.f32(512), qb=ar.bf(512), t1=ar.f32(512),
                    sfx=sfx, pb2=pb2)

    def load_rope(self):
        ar = self.ar
        self.cosT = ar.f32(LT)
        self.sinT = ar.f32(LT)
        self.kb.dma('sp', self.cosT, self.rope_d[0], writes=["rope"], sem="rope", group=True)
        self.kb.dma('sp', self.sinT, self.rope_d[1], writes=["rope"], sem="rope", group=True)

    def load_consts(self):
        ar = self.ar
        kb = self.kb
        cb_d = self.din("cbf", [128, 4, 128])
        cb = ar.bf(4 * 128).rearrange("p (a c) -> p a c", a=4)
        self.c_ident = cb[:, 3, :]
        kb.dma('pool', cb, cb_d, writes=["const"], sem="const", group=True)
        self.c_ones, self.c_bd, self.c_perm = cb[:, 0, :], cb[:, 1, :], cb[:, 2, :]
        self.c_eps = ar.f32(1)
        self.memset(self.c_eps, EPS, ["const"])
        self.c_onesf = ar.f32(128)
        self.memset(self.c_onesf, 1.0 / 256.0, ["const"])

    def part_a(self, l):
        kb, ar = self.kb, self.ar
        m0 = ar.mark()
        T = self.load_tab(l)
        w13_d = self.din("w13_%d_0" % l, [NJ, 128, 8, 256])
        w2_d = self.din("w2_%d_0" % l, [NJ, 128, 8, 128])
        wk_d = self.din("wk_%d" % l, [128, 8, 128])
        wvf_d = self.din("wvf_%d" % l, [128, 8, 384])
        self.ffn(T, 0, w13_d, w2_d, "f1")
        xk = ["x%d" % i for i in range(5)]
        kb.dma('sp', self.xdram.ap(), self.xT, reads=xk, writes=["xdram"], sem="xout")
        xe = self.b_xe.ap().rearrange("p (k t) -> p k t", k=8)
        m = ar.mark()
        u = ar.bf(8 * NT).rearrange("p (k t) -> p k t", k=8)
        wk = ar.bf(8 * 128).rearrange("p (k c) -> p k c", k=8)
        wvf = ar.bf(8 * 384).rearrange("p (k c) -> p k c", k=8)
        kb.dma('pool', wk, wk_d, writes=["wk"], sem="wk")
        kb.dma('pool', wvf, wvf_d, writes=["wvf"], sem="wvf")
        self.load_rope()
        tmp = self.norm_tmps()
        qt = self.qk_tmps("a", 6)
        qt2 = self.qk_tmps("b", 5)
        ko = [ar.f32(512), ar.f32(512)]
        vfo = [ar.f32(384), ar.f32(384)]
        xT = self.xT
        kT_d = self.b_k.ap()
        vf_d = self.b_vf.ap()
        for bi, (t0, n, ty) in enumerate(BLK):
            self.modnorm_block(T, 1, ty, lambda k: xT[:, k, t0:t0 + n], ["x%d" % bi],
                               lambda k: u[:, k, t0:t0 + n], ["u%d" % bi], n, tmp)
        kb.dma('sp', xe[:, :, 0:16], u[:, :, 0:16], reads=["u0"], writes=["b_xe"], sem="xe", group=True)
        kb.dma('sp', xe[:, :, 16:32], u[:, :, LT - 16:LT], reads=["u3"], writes=["b_xe"], sem="xe", group=True)
        kb.dma('sp', self.udram.ap(), u, reads=["u%d" % i for i in range(5)], writes=["udram"], sem="uout")
        for bi, (t0, n, ty) in enumerate(LBLK):
            pb = bi % 2
            for k in range(8):
                self.mm(self.ps(pb, n), wk[:, k, :], u[:, k, t0:t0 + n], k == 0, k == 7, ["wk", "u%d" % bi],
                        ["ps%d" % pb], k == 7)
            self.qknorm(pb, n, self.tcol(T, 'kg'), ko[bi % 2][:, 0:n], ["ko%d" % (bi % 2)], t0, qt if bi % 2 == 0 else qt2, T)
            kb.dma('sp', kT_d[:, t0:t0 + n], ko[bi % 2][:, 0:n], reads=["ko%d" % (bi % 2)], writes=["b_k%d" % (bi % 2)],
                   sem="kout%d" % (bi % 2))
        for tt_ in range(LT // 128):
            pb = 2 + tt_ % 2
            bi = tt_ // 4
            for k in range(8):
                self.mm(self.ps(pb, 384), u[:, k, tt_ * 128:(tt_ + 1) * 128], wvf[:, k, :], k == 0, k == 7,
                        ["wvf", "u%d" % bi], ["ps%d" % pb], k == 7)
            self.act(vfo[tt_ % 2], self.ps(pb, 384), AF.Copy, ["ps%d" % pb], ["vfo%d" % (tt_ % 2)])
            kb.dma('sp', vf_d[tt_ * 128:(tt_ + 1) * 128, :], vfo[tt_ % 2], reads=["vfo%d" % (tt_ % 2)],
                   writes=["b_vf%d" % (tt_ % 2)], sem="vfout%d" % (tt_ % 2))
        kb.cc(self.b_xe, self.g_xe, reads=["b_xe"], writes=["g_xe"])
        kb.cc(self.b_k, self.g_k, reads=["b_k0", "b_k1"], writes=["g_k"])
        kb.cc(self.b_vf, self.g_vf, reads=["b_vf0", "b_vf1"], writes=["g_vf"])
        kb.barrier()
        ar.release(m0)

    def mod_phase(self):
        kb, ar = self.kb, self.ar
        m = ar.mark()
        cT_d = self.din("cT", [128, 8, 2])
        w_d = self.din("wada", [36, 128, 8, 128])
        b_d = self.din("bada", [128, 36])
        cT = ar.f32(16).rearrange("p (k w) -> p k w", k=8)
        sc = ar.f32(16).rearrange("p (k w) -> p k w", k=8)
        bs = ar.f32(36)
        os_ = ar.f32(72).rearrange("p (w g) -> p w g", w=2)
        wb = [ar.f32(1024).rearrange("p (k n) -> p k n", k=8) for _ in range(3)]
        kb.dma('sp', cT, cT_d, writes=["cT"], sem="cT")
        kb.dma('sp', bs, b_d, writes=["bs"], sem="bs")
        self.act(sc, cT, AF.Silu, ["cT"], ["sc"])
        for g in range(36):
            sl = g % 3
            kb.dma('sp', wb[sl], w_d[g], writes=["wb%d" % sl], sem="wb%d" % sl)
            for k in range(8):
                self.mm(self.pst[:, 2 * g:2 * g + 2], wb[sl][:, k, :], sc[:, k, :], k == 0, k == 7,
                        ["wb%d" % sl, "sc"], ["ps0"], k == 7)
        psv = self.pst[:, 0:72].rearrange("p (g w) -> p g w", w=2)
        for w in range(2):
            self.tt(os_[:, w, :], psv[:, :, w], bs, ALU.add, ["ps0", "bs"], ["os"])
        kb.dma('sp', self.b_mod.ap(), os_.rearrange("p w g -> p (w g)"), reads=["os"], writes=["b_mod"], sem="modout")
        kb.cc(self.b_mod, self.g_mod, reads=["b_mod"], writes=["g_mod"])
        kb.barrier()
        ar.release(m)

    def part_b(self, l):
        kb, ar = self.kb, self.ar
        mB = ar.mark()
        T = self.load_tab(l)
        wfm_d = self.din("wfm_%d" % l, [128, 15, 8, 128])
        wvf_d = self.din("wvf_%d" % l, [128, 8, 384])
        wg_d = self.din("wg_%d" % l, [8, 128, 4, 8, 128])
        wbr_d = self.din("wbr_%d" % l, [8, 128, 10, 128])
        wo_d = self.din("wo_%d" % l, [8, 128, 8, 128])
        w13_d = self.din("w13_%d_1" % l, [NJ, 128, 8, 256])
        w2_d = self.din("w2_%d_1" % l, [NJ, 128, 8, 128])
        xin = self.xdram.ap()
        gk = self.g_k.ap().rearrange("(r p) t -> p r t", p=128)
        vf_d = self.g_vf.ap()
        hm_d, cs_d, mt_d, c256_d, cbsb_d = self.hm_d, self.cs_d, self.mt_d, self.c256_d, self.cbsb_d
        one1 = ar.f32(64)
        self.memset(one1, 1.0, ["const"])
        Xreg = self.xT
        Xflat = Xreg.rearrange("p k t -> p (k t)")
        xoff = [0]

        def xf32(n):
            o = xoff[0]
            xoff[0] += n
            assert xoff[0] <= 8 * NT, ("X region overflow", xoff[0])
            return Xflat[:, o:o + n]

        def xbf(n):
            w = (n + 1) // 2
            return xf32(w).bitcast(BF16)[:, 0:n]

        m0 = ar.mark()
        attnT = ar.bf(4 * NT).rearrange("p (k t) -> p k t", k=4)
        convo = ar.bf(2 * NT).rearrange("p (k t) -> p k t", k=2)
        sco = ar.bf(2 * NT).rearrange("p (k t) -> p k t", k=2)
        Y = ar.bf(2 * NT).rearrange("p (k t) -> p k t", k=2)
        KTc = ar.bf(CT)
        Vc = ar.bf(2 * 2 * 65).rearrange("p (a b c) -> p a b c", a=2, b=2)
        Fc = ar.bf(2 * 256).rearrange("p (a c) -> p a c", a=2)
        mP = ar.mark()
        qT = ar.bf(4 * NT).rearrange("p (k t) -> p k t", k=4)
        NU = NT + NH
        ABLK = BLK + [HBLK]
        xoff[0] = 0
        u = xbf(8 * NU).rearrange("p (k t) -> p k t", k=8)
        m1 = ar.mark()
        ud = self.udram.ap()
        for bi, (t0, n, ty) in enumerate(BLK):
            kb.dma('sp', u[:, :, t0:t0 + n], ud[:, :, t0:t0 + n], reads=["udram"], writes=["u%d" % bi], sem="uin%d" % bi)
        E = ar.bf(8 * 256).rearrange("p (r k t) -> p r k t", r=8, k=8)
        kb.dma('sp', E.rearrange("p r k t -> p r (k t)"), self.g_xe.ap().rearrange("(r p) n -> p r n", p=128),
               reads=["g_xe"], writes=["E"], sem="E")
        hs = ar.f32(16)
        kb.dma('sp', hs, self.hsel_d, writes=["hs"], sem="hs")
        xhal = ar.f32(8 * NH).rearrange("p (k t) -> p k t", k=8)
        for side, (d0, s0) in enumerate(((0, 16), (16, 0))):
            dst = xhal[:, :, d0:d0 + 16]
            for r in range(8):
                sc_ = hs[:, side * 8 + r:side * 8 + r + 1]
                if r == 0:
                    self.ts(dst, E[:, r, :, s0:s0 + 16], sc_, None, ALU.mult, None, ["E", "hs"], ["xhal"])
                else:
                    self.stt(dst, E[:, r, :, s0:s0 + 16], sc_, dst, ALU.mult, ALU.add, ["E", "hs", "xhal"], ["xhal"])
        self.cp(u[:, :, NT:NT + NH], xhal, ["xhal"], ["u5"])
        kb.barrier()
        ar.release(m1)
        self.debug("u", u, ["u%d" % i for i in range(6)], [128, 8, NU], BF16)
        m1 = ar.mark()
        xm = xoff[0]
        KT = None
        wq = ar.bf(5 * 8 * 128).rearrange("p (c k n) -> p c k n", c=5, k=8)
        kb.dma('pool', wq, wfm_d[:, 0:5], writes=["wq"], sem="wq")
        wvf = ar.bf(8 * 384).rearrange("p (k c) -> p k c", k=8)
        kb.dma('pool', wvf, wvf_d, writes=["wvf"], sem="wvf")
        self.load_rope()
        qt = self.qk_tmps("a", 6)
        qt2 = self.qk_tmps("b", 7)
        self.memset(Vc[:, :, :, 64:65], 1.0, ["Vc1"])
        it = 0
        for c in range(4):
            for bi, (t0, n, ty) in enumerate(BLK):
                pb = it % 4
                it += 1
                for k in range(8):
                    self.mm(self.ps(pb, n), wq[:, c, k, :], u[:, k, t0:t0 + n], k == 0, k == 7, ["wq", "u%d" % bi],
                            ["ps%d" % pb], k == 7)
                self.qknorm(pb, n, self.tcol(T, 'qg'), qT[:, c, t0:t0 + n], ["qT%d_%d" % (c, bi)],
                            t0 if ty == 'x' else None, qt if it % 2 == 0 else qt2, T)
        t0, n, ty = BLK[4]
        for k in range(8):
            self.mm(self.ps(0, n), wq[:, 4, k, :], u[:, k, t0:t0 + n], k == 0, k == 7, ["wq", "u4"], ["ps0"], k == 7)
        self.qknorm(0, n, self.tcol(T, 'kg'), KTc[:, 0:n], ["KTc"], None, qt, T)
        for tt_ in range(2):
            pb = 1 + tt_
            for k in range(8):
                self.mm(self.ps(pb, 384), u[:, k, LT + tt_ * 128:LT + (tt_ + 1) * 128], wvf[:, k, :], k == 0, k == 7,
                        ["wvf", "u4"], ["ps%d" % pb], k == 7)
            self.act(Vc[:, tt_, :, 0:64], self.ps(pb, 128).rearrange("p (a d) -> p a d", a=2), AF.Copy,
                     ["ps%d" % pb], ["Vc"])
            self.act(Fc[:, tt_, :], self.pst[:, pb * 512 + 128:pb * 512 + 384], AF.Copy, ["ps%d" % pb], ["Fc"])
        kb.barrier()
        ar.release(m1)
        self.debug("qT", qT, [], [128, 4, NT], BF16)
        m1 = ar.mark()
        wc_raw = ar.f32(3 * 8 * 128)
        wcb = wc_raw.bitcast(BF16).rearrange("p (c k n) -> p c k n", c=6, k=8)
        HB = LT + 30
        PBW = LT + 2
        hb = xbf(2 * HB).rearrange("p (i t) -> p i t", i=2)
        pbf = xf32(2 * PBW).rearrange("p (i t) -> p i t", i=2)
        hbc = xbf(2 * (CT + 30)).rearrange("p (i t) -> p i t", i=2)
        dg = ar.bf(62 * 128).rearrange("p (a c) -> p a c", a=62)
        for a_ in range(62):
            self.ts(dg[:, a_, :], self.c_ident, self.tcol(T, 'cw', a_), None, ALU.mult, None, ["const", T['name']], ["dg"])
        pbc = xf32(2 * (CT + 2)).rearrange("p (i t) -> p i t", i=2)
        gb = ar.bf(2 * NT).rearrange("p (i t) -> p i t", i=2)
        acc = ar.f32(2 * NT).rearrange("p (i t) -> p i t", i=2)
        sg = [xf32(512), xf32(512)]
        hht = ar.f32(2 * NH).rearrange("p (i t) -> p i t", i=2)
        pht = ar.f32(2 * NH).rearrange("p (i t) -> p i t", i=2)
        hm = ar.f32(NH)
        kb.dma('sp', hm, hm_d, writes=["hm"], sem="hm")
        self.memset(hbc, 0.0, ["hbc"])
        self.memset(pbc, 0.0, ["pbc"])
        it = 0
        kb.dma('pool', wcb[:, 0:4], wfm_d[:, 5:9], writes=["wc"], sem="wc")
        for bi, (t0, n, ty) in enumerate(ABLK):
            uk = "u%d" % bi
            for i in range(2):
                for k in range(8):
                    self.mm(self.ps(0, n), wcb[:, i, k, :], u[:, k, t0:t0 + n], k == 0, k == 7, ["wc", uk], ["ps0"], k == 7)
                for k in range(8):
                    self.mm(self.ps(1, n), wcb[:, 2 + i, k, :], u[:, k, t0:t0 + n], k == 0, k == 7, ["wc", uk], ["ps1"], k == 7)
                si = it % 2
                it += 1
                self.act(sg[si][:, 0:n], self.ps(1, n), AF.Sigmoid, ["ps1"], ["sg%d" % si])
                if bi < 4:
                    dst = hb[:, i, 15 + t0:15 + t0 + n]
                    dk = "hb"
                elif bi == 4:
                    dst = hbc[:, i, 15:15 + CT]
                    dk = "hbc"
                else:
                    dst = hht[:, i, :]
                    dk = "hht"
                self.tt(dst, sg[si][:, 0:n], self.ps(0, n), ALU.mult, ["sg%d" % si, "ps0"], [dk])
        kb.dma('pool', wcb[:, 0:6], wfm_d[:, 9:15], writes=["wc"], sem="wc")
        for bi, (t0, n, ty) in enumerate(ABLK):
            uk = "u%d" % bi
            for i in range(2):
                for k in range(8):
                    self.mm(self.ps(2, n), wcb[:, 2 + i, k, :], u[:, k, t0:t0 + n], k == 0, k == 7, ["wc", uk], ["ps2"], k == 7)
                for k in range(8):
                    self.mm(self.ps(3, n), wcb[:, 4 + i, k, :], u[:, k, t0:t0 + n], k == 0, k == 7, ["wc", uk], ["ps3"], k == 7)
                si = it % 2
                it += 1
                self.act(sg[si][:, 0:n], self.ps(2, n), AF.Copy, ["ps2"], ["sg%d" % si])
                if bi < 4:
                    dst = pbf[:, i, 1 + t0:1 + t0 + n]
                    dk = "pb"
                elif bi == 4:
                    dst = pbc[:, i, 1:1 + CT]
                    dk = "pbc"
                else:
                    dst = pht[:, i, :]
                    dk = "pht"
                self.tt(dst, sg[si][:, 0:n], self.ps(3, n), ALU.mult, ["sg%d" % si, "ps3"], [dk])
                if bi < 5:
                    for k in range(8):
                        self.mm(self.ps(5, n), wcb[:, i, k, :], u[:, k, t0:t0 + n], k == 0, k == 7, ["wc", uk], ["ps5"], k == 7)
                    self.act(gb[:, i, t0:t0 + n], self.ps(5, n), AF.Copy, ["ps5"], ["gb"])
        for i in range(2):
            self.tt(hht[:, i, :], hht[:, i, :], hm, ALU.mult, ["hht", "hm"], ["hht"])
            self.tt(pht[:, i, :], pht[:, i, :], hm, ALU.mult, ["pht", "hm"], ["pht"])
            self.cp(hb[:, i, 0:15], hht[:, i, 1:16], ["hht"], ["hb"])
            self.cp(hb[:, i, 15 + LT:30 + LT], hht[:, i, 16:31], ["hht"], ["hb"])
            self.cp(pbf[:, i, 0:1], pht[:, i, 15:16], ["pht"], ["pb"])
            self.cp(pbf[:, i, 1 + LT:2 + LT], pht[:, i, 16:17], ["pht"], ["pb"])
        for i in range(2):
            for (src, skey, n0, d0) in ((pbf, "pb", LT, 0), (pbc, "pbc", CT, LT)):
                a_ = acc[:, i, d0:d0 + n0]
                self.ts(a_, src[:, i, 0:n0], self.tcol(T, 'scw', i * 3), None, ALU.mult, None, [skey, T['name']], ["acc"])
                for tap in range(1, 3):
                    self.stt(a_, src[:, i, tap:tap + n0], self.tcol(T, 'scw', i * 3 + tap), a_, ALU.mult, ALU.add,
                             [skey, "acc", T['name']], ["acc"])
            self.tt(sco[:, i, :], acc[:, i, :], gb[:, i, :], ALU.mult, ["acc", "gb"], ["sco"])
        itc = 0
        for i in range(2):
            for (src, skey, blks, d0) in ((hb, "hb", LBLK, 0), (hbc, "hbc", [(0, CT, 'c')], LT)):
                for (t0, n, ty) in blks:
                    pb = 2 + itc % 2
                    itc += 1
                    for tap in range(31):
                        self.mm(self.ps(pb, n), dg[:, i * 31 + tap, :], src[:, i, t0 + tap:t0 + tap + n], tap == 0, tap == 30,
                                ["dg", skey], ["ps%d" % pb], tap == 30)
                    self.act(acc[:, i, d0 + t0:d0 + t0 + n], self.ps(pb, n), AF.Identity, ["ps%d" % pb, T['name']], ["acc"],
                             bias=self.tcol(T, 'cb', i))
        kb.barrier()
        lt_ = [wc_raw[:, i_ * 512:(i_ + 1) * 512] for i_ in range(4)]
        for bi, (t0, n, ty) in enumerate(BLK):
            for i in range(2):
                self.mm(self.ps(0, n), self.c_onesf, acc[:, i, t0:t0 + n], i == 0, i == 1, ["acc", "const"], ["ps0"], i == 1)
            for i in range(2):
                self.act(lt_[i][:, 0:n], acc[:, i, t0:t0 + n], AF.Square, ["acc"], ["lt%d" % i])
                self.mm(self.ps(1, n), self.c_onesf, lt_[i][:, 0:n], i == 0, i == 1, ["lt%d" % i, "const"], ["ps1"], i == 1)
            self.act(lt_[2][:, 0:n], self.ps(0, n), AF.Square, ["ps0"], ["lt2"])
            self.tt(lt_[3][:, 0:n], self.ps(1, n), lt_[2][:, 0:n], ALU.subtract, ["ps1", "lt2"], ["lt3"])
            self.act(lt_[3][:, 0:n], lt_[3][:, 0:n], AF.Sqrt, ["lt3"], ["lt3"], bias=self.c_eps, scale=1.0)
            self.recip(lt_[3][:, 0:n], lt_[3][:, 0:n], ["lt3"], ["lt3"])
            for i in range(2):
                self.tt(lt_[i][:, 0:n], acc[:, i, t0:t0 + n], self.ps(0, n), ALU.subtract, ["acc", "ps0"], ["lt%d" % i])
                self.tt(lt_[i][:, 0:n], lt_[i][:, 0:n], lt_[3][:, 0:n], ALU.mult, ["lt%d" % i, "lt3"], ["lt%d" % i])
                self.act(convo[:, i, t0:t0 + n], lt_[i][:, 0:n], AF.Silu, ["lt%d" % i, T['name']], ["convo"],
                         bias=self.tcol(T, 'lb', i), scale=self.tcol(T, 'lg', i))
        kb.barrier()
        ar.release(m1)
        self.debug("convo", convo, [], [128, 2, NT], BF16)
        self.debug("sco", sco, [], [128, 2, NT], BF16)
        m1 = ar.mark()
        xoff[0] = 0
        NKC = 2 + L // 128
        KT = xbf(128 * NKC)
        V = xbf(NKC * 2 * 65).rearrange("p (n a c) -> p n a c", n=NKC, a=2)
        self.memset(V[:, 2:, :, 64:65], 1.0, ["V"])
        self.cp(KT[:, 0:CT], KTc, ["KTc"], ["KT"])
        self.cp(V[:, 0:2], Vc, ["Vc", "Vc1"], ["V"])
        for r in range(8):
            kb.dma('pool', KT[:, CT + r * LT:CT + (r + 1) * LT], gk[:, r, :], reads=["g_k"],
                   writes=["KT"], sem="KTd", group=True)
        vsrc = vf_d.rearrange("(n p) c -> p n c", p=128)
        for qd in range(4):
            for a in range(2):
                kb.dma('pool', V[:, 2 + qd * 32:2 + (qd + 1) * 32, a, 0:64], vsrc[:, qd * 32:(qd + 1) * 32, a * 64:(a + 1) * 64],
                       reads=["g_vf"], writes=["V"], sem="Vd", group=True)
        pt = [ar.bf(1024).rearrange("p (a n) -> p a n", a=2) for _ in range(3)]
        ou = [ar.f32(512), ar.f32(512)]
        rden = ar.f32(512)
        obB = [ar.bf(512), ar.bf(512)]
        itp = 0
        fin = 0
        for (t0, n, ty) in BLK:
            chunks = list(range(NKC)) if ty == 'x' else [0, 1]
            bi = t0 // 512
            for c in range(4):
                def qk(kc, sb):
                    self.mm(self.ps(2 * sb, n), KT[0:64, kc * 128:(kc + 1) * 128], qT[0:64, c, t0:t0 + n], True, True,
                            ["KT", "qT%d_%d" % (c, bi)], ["S%d" % sb], False)
                    self.mm(self.ps(2 * sb + 1, n), KT[64:128, kc * 128:(kc + 1) * 128], qT[64:128, c, t0:t0 + n], True, True,
                            ["KT", "qT%d_%d" % (c, bi)], ["S%d" % sb], True)
                LA = 2
                for i_ in range(min(LA, len(chunks))):
                    qk(chunks[i_], i_ % 3)
                for ci, kc in enumerate(chunks):
                    sb = ci % 3
                    if ci + LA < len(chunks):
                        qk(chunks[ci + LA], (ci + LA) % 3)
                    P = pt[itp % 3]
                    pk = "P%d" % (itp % 3)
                    itp += 1
                    sin_ = self.pst[:, 2 * sb * 512:2 * sb * 512 + 1024].rearrange("p (a n) -> p a n", a=2)[:, :, 0:n]
                    self.act(P[:, :, 0:n], sin_, AF.Exp, ["S%d" % sb], [pk], scale=0.125)
                    first, last = ci == 0, ci == len(chunks) - 1
                    self.mm(self.ps(6, n, 65), V[:, kc, 0, :], P[:, 0, 0:n], first, last, ["V", pk], ["acc4"], False)
                    self.mm(self.ps(7, n, 65), V[:, kc, 1, :], P[:, 1, 0:n], first, last, ["V", pk], ["acc4"], True)
                for hh in range(2):
                    pa = 6 + hh
                    f = fin % 2
                    fin += 1
                    self.act(ou[f][0:65, 0:n], self.ps(pa, n, 65), AF.Copy, ["acc4"], ["ou%d" % f])
                    self.act(rden[64:65, 0:n], ou[f][64:65, 0:n], AF.Ln, ["ou%d" % f], ["rden"])
                    self.act(rden[64:65, 0:n], rden[64:65, 0:n], AF.Exp, ["rden"], ["rden"], scale=-1.0)
                    self.mm(self.ps(4, n, 64), one1[64:65, 0:64], rden[64:65, 0:n], True, True, ["rden", "const"], ["S2"], True)
                    if hh == 0:
                        self.tt(attnT[0:64, c, t0:t0 + n], ou[f][0:64, 0:n], self.ps(4, n, 64), ALU.mult,
                                ["ou%d" % f, "S2"], ["attnT"])
                    else:
                        self.tt(obB[f][0:64, 0:n], ou[f][0:64, 0:n], self.ps(4, n, 64), ALU.mult,
                                ["ou%d" % f, "S2"], ["obB%d" % f])
                        kb.dma('sp', attnT[64:128, c, t0:t0 + n], obB[f][0:64, 0:n], reads=["obB%d" % f], writes=["attnT"],
                               sem="obB%d" % f)
        kb.barrier()
        ar.release(mP)
        self.debug("attnT", attnT, [], [128, 4, NT], BF16)
        m1 = ar.mark()
        xoff[0] = 0
        Fg = xbf(128 * 64).rearrange("p (n c) -> p n c", n=128)
        Asb = xbf(2 * 128 * 64).rearrange("p (k c r) -> p k c r", k=128, c=64)
        MT = ar.bf(128 * 2 * 32).rearrange("p (k s n) -> p k s n", k=128, s=2)
        CS = xbf(256)
        Xs = xbf(2 * 2048).rearrange("p (r n) -> p r n", r=2)
        C256 = xbf(2 * 2 * 256).rearrange("p (h r n) -> p h r n", h=2, r=2)
        CBSB = xbf(2 * 64).rearrange("p (r n) -> p r n", r=2)
        kb.dma('pool', MT, mt_d, writes=["ftab"], sem="ftab", group=True)
        kb.dma('pool', CS, cs_d, writes=["ftab"], sem="ftab", group=True)
        kb.dma('pool', C256, c256_d, writes=["ftab"], sem="ftab", group=True)
        kb.dma('pool', CBSB[0:64], cbsb_d, writes=["ftab"], sem="ftab", group=True)
        fsrc = vf_d.rearrange("(a b) c -> a b c", b=128)
        ev = 0
        for g in range(4):
            for qd in range(4):
                kb.dma('pool', Fg[:, qd * 32:(qd + 1) * 32, :], fsrc[:, qd * 32:(qd + 1) * 32, 128 + g * 64:128 + (g + 1) * 64],
                       reads=["g_vf"], writes=["Fg"], sem="Fg", group=True)
            for c0 in range(0, 64, 2):
                pb = ((c0 // 2) % 2) * 2
                k0, k1_ = "ps%d" % pb, "ps%d" % (pb + 1)
                self.mm(self.ps(pb, 256), Fg[:, :, c0], CS, True, True, ["Fg", "ftab"], [k0], True)
                self.mm(self.ps(pb + 1, 256), Fg[:, :, c0 + 1], CS, True, True, ["Fg", "ftab"], [k1_], True)
                for b_ in range(2):
                    src = self.ps(pb + b_, 256).rearrange("p (r k) -> p k r", r=2)
                    dst = Asb[:, :, c0 + b_, :]
                    kk = "ps%d" % (pb + b_)
                    if b_ == 0:
                        self.act(dst, src, AF.Copy, [kk], ["Asb"])
                    else:
                        self.cp(dst, src, [kk], ["Asb"])
            for k1h in range(2):
                for k1l in range(64):
                    k1 = k1h * 64 + k1l
                    col = k1l * 32
                    o_ = self.pst[0:64, col:col + 32]
                    lastk = k1l == 63
                    self.mm(o_, Asb[:, k1, :, 0], MT[:, k1, 0, :], True, False, ["Asb", "ftab"], ["ps0", "ps1", "ps2", "ps3"], False)
                    self.mm(o_, Asb[:, k1, :, 1], MT[:, k1, 1, :], False, True, ["Asb", "ftab"], ["ps0", "ps1", "ps2", "ps3"], lastk)
                srcv = self.pst[0:64, 0:2048].rearrange("p (k r a) -> p k r a", r=2, a=16)
                for r in range(2):
                    src = srcv[:, :, r, :]
                    dst = Xs[0:64, r, :].rearrange("p (a k) -> p k a", k=128)[:, k1h * 64:(k1h + 1) * 64, :]
                    if r == 0:
                        self.act(dst, src, AF.Copy, ["ps0", "ps1", "ps2", "ps3"], ["Xs"])
                    else:
                        self.cp(dst, src, ["ps0", "ps1", "ps2", "ps3"], ["Xs"])
            ro = (g % 2) * 64
            for tb in range(4):
                pb = 4 + tb % 2
                o_ = self.pst[ro:ro + 64, pb * 512:pb * 512 + 512]
                self.mm(o_, CBSB[0:64, 0, :], Xs[0:64, 0, tb * 512:(tb + 1) * 512], True, False, ["Xs", "ftab"], ["ps%d" % pb], False)
                self.mm(o_, CBSB[0:64, 1, :], Xs[0:64, 1, tb * 512:(tb + 1) * 512], False, True, ["Xs", "ftab"], ["ps%d" % pb], True)
                self.act(Y[ro:ro + 64, g // 2, tb * 512:(tb + 1) * 512], o_, AF.Copy, ["ps%d" % pb], ["Y"])
            xc = [self.pst[0:64, 6 * 512:6 * 512 + 256], self.pst[0:64, 7 * 512:7 * 512 + 256]]
            for r in range(2):
                for nh in range(2):
                    self.mm(xc[r], Fc[:, nh, g * 64:(g + 1) * 64], C256[:, nh, r, :], nh == 0, nh == 1, ["Fc", "ftab"],
                            ["ps%d" % (6 + r)], nh == 1)
                self.cp(Xs[0:64, r, 0:256], xc[r], ["ps%d" % (6 + r)], ["Xs"])
            o_ = self.pst[ro:ro + 64, 4 * 512:4 * 512 + 256]
            self.mm(o_, CBSB[0:64, 0, :], Xs[0:64, 0, 0:256], True, False, ["Xs", "ftab"], ["ps4"], False)
            self.mm(o_, CBSB[0:64, 1, :], Xs[0:64, 1, 0:256], False, True, ["Xs", "ftab"], ["ps4"], True)
            self.act(Y[ro:ro + 64, g // 2, LT:NT], o_, AF.Copy, ["ps4"], ["Y"])
        kb.barrier()
        ar.release(m1)
        self.debug("Y", Y, [], [128, 2, NT], BF16)
        m1 = ar.mark()
        xoff[0] = 0
        u = xbf(8 * NT).rearrange("p (k t) -> p k t", k=8)
        wgb = [xbf(4 * 8 * 128).rearrange("p (b k n) -> p b k n", b=4, k=8) for _ in range(2)]
        wbrb = [xbf(10 * 128).rearrange("p (k n) -> p k n", k=10) for _ in range(2)]
        merged = ar.bf(8 * NT).rearrange("p (k t) -> p k t", k=8)
        for bi, (t0, n, ty) in enumerate(BLK):
            kb.dma('sp', u[:, :, t0:t0 + n], self.udram.ap()[:, :, t0:t0 + n], reads=["udram"], writes=["u%d" % bi],
                   sem="uin%d" % bi)
        sg = [ar.f32(512), ar.f32(512)]
        macc = [ar.f32(512), ar.f32(512)]
        mt2 = [ar.f32(512), ar.f32(512)]
        brs = [(attnT, 4, 0, "attnT"), (Y, 2, 4, "Y"), (convo, 2, 6, "convo"), (sco, 2, 8, "sco")]
        it = 0
        for o in range(8):
            sl = o % 2
            kb.dma('pool', wgb[sl], wg_d[o], writes=["wg%d" % sl], sem="wg%d" % sl)
            kb.dma('pool', wbrb[sl], wbr_d[o], writes=["wbr%d" % sl], sem="wbr%d" % sl)
            for bi, (t0, n, ty) in enumerate(BLK):
                ma = macc[bi % 2]
                mk = "macc%d" % (bi % 2)
                for br, (src, nk, k0, skey) in enumerate(brs):
                    pg = it % 2
                    py = 2 + it % 2
                    si = it % 2
                    it += 1
                    for k in range(8):
                        self.mm(self.ps(pg, n), wgb[sl][:, br, k, :], u[:, k, t0:t0 + n], k == 0, k == 7,
                                ["wg%d" % sl, "u%d" % bi], ["ps%d" % pg], k == 7)
                    for k in range(nk):
                        self.mm(self.ps(py, n), wbrb[sl][:, k0 + k, :], src[:, k, t0:t0 + n], k == 0, k == nk - 1,
                                ["wbr%d" % sl, skey], ["ps%d" % py], k == nk - 1)
                    self.act(sg[si][:, 0:n], self.ps(pg, n), AF.Sigmoid, ["ps%d" % pg, T['name']], ["sg%d" % si],
                             bias=self.tcol(T, 'bg', br * 8 + o))
                    if br == 0:
                        self.tt(ma[:, 0:n], sg[si][:, 0:n], self.ps(py, n), ALU.mult, ["sg%d" % si, "ps%d" % py], [mk])
                    else:
                        self.tt(mt2[si][:, 0:n], sg[si][:, 0:n], self.ps(py, n), ALU.mult, ["sg%d" % si, "ps%d" % py], ["mt%d" % si])
                        if br < 3:
                            self.tt(ma[:, 0:n], ma[:, 0:n], mt2[si][:, 0:n], ALU.add, [mk, "mt%d" % si], [mk])
                        else:
                            self.tt(merged[:, o, t0:t0 + n], ma[:, 0:n], mt2[si][:, 0:n], ALU.add, [mk, "mt%d" % si], ["merged%d" % bi])
        kb.barrier()
        self.debug("merged", merged, [], [128, 8, NT], BF16)
        xT = self.xT
        for bi, (t0, n, ty) in enumerate(BLK):
            kb.dma('sp', xT[:, :, t0:t0 + n], xin[:, :, t0:t0 + n], reads=["xdram"], writes=["x%d" % bi], sem="xin%d" % bi)
        wob = [ar.bf(8 * 128).rearrange("p (k n) -> p k n", k=8) for _ in range(2)]
        it = 0
        for o in range(8):
            sl = o % 2
            kb.dma('pool', wob[sl], wo_d[o], writes=["wo%d" % sl], sem="wo%d" % sl)
            for bi, (t0, n, ty) in enumerate(BLK):
                py = 4 + it % 2
                it += 1
                for k in range(8):
                    self.mm(self.ps(py, n), wob[sl][:, k, :], merged[:, k, t0:t0 + n], k == 0, k == 7,
                            ["wo%d" % sl, "merged%d" % bi], ["ps%d" % py], k == 7)
                self.stt(xT[:, o, t0:t0 + n], self.ps(py, n), self.tG(T, ty, 1, o), xT[:, o, t0:t0 + n],
                         ALU.mult, ALU.add, ["ps%d" % py, "x%d" % bi, T['name'] + "_der"], ["x%d" % bi])
        kb.barrier()
        ar.release(m0)
        self.debug("x_mix", self.xT, ["x%d" % i for i in range(5)], [128, 8, NT])
        self.ffn(T, 2, w13_d, w2_d, "f2")
        self.debug("x_ffn2", self.xT, ["x%d" % i for i in range(5)], [128, 8, NT])
        kb.barrier()
        ar.release(mB)

    def final_norm(self):
        kb, ar = self.kb, self.ar
        g_d = self.din("fng", [128, 8])
        gt = ar.f32(8)
        kb.dma('sp', gt, g_d, writes=["fng"], sem="fng")
        y_d = self.dout("yT_out", [128, 8, LT])
        tmp = self.norm_tmps()
        yo = [ar.f32(512), ar.f32(512)]
        xT = self.xT
        it = 0
        for bi, (t0, n, ty) in enumerate(LBLK):
            sq, rstd = tmp['sq'], tmp['rstd']
            for k in range(8):
                i = k % 2
                self.act(sq[i][:, 0:n], xT[:, k, t0:t0 + n], AF.Square, ["x%d" % bi], ["sq%d" % i])
                self.mm(self.ps(7, n), self.c_ones, sq[i][:, 0:n], k == 0, k == 7, ["sq%d" % i, "const"], ["ps7"], True)
            self.act(rstd[:, 0:n], self.ps(7, n), AF.Sqrt, ["ps7"], ["rstd"], bias=self.c_eps, scale=1.0 / D)
            self.recip(rstd[:, 0:n], rstd[:, 0:n], ["rstd"], ["rstd"])
            for k in range(8):
                i = it % 2
                it += 1
                self.stt(yo[i][:, 0:n], xT[:, k, t0:t0 + n], gt[:, k:k + 1], rstd[:, 0:n], ALU.mult, ALU.mult,
                         ["x%d" % bi, "rstd", "fng"], ["yo%d" % i])
                kb.dma('sp', y_d[:, k, t0:t0 + n], yo[i][:, 0:n], reads=["yo%d" % i], sem="yout%d" % i)

    def build(self):
        kb, ar, nc = self.kb, self.ar, self.nc
        self.load_consts()
        self.rope_d = self.din("rope", [2, 128, LT])
        self.hm_d = self.din("hmask", [128, NH])
        self.hsel_d = self.din("hsel", [128, 16])
        self.cs_d = self.din("cs128", [128, 256])
        self.mt_d = self.din("mtab", [128, 128, 2, 32])
        self.c256_d = self.din("c256", [128, 2, 2, 256])
        self.cbsb_d = self.din("cbsb", [64, 2, 64])
        self.xdram = nc.dram_tensor("xdram", [128, 8, NT], F32)
        self.b_k = nc.dram_tensor("b_k", [128, LT], F32)
        self.g_k = nc.dram_tensor("g_k", [NCORE * 128, LT], F32)
        self.b_vf = nc.dram_tensor("b_vf", [LT, 384], F32)
        self.g_vf = nc.dram_tensor("g_vf", [NCORE * LT, 384], F32)
        self.b_xe = nc.dram_tensor("b_xe", [128, 8 * NH], BF16)
        self.g_xe = nc.dram_tensor("g_xe", [NCORE * 128, 8 * NH], BF16)
        self.udram = nc.dram_tensor("udram", [128, 8, NT], BF16)
        self.b_mod = nc.dram_tensor("b_mod", [128, 72], F32)
        self.g_mod = nc.dram_tensor("g_mod", [NCORE * 128, 72], F32)
        self.xT = ar.f32(8 * NT).rearrange("p (k t) -> p k t", k=8)
        xin = self.din("xT_in", [128, 8, NT])
        self.mod_phase()
        for bi, (t0, n, ty) in enumerate(BLK):
            kb.dma('sp', self.xT[:, :, t0:t0 + n], xin[:, :, t0:t0 + n], writes=["x%d" % bi], sem="xin%d" % bi)
        self.part_a(0)
        for l in range(DEPTH):
            self.part_b(l)
            if l < DEPTH - 1:
                self.part_a(l + 1)
            else:
                self.final_norm()


def _bf_consts():
    ones = np.ones((128, 128), np.float32)
    bd = np.zeros((128, 128), np.float32)
    bd[:64, :64] = 1
    bd[64:, 64:] = 1
    perm = np.zeros((128, 128), np.float32)
    for d in range(128):
        hd = d % 64
        half = (hd % 32) // 16
        partner = d + 16 if half == 0 else d - 16
        perm[partner, d] = 1.0
    return np.ascontiguousarray(np.stack([ones, bd, perm, np.eye(128, dtype=np.float32)], 1))


def _rope_tables(core):
    pos = np.arange(core * LT, (core + 1) * LT)
    row = (pos // 64).astype(np.float64)
    col = (pos % 64).astype(np.float64)
    inv = 10000.0 ** (-np.arange(0, 32, 2, dtype=np.float64) / 32)
    cos = np.zeros((128, LT), np.float32)
    sin = np.zeros((128, LT), np.float32)
    for d in range(128):
        hd = d % 64
        axis = hd // 32
        half = (hd % 32) // 16
        f = hd % 16
        ang = (row if axis == 0 else col) * inv[f]
        cos[d] = np.cos(ang)
        sin[d] = np.sin(ang) * (-1.0 if half == 0 else 1.0)
    return np.stack([cos, sin], 0)


def _fm(w, cols):
    return np.ascontiguousarray(w.reshape(8, 128, w.shape[1])[:, :, cols].transpose(1, 0, 2))


def _layer_tab(l, modx, modc, I):
    tab = np.zeros((128, NTAB), np.float32)
    tab[:, TO['modx']:TO['modx'] + 72] = modx
    tab[:, TO['modc']:TO['modc'] + 72] = modc
    tab[:, TO['normg']:TO['normg'] + 24] = I['norm_g'][l].reshape(3, 8, 128).transpose(2, 0, 1).reshape(128, 24)
    tab[:, TO['bg']:TO['bg'] + 32] = I['b_gate'][l].reshape(4, 8, 128).transpose(2, 0, 1).reshape(128, 32)
    tab[:, TO['cw']:TO['cw'] + 62] = I['conv_dw_w'][l].reshape(31, 2, 128).transpose(2, 1, 0).reshape(128, 62)
    tab[:, TO['cb']:TO['cb'] + 2] = I['conv_dw_b'][l].reshape(2, 128).T
    tab[:, TO['lg']:TO['lg'] + 2] = I['conv_ln_g'][l].reshape(2, 128).T
    tab[:, TO['lb']:TO['lb'] + 2] = I['conv_ln_b'][l].reshape(2, 128).T
    tab[:, TO['scw']:TO['scw'] + 6] = I['sc_conv_w'][l].reshape(3, 2, 128).transpose(2, 1, 0).reshape(128, 6)
    tab[:, TO['qg']] = np.tile(I['q_norm_g'][l], 2)
    tab[:, TO['kg']] = np.tile(I['k_norm_g'][l], 2)
    return tab


def _w13p(w13):
    a = w13[:, :DFF].reshape(8, 128, NJ, 128)
    b = w13[:, DFF:].reshape(8, 128, NJ, 128)
    ab = np.concatenate([a, b], axis=3)
    return np.ascontiguousarray(ab.transpose(2, 1, 0, 3))


def _a_inputs(l, I):
    w_in = I['w_in'][l]
    return dict(
        a_w13=_w13p(I['ffn_w13'][l, 0]),
        a_w2=np.ascontiguousarray(I['ffn_w2'][l, 0].reshape(NJ, 128, 8, 128)),
        a_wk=_fm(w_in, np.arange(512, 640)),
        a_wvf=_fm(w_in, np.arange(640, 1024)),
    )


_CACHE = {}


def _get_prog(kind, dbg=()):
    key = (kind, tuple(dbg))
    if key not in _CACHE:
        _CACHE[key] = Prog(kind, dbg)
    return _CACHE[key]


def _xT_from(x_core, ctx):
    a = np.concatenate([x_core, ctx], axis=0)
    return np.ascontiguousarray(a.T.reshape(8, 128, NT).transpose(1, 0, 2))


def _b_inputs(l, I):
    w_in = I['w_in'][l]
    pair = lambda c: np.concatenate([np.arange(c * 64, c * 64 + 64), np.arange((4 + c) * 64, (4 + c) * 64 + 64)])
    chunks = [pair(c) for c in range(4)] + [np.arange(512, 640)]
    chunks += [np.arange(1024 + i * 128, 1024 + (i + 1) * 128) for i in range(4)]
    chunks += [np.arange(1536 + i * 128, 1536 + (i + 1) * 128) for i in range(6)]
    wfm = np.ascontiguousarray(np.stack([_fm(w_in, c) for c in chunks], axis=1))
    g = w_in[:, 2304:].reshape(8, 128, 4, 8, 128)
    wg = np.ascontiguousarray(g.transpose(3, 1, 2, 0, 4))
    rows = np.stack([pair(c) for c in range(4)])
    a = I['w_attn_out'][l][rows]
    allw = np.concatenate([a, I['w_fnet'][l].reshape(2, 128, 1024), I['w_conv_out'][l].reshape(2, 128, 1024),
                           I['w_sc_out'][l].reshape(2, 128, 1024)], 0)
    wbr = np.ascontiguousarray(allw.reshape(10, 128, 8, 128).transpose(2, 1, 0, 3))
    wo = np.ascontiguousarray(I['w_o'][l].reshape(8, 128, 8, 128).transpose(2, 1, 0, 3))
    return dict(b_wfm=wfm, b_wvf=_fm(w_in, np.arange(640, 1024)), b_wg=wg, b_wbr=wbr, b_wo=wo,
                b_w13=_w13p(I['ffn_w13'][l, 1]),
                b_w2=np.ascontiguousarray(I['ffn_w2'][l, 1].reshape(NJ, 128, 8, 128)))


def _fnet_tables(core):
    n = np.arange(128, dtype=np.float64)
    a1 = 2 * np.pi * np.outer(n, n) / 128
    cs128 = np.concatenate([np.cos(a1), -np.sin(a1)], 1).astype(np.float32)
    k1 = np.arange(128)[:, None]
    k2 = np.arange(16)[None, :] + 16 * core
    kk = (k1 + 128 * k2).reshape(-1).astype(np.float64)
    a2 = 2 * np.pi * np.outer(n, kk) / L
    s = 1.0 / 128.0
    mc = (np.cos(a2) * s).reshape(128, 128, 16)
    ms = (np.sin(a2) * s).reshape(128, 128, 16)
    mtab = np.zeros((128, 128, 2, 32), np.float32)
    mtab[:, :, 0, 0:16] = mc
    mtab[:, :, 0, 16:32] = -ms
    mtab[:, :, 1, 0:16] = ms
    mtab[:, :, 1, 16:32] = mc
    nn = (np.arange(2)[None, :] * 128 + np.arange(128)[:, None]).astype(np.float64)
    a3 = 2 * np.pi * nn[:, :, None] * np.arange(256)[None, None, :] / 256
    c256 = (np.stack([np.cos(a3), -np.sin(a3)], 2) / 16.0).astype(np.float32)
    j = np.arange(64, dtype=np.float64)
    a4 = 2 * np.pi * np.outer(j, j) / 64
    cbsb = (np.stack([np.cos(a4), np.sin(a4)], 1) / 8.0).astype(np.float32)
    return dict(cs128=cs128, mtab=np.ascontiguousarray(mtab), c256=np.ascontiguousarray(c256),
                cbsb=np.ascontiguousarray(cbsb))


def make_in_maps(I):
    I = {k: np.asarray(v) for k, v in I.items()}
    cbf = _bf_consts()
    x = I['x'][0]
    ctx = I['ctx'][0]
    cT = np.ascontiguousarray(np.stack([I['c'].reshape(8, 128).T, I['c_ctx'].reshape(8, 128).T], axis=2).astype(np.float32))
    zero = np.zeros((128, 72), np.float32)
    shared = dict(cbf=cbf, cT=cT, fng=np.ascontiguousarray(I['final_norm_g'].reshape(8, 128).T))
    for l in range(DEPTH):
        shared["tab%d" % l] = _layer_tab(l, zero, zero, I)
        a = _a_inputs(l, I)
        bw = _b_inputs(l, I)
        shared["w13_%d_0" % l] = a['a_w13']
        shared["w2_%d_0" % l] = a['a_w2']
        shared["wk_%d" % l] = a['a_wk']
        shared["wvf_%d" % l] = a['a_wvf']
        shared["w13_%d_1" % l] = bw['b_w13']
        shared["w2_%d_1" % l] = bw['b_w2']
        shared["wfm_%d" % l] = bw['b_wfm']
        shared["wg_%d" % l] = bw['b_wg']
        shared["wbr_%d" % l] = bw['b_wbr']
        shared["wo_%d" % l] = bw['b_wo']
    in_maps = []
    for i in range(NCORE):
        m = dict(shared)
        ft = _fnet_tables(i)
        m.update(ft)
        m["rope"] = _rope_tables(i)
        m["xT_in"] = _xT_from(x[i * LT:(i + 1) * LT], ctx)
        cols = np.arange(i * 1152, (i + 1) * 1152)
        w = I['w_ada'][:, :, cols].reshape(4, 8, 128, 9, 128).transpose(0, 3, 2, 1, 4).reshape(36, 128, 8, 128)
        m["wada"] = np.ascontiguousarray(w)
        m["bada"] = np.ascontiguousarray(I['b_ada'][:, cols].reshape(4, 9, 128).transpose(2, 0, 1).reshape(128, 36))
        hm = np.zeros((128, NH), np.float32)
        hs = np.zeros((128, 16), np.float32)
        if i > 0:
            hm[:, 0:16] = 1.0
            hs[:, i - 1] = 1.0
        if i < NCORE - 1:
            hm[:, 16:32] = 1.0
            hs[:, 8 + i + 1] = 1.0
        m["hmask"] = hm
        m["hsel"] = hs
        in_maps.append(m)
    return in_maps


def kernel(**I):
    prog = _get_prog('F')
    in_maps = make_in_maps(I)
    res = run_bass_kernel_spmd(prog.nc, in_maps, core_ids=list(range(NCORE)))
    outs = res.results
    y = np.concatenate([o["yT_out"].transpose(2, 1, 0).reshape(LT, D) for o in outs], axis=0)
    return np.ascontiguousarray(y[None].astype(np.float32))
```

```python
import numpy as np
from contextlib import ExitStack
import concourse.bass as bass
import concourse.mybir as mybir
from concourse.bass_utils import run_bass_kernel_spmd

F32 = mybir.dt.float32
BF16 = mybir.dt.bfloat16
AF = mybir.ActivationFunctionType
ALU = mybir.AluOpType

NCORE = 8
D = 1024
L = 16384
LT = 2048
CT = 256
NT = LT + CT
NH = 32
DEPTH = 4
DFF = 2816
NJ = 22
EPS = 1e-6
BLK = [(0, 512, 'x'), (512, 512, 'x'), (1024, 512, 'x'), (1536, 512, 'x'), (2048, 256, 'c')]
LBLK = BLK[:4]
HBLK = (NT, NH, 'x')
JPARTS = [(0, 6), (6, 6), (12, 5), (17, 5)]

TO = {}
_o = 0
for _n, _w in [('modx', 72), ('modc', 72), ('normg', 24), ('bg', 32), ('cw', 62), ('cb', 2), ('lg', 2),
               ('lb', 2), ('scw', 6), ('qg', 1), ('kg', 1)]:
    TO[_n] = _o
    _o += _w
NTAB = _o


class KB:
    def __init__(self, nc, es):
        self.nc = nc
        self.es = es
        self.E = dict(pe=nc.tensor, act=nc.scalar, dve=nc.vector, pool=nc.gpsimd, sp=nc.sync)
        self.sem = {e: es.enter_context(nc.semaphore("s_" + e)) for e in self.E}
        self.cnt = {e: 0 for e in self.E}
        self.seen = {e: {} for e in self.E}
        self.prog = {e: [] for e in self.E}
        self.W = {}
        self.R = {}
        self.pend = {e: [[], []] for e in self.E}
        self.dsems = {}

    def _semh(self, sk):
        if isinstance(sk, tuple):
            return self.dsems[sk[1]][0]
        return self.sem[sk]

    def _need(self, e, reads, writes, skip=None):
        toks = []
        for k in reads:
            t = self.W.get(k)
            if t:
                toks.append(t)
        for k in writes:
            t = self.W.get(k)
            if t:
                toks.append(t)
            toks.extend(self.R.get(k, {}).items())
        d = {}
        for sk, v in toks:
            if sk == skip:
                continue
            if sk == e and e == 'pe':
                continue
            if self.seen[e].get(sk, 0) >= v:
                continue
            d[sk] = max(d.get(sk, 0), v)
        for sk, v in d.items():
            self.seen[e][sk] = v
        return list(d.items())

    def _reg(self, tok, reads, writes):
        sk, v = tok
        for k in reads:
            r = self.R.setdefault(k, {})
            r[sk] = max(r.get(sk, 0), v)
        for k in writes:
            self.W[k] = tok
            self.R[k] = {}

    def op(self, e, fn, reads=(), writes=(), inc=True):
        for e2 in self.E:
            if e2 != e:
                for k in list(reads) + list(writes):
                    assert k not in self.pend[e2][1], ("pending write conflict", k)
                for k in writes:
                    assert k not in self.pend[e2][0], ("pending read conflict", k)
        waits = self._need(e, reads, writes)
        if inc:
            self.cnt[e] += 1
            tok = (e, self.cnt[e])
            pr, pw = self.pend[e]
            self._reg(tok, list(reads) + pr, list(writes) + pw)
            self.pend[e] = [[], []]
        else:
            self.pend[e][0] += list(reads)
            self.pend[e][1] += list(writes)
        self.prog[e].append((waits, fn, inc, None))

    def dma(self, q, out, in_, reads=(), writes=(), sem=None, group=False):
        if sem not in self.dsems:
            self.dsems[sem] = [self.es.enter_context(self.nc.semaphore("d_" + sem)), 0]
        skip = ('d', sem) if group else None
        waits = self._need(q, reads, writes, skip=skip)
        self.dsems[sem][1] += 16
        tok = (('d', sem), self.dsems[sem][1])
        self._reg(tok, reads, writes)
        self.prog[q].append((waits, lambda e, o=out, i=in_: e.dma_start(out=o, in_=i), 'dma', sem))

    def cc(self, in_t, out_t, reads=(), writes=()):
        sem = "cc"
        if sem not in self.dsems:
            self.dsems[sem] = [self.es.enter_context(self.nc.semaphore("d_" + sem)), 0]
        waits = self._need('pool', reads, writes)
        self.dsems[sem][1] += 1
        tok = (('d', sem), self.dsems[sem][1])
        self._reg(tok, reads, writes)
        self.prog['pool'].append((waits, lambda e, i=in_t, o=out_t: e.collective_compute(
            "AllGather", ALU.bypass, replica_groups=[list(range(NCORE))], ins=[i.ap().opt()], outs=[o.ap().opt()]),
            'cc', sem))

    def barrier(self, full=True):
        for e in self.E:
            waits = []
            for e2 in self.E:
                if e2 != e and self.cnt[e2] > self.seen[e].get(e2, 0):
                    waits.append((e2, self.cnt[e2]))
                    self.seen[e][e2] = self.cnt[e2]
            for name, (s, c) in self.dsems.items():
                sk = ('d', name)
                if name == "cc" and not full:
                    continue
                if c > self.seen[e].get(sk, 0):
                    waits.append((sk, c))
                    self.seen[e][sk] = c
            if waits:
                self.prog[e].append((waits, None, False, None))

    def emit(self):
        self.barrier(full=True)
        with self.nc.Block() as block:
            for e, dec in (('sp', block.sync), ('act', block.scalar), ('dve', block.vector),
                           ('pool', block.gpsimd), ('pe', block.tensor)):
                def body(eng, e=e):
                    for waits, fn, inc, sem in self.prog[e]:
                        for sk, v in waits:
                            eng.wait_ge(self._semh(sk), v)
                        if fn is None:
                            continue
                        ins = fn(eng)
                        if inc is True:
                            ins.then_inc(self.sem[e], 1)
                        elif inc == 'dma':
                            ins.then_inc(self.dsems[sem][0], 16)
                        elif inc == 'cc':
                            ins.then_inc(self.dsems[sem][0], 1)
                dec(body)


class Arena:
    def __init__(self, nc, es, words):
        self.t = es.enter_context(nc.sbuf_tensor("arena", [128, words], F32))
        self.words = words
        self.top = 0

    def mark(self):
        return self.top

    def release(self, m):
        self.top = m

    def f32(self, n):
        o = self.top
        self.top += n
        assert self.top <= self.words, ("arena overflow", self.top)
        return self.t[:, o:o + n]

    def bf(self, n):
        w = (n + 1) // 2
        o = self.top
        self.top += w
        assert self.top <= self.words, ("arena overflow", self.top)
        return self.t[:, o:o + w].bitcast(BF16)[:, 0:n]


class Prog:
    def __init__(self, kind, dbg=()):
        self.kind = kind
        self.dbg = dbg
        self.es = ExitStack()
        nc = self.nc = bass.Bass("TRN2", target_bir_lowering=False)
        self.kb = KB(nc, self.es)
        self.ar = Arena(nc, self.es, 51712)
        self.pst = self.es.enter_context(nc.psum_tensor("ps", [128, 4096], F32))
        self.uid = 0
        self.outs = []
        self.dins = {}
        self.build()
        self.kb.emit()
        self.es.close()

    def ps(self, b, n=512, p=128):
        return self.pst[0:p, b * 512:b * 512 + n]

    def din(self, name, shape, dt=F32):
        if name not in self.dins:
            self.dins[name] = self.nc.dram_tensor(name, list(shape), dt, kind="ExternalInput").ap()
        return self.dins[name]

    def dout(self, name, shape, dt=F32):
        self.outs.append(name)
        return self.nc.dram_tensor(name, list(shape), dt, kind="ExternalOutput").ap()

    def key(self, s):
        self.uid += 1
        return "%s#%d" % (s, self.uid)

    def mm(self, out, lhsT, rhs, start, stop, reads, writes, inc):
        self.kb.op('pe', lambda e: e.matmul(out=out, lhsT=lhsT, rhs=rhs, start=start, stop=stop),
                   reads, writes, inc)

    def act(self, out, in_, func, reads, writes, bias=None, scale=None):
        kw = {}
        if bias is not None:
            kw['bias'] = bias
        if scale is not None:
            kw['scale'] = scale
        self.kb.op('act', lambda e: e.activation(out=out, in_=in_, func=func, **kw), reads, writes)

    def tt(self, out, in0, in1, op, reads, writes, eng='dve'):
        self.kb.op(eng, lambda e: e.tensor_tensor(out=out, in0=in0, in1=in1, op=op), reads, writes)

    def ts(self, out, in0, s1, s2, op0, op1, reads, writes, eng='dve'):
        if s2 is None:
            self.kb.op(eng, lambda e: e.tensor_scalar(out=out, in0=in0, scalar1=s1, scalar2=None, op0=op0),
                       reads, writes)
        else:
            self.kb.op(eng, lambda e: e.tensor_scalar(out=out, in0=in0, scalar1=s1, scalar2=s2, op0=op0, op1=op1),
                       reads, writes)

    def stt(self, out, in0, scalar, in1, op0, op1, reads, writes):
        self.kb.op('dve', lambda e: e.scalar_tensor_tensor(out=out, in0=in0, scalar=scalar, in1=in1, op0=op0, op1=op1),
                   reads, writes)

    def cp(self, out, in_, reads, writes, eng='dve'):
        self.kb.op(eng, lambda e: e.tensor_copy(out=out, in_=in_), reads, writes)

    def rsqrt(self, out, in_, scale, reads, writes):
        self.act(out, in_, AF.Ln, reads, writes, bias=self.c_eps, scale=scale)
        self.act(out, out, AF.Exp, writes, writes, scale=-0.5)

    def recip(self, out, in_, reads, writes):
        self.kb.op('dve', lambda e: e.reciprocal(out=out, in_=in_), reads, writes)

    def memset(self, ap, val, writes, eng='dve'):
        self.kb.op(eng, lambda e: e.memset(ap, val), (), writes)

    def debug(self, name, ap, reads, shape, dt=F32):
        if name in self.dbg:
            d = self.dout("dbg_" + name, shape, dt)
            self.kb.dma('sp', d, ap, reads=reads, writes=(), sem=self.key("dbg"))

    def load_tab(self, l):
        name = "tab%d" % l
        tab_d = self.din(name, [128, NTAB])
        tab = self.ar.f32(NTAB)
        self.kb.dma('sp', tab[:, 144:NTAB], tab_d[:, 144:NTAB], writes=[name], sem="tab", group=True)
        gm = self.g_mod.ap().rearrange("(r p) (w l c) -> p w l r c", p=128, w=2, l=DEPTH)
        for w in range(2):
            self.kb.dma('sp', tab[:, 72 * w:72 * w + 72].rearrange("p (r c) -> p r c", r=8), gm[:, w, l],
                        reads=["g_mod"], writes=[name], sem="tab", group=True)
        der = self.ar.f32(96)
        derv = der.rearrange("p (t s a o) -> p t s a o", t=2, s=3, a=2)
        for ti, mn in enumerate(('modx', 'modc')):
            for s in range(3):
                mo = TO[mn]
                sc = tab[:, mo + (3 * s + 1) * 8: mo + (3 * s + 1) * 8 + 8]
                gt = tab[:, mo + (3 * s + 2) * 8: mo + (3 * s + 2) * 8 + 8]
                ng = tab[:, TO['normg'] + s * 8: TO['normg'] + s * 8 + 8]
                self.stt(derv[:, ti, s, 0, :], sc, 1.0, ng, ALU.add, ALU.mult, [name], [name + "_der"])
                self.ts(derv[:, ti, s, 1, :], gt, 0.5 if s != 1 else 1.0, None, ALU.mult, None, [name], [name + "_der"])
        return dict(tab=tab, der=derv, name=name)

    def tA(self, T, ty, s, o):
        return T['der'][:, 0 if ty == 'x' else 1, s, 0, o:o + 1]

    def tG(self, T, ty, s, o):
        return T['der'][:, 0 if ty == 'x' else 1, s, 1, o:o + 1]

    def tB(self, T, ty, s, o):
        mo = TO['modx' if ty == 'x' else 'modc']
        return T['tab'][:, mo + 3 * s * 8 + o: mo + 3 * s * 8 + o + 1]

    def tcol(self, T, n, i=0):
        return T['tab'][:, TO[n] + i: TO[n] + i + 1]

    def modnorm_s1(self, xsrc, xkeys, n, tmp, par):
        sq, rstd = tmp['sq'], tmp['rstd2'][par]
        pb = 7 - par
        rk = "rstd%d" % par
        for k in range(8):
            i = k % 2
            self.act(sq[i][:, 0:n], xsrc(k), AF.Square, xkeys, ["sq%d" % i])
            self.mm(self.ps(pb, n), self.c_ones, sq[i][:, 0:n], k == 0, k == 7, ["sq%d" % i, "const"], ["ps%d" % pb], True)
        self.rsqrt(rstd[:, 0:n], self.ps(pb, n), 1.0 / D, ["ps%d" % pb], [rk])

    def modnorm_s2(self, T, s, ty, xsrc, xkeys, dst, dkeys, n, tmp, par):
        rstd, t32 = tmp['rstd2'][par], tmp['t32']
        rk = "rstd%d" % par
        for k in range(8):
            i = k % 2
            self.stt(t32[i][:, 0:n], xsrc(k), self.tA(T, ty, s, k), rstd[:, 0:n], ALU.mult, ALU.mult,
                     list(xkeys) + [rk, T['name'] + "_der"], ["t32_%d" % i])
            self.act(dst(k), t32[i][:, 0:n], AF.Identity, ["t32_%d" % i, T['name']], dkeys, bias=self.tB(T, ty, s, k))

    def modnorm_all(self, T, s, blocks, xsrc_f, xkeys_f, dst_f, dkeys_f, tmp):
        nb = len(blocks)
        self.modnorm_s1(xsrc_f(0), xkeys_f(0), blocks[0][1], tmp, 0)
        for bi, (t0, n, ty) in enumerate(blocks):
            if bi + 1 < nb:
                self.modnorm_s1(xsrc_f(bi + 1), xkeys_f(bi + 1), blocks[bi + 1][1], tmp, (bi + 1) % 2)
            self.modnorm_s2(T, s, ty, xsrc_f(bi), xkeys_f(bi), dst_f(bi), dkeys_f(bi), n, tmp, bi % 2)

    def norm_tmps(self):
        r0 = self.ar.f32(512)
        return dict(sq=[self.ar.bf(512), self.ar.bf(512)], rstd=r0, rstd2=[r0, self.ar.f32(512)],
                    t32=[self.ar.f32(512), self.ar.f32(512)])

    def ffn(self, T, s, w13_d, w2_d, tag, blocks=None):
        kb = self.kb
        ar = self.ar
        blocks = BLK if blocks is None else blocks
        m = ar.mark()
        h = ar.bf(8 * NT).rearrange("p (k t) -> p k t", k=8)
        g = ar.bf(6 * NT).rearrange("p (j t) -> p j t", j=6)
        w13b = [ar.bf(8 * 256).rearrange("p (k c) -> p k c", k=8) for _ in range(2)]
        w2b = [ar.bf(6 * 128).rearrange("p (j c) -> p j c", j=6) for _ in range(2)]
        sa = [ar.f32(512), ar.f32(512)]
        tmp = self.norm_tmps()
        xT = self.xT
        self.modnorm_all(T, s, blocks,
                         lambda bi: (lambda k, bi=bi: xT[:, k, BLK[bi][0]:BLK[bi][0] + BLK[bi][1]]),
                         lambda bi: ["x%d" % bi],
                         lambda bi: (lambda k, bi=bi: h[:, k, BLK[bi][0]:BLK[bi][0] + BLK[bi][1]]),
                         lambda bi: ["h%d" % bi], tmp)
        it = 0
        w2it = 0
        for (j0, nj) in JPARTS:
            for jj in range(nj):
                j = j0 + jj
                sl = it % 2
                kb.dma('pool', w13b[sl], w13_d[j], writes=["w13b%d" % sl], sem="w13b%d" % sl)
                for bi, (t0, n, ty) in enumerate(blocks):
                    pa, pb_ = (0, 1) if (it * 5 + bi) % 2 == 0 else (2, 3)
                    for k in range(8):
                        self.mm(self.ps(pa, n), w13b[sl][:, k, 0:128], h[:, k, t0:t0 + n], k == 0, k == 7,
                                ["w13b%d" % sl, "h%d" % bi], ["ps%d" % pa], k == 7)
                    for k in range(8):
                        self.mm(self.ps(pb_, n), w13b[sl][:, k, 128:256], h[:, k, t0:t0 + n], k == 0, k == 7,
                                ["w13b%d" % sl, "h%d" % bi], ["ps%d" % pb_], k == 7)
                    si = (it * 5 + bi) % 2
                    self.act(sa[si][:, 0:n], self.ps(pa, n), AF.Silu, ["ps%d" % pa], ["sa%d" % si])
                    self.tt(g[:, jj, t0:t0 + n], sa[si][:, 0:n], self.ps(pb_, n), ALU.mult,
                            ["sa%d" % si, "ps%d" % pb_], ["g%d_%d" % (jj, bi)])
                it += 1
            for o in range(8):
                sl = w2it % 2
                kb.dma('pool', w2b[sl][:, 0:nj, :], w2_d[j0:j0 + nj, :, o, :].rearrange("j p c -> p j c"),
                       writes=["w2b%d" % sl], sem="w2b%d" % sl)
                for bi, (t0, n, ty) in enumerate(blocks):
                    py = 4 + (w2it * 5 + bi) % 2
                    for jj in range(nj):
                        self.mm(self.ps(py, n), w2b[sl][:, jj, :], g[:, jj, t0:t0 + n], jj == 0, jj == nj - 1,
                                ["w2b%d" % sl, "g%d_%d" % (jj, bi)], ["ps%d" % py], jj == nj - 1)
                    self.stt(xT[:, o, t0:t0 + n], self.ps(py, n), self.tG(T, ty, s, o), xT[:, o, t0:t0 + n],
                             ALU.mult, ALU.add, ["ps%d" % py, "x%d" % bi, T['name'] + "_der"], ["x%d" % bi])
                w2it += 1
        kb.barrier()
        ar.release(m)

    def qknorm(self, pb, n, gcol, dst, dkeys, rope_t0, tmp, T):
        qf, sq, rstd, qn, qb, t1 = tmp['qf'], tmp['sq'], tmp['rstd'], tmp['qn'], tmp['qb'], tmp['t1']
        x_ = tmp['sfx']
        p2 = tmp['pb2']
        p2k = "ps%d" % p2
        pk = "ps%d" % pb
        self.act(qf[:, 0:n], self.ps(pb, n), AF.Copy, [pk], ["qf" + x_])
        self.act(sq[:, 0:n], self.ps(pb, n), AF.Square, [pk], ["qsq" + x_])
        self.mm(self.ps(p2, n), self.c_bd, sq[:, 0:n], True, True, ["qsq" + x_, "const"], [p2k], True)
        self.rsqrt(rstd[:, 0:n], self.ps(p2, n), 1.0 / 64, [p2k], ["qrstd" + x_])
        if rope_t0 is None:
            self.stt(dst, qf[:, 0:n], gcol, rstd[:, 0:n], ALU.mult, ALU.mult, ["qf" + x_, "qrstd" + x_, T['name']], dkeys)
            return
        self.stt(qn[:, 0:n], qf[:, 0:n], gcol, rstd[:, 0:n], ALU.mult, ALU.mult, ["qf" + x_, "qrstd" + x_, T['name']], ["qn" + x_])
        self.cp(qb[:, 0:n], qn[:, 0:n], ["qn" + x_], ["qb" + x_])
        self.mm(self.ps(p2, n), self.c_perm, qb[:, 0:n], True, True, ["qb" + x_, "const"], [p2k], True)
        self.tt(t1[:, 0:n], qn[:, 0:n], self.cosT[:, rope_t0:rope_t0 + n], ALU.mult, ["qn" + x_, "rope"], ["qt1" + x_])
        self.tt(qn[:, 0:n], self.ps(p2, n), self.sinT[:, rope_t0:rope_t0 + n], ALU.mult, [p2k, "rope"], ["qn" + x_])
        self.tt(dst, t1[:, 0:n], qn[:, 0:n], ALU.add, ["qt1" + x_, "qn" + x_], dkeys)

    def qk_tmps(self, sfx="", pb2=6):
        ar = self.ar
        return dict(qf=ar.f32(512), sq=ar.bf(512), rstd=ar.f32(512), qn=ar.f32(512), qb=ar.bf(512), t1=ar.f32(512),
                    sfx=sfx, pb2=pb2)

    def load_rope(self):
        ar = self.ar
        self.cosT = ar.f32(LT)
        self.sinT = ar.f32(LT)
        self.kb.dma('sp', self.cosT, self.rope_d[0], writes=["rope"], sem="rope", group=True)
        self.kb.dma('sp', self.sinT, self.rope_d[1], writes=["rope"], sem="rope", group=True)

    def load_consts(self):
        ar = self.ar
        kb = self.kb
        cb_d = self.din("cbf", [128, 4, 128])
        cb = ar.bf(4 * 128).rearrange("p (a c) -> p a c", a=4)
        self.c_ident = cb[:, 3, :]
        kb.dma('pool', cb, cb_d, writes=["const"], sem="const", group=True)
        self.c_ones, self.c_bd, self.c_perm = cb[:, 0, :], cb[:, 1, :], cb[:, 2, :]
        self.c_eps = ar.f32(1)
        self.memset(self.c_eps, EPS, ["const"])
        self.c_onesf = ar.f32(128)
        self.memset(self.c_onesf, 1.0 / 256.0, ["const"])

    def part_a(self, l):
        kb, ar = self.kb, self.ar
        m0 = ar.mark()
        T = self.load_tab(l)
        w13_d = self.din("w13_%d_0" % l, [NJ, 128, 8, 256])
        w2_d = self.din("w2_%d_0" % l, [NJ, 128, 8, 128])
        wk_d = self.din("wk_%d" % l, [128, 8, 128])
        wvf_d = self.din("wvf_%d" % l, [128, 8, 384])
        self.ffn(T, 0, w13_d, w2_d, "f1")
        xk = ["x%d" % i for i in range(5)]
        kb.dma('sp', self.xdram.ap(), self.xT, reads=xk, writes=["xdram"], sem="xout")
        xe = self.b_xe.ap().rearrange("p (k t) -> p k t", k=8)
        m = ar.mark()
        u = ar.bf(8 * NT).rearrange("p (k t) -> p k t", k=8)
        wk = ar.bf(8 * 128).rearrange("p (k c) -> p k c", k=8)
        wvf = ar.bf(8 * 384).rearrange("p (k c) -> p k c", k=8)
        kb.dma('pool', wk, wk_d, writes=["wk"], sem="wk")
        kb.dma('pool', wvf, wvf_d, writes=["wvf"], sem="wvf")
        self.load_rope()
        tmp = self.norm_tmps()
        qt = self.qk_tmps("a", 6)
        qt2 = self.qk_tmps("b", 5)
        ko = [ar.bf(512), ar.bf(512)]
        vfo = [ar.bf(384), ar.bf(384)]
        xT = self.xT
        kT_d = self.b_k.ap()
        vf_d = self.b_vf.ap()
        self.modnorm_all(T, 1, BLK,
                         lambda bi: (lambda k, bi=bi: xT[:, k, BLK[bi][0]:BLK[bi][0] + BLK[bi][1]]),
                         lambda bi: ["x%d" % bi],
                         lambda bi: (lambda k, bi=bi: u[:, k, BLK[bi][0]:BLK[bi][0] + BLK[bi][1]]),
                         lambda bi: ["u%d" % bi], tmp)
        kb.dma('sp', xe[:, :, 0:16], u[:, :, 0:16], reads=["u0"], writes=["b_xe"], sem="xe", group=True)
        kb.dma('sp', xe[:, :, 16:32], u[:, :, LT - 16:LT], reads=["u3"], writes=["b_xe"], sem="xe", group=True)
        kb.dma('sp', self.udram.ap(), u, reads=["u%d" % i for i in range(5)], writes=["udram"], sem="uout")
        for bi, (t0, n, ty) in enumerate(LBLK):
            pb = bi % 2
            for k in range(8):
                self.mm(self.ps(pb, n), wk[:, k, :], u[:, k, t0:t0 + n], k == 0, k == 7, ["wk", "u%d" % bi],
                        ["ps%d" % pb], k == 7)
            self.qknorm(pb, n, self.tcol(T, 'kg'), ko[bi % 2][:, 0:n], ["ko%d" % (bi % 2)], t0, qt if bi % 2 == 0 else qt2, T)
            kb.dma('sp', kT_d[:, t0:t0 + n], ko[bi % 2][:, 0:n], reads=["ko%d" % (bi % 2)], writes=["b_k%d" % (bi % 2)],
                   sem="kout%d" % (bi % 2))
        for tt_ in range(LT // 128):
            pb = 2 + tt_ % 2
            bi = tt_ // 4
            for k in range(8):
                self.mm(self.ps(pb, 384), u[:, k, tt_ * 128:(tt_ + 1) * 128], wvf[:, k, :], k == 0, k == 7,
                        ["wvf", "u%d" % bi], ["ps%d" % pb], k == 7)
            self.act(vfo[tt_ % 2], self.ps(pb, 384), AF.Copy, ["ps%d" % pb], ["vfo%d" % (tt_ % 2)])
            kb.dma('sp', vf_d[tt_ * 128:(tt_ + 1) * 128, :], vfo[tt_ % 2], reads=["vfo%d" % (tt_ % 2)],
                   writes=["b_vf%d" % (tt_ % 2)], sem="vfout%d" % (tt_ % 2))
        kb.cc(self.b_xe, self.g_xe, reads=["b_xe"], writes=["g_xe"])
        kb.cc(self.b_k, self.g_k, reads=["b_k0", "b_k1"], writes=["g_k"])
        kb.cc(self.b_vf, self.g_vf, reads=["b_vf0", "b_vf1"], writes=["g_vf"])
        kb.barrier()
        ar.release(m0)

    def mod_phase(self):
        kb, ar = self.kb, self.ar
        m = ar.mark()
        cT_d = self.din("cT", [128, 8, 2])
        w_d = self.din("wada", [36, 128, 8, 128])
        b_d = self.din("bada", [128, 36])
        cT = ar.f32(16).rearrange("p (k w) -> p k w", k=8)
        sc = ar.f32(16).rearrange("p (k w) -> p k w", k=8)
        bs = ar.f32(36)
        os_ = ar.f32(72).rearrange("p (w g) -> p w g", w=2)
        wb = [ar.f32(1024).rearrange("p (k n) -> p k n", k=8) for _ in range(3)]
        kb.dma('sp', cT, cT_d, writes=["cT"], sem="cT")
        kb.dma('sp', bs, b_d, writes=["bs"], sem="bs")
        self.act(sc, cT, AF.Silu, ["cT"], ["sc"])
        for g in range(36):
            sl = g % 3
            kb.dma('sp', wb[sl], w_d[g], writes=["wb%d" % sl], sem="wb%d" % sl)
            for k in range(8):
                self.mm(self.pst[:, 2 * g:2 * g + 2], wb[sl][:, k, :], sc[:, k, :], k == 0, k == 7,
                        ["wb%d" % sl, "sc"], ["ps0"], k == 7)
        psv = self.pst[:, 0:72].rearrange("p (g w) -> p g w", w=2)
        for w in range(2):
            self.tt(os_[:, w, :], psv[:, :, w], bs, ALU.add, ["ps0", "bs"], ["os"])
        kb.dma('sp', self.b_mod.ap(), os_.rearrange("p w g -> p (w g)"), reads=["os"], writes=["b_mod"], sem="modout")
        kb.cc(self.b_mod, self.g_mod, reads=["b_mod"], writes=["g_mod"])
        kb.barrier()
        ar.release(m)

    def part_b(self, l):
        kb, ar = self.kb, self.ar
        mB = ar.mark()
        T = self.load_tab(l)
        wfm_d = self.din("wfm_%d" % l, [128, 15, 8, 128])
        wvf_d = self.din("wvf_%d" % l, [128, 8, 384])
        wg_d = self.din("wg_%d" % l, [8, 128, 4, 8, 128])
        wbr_d = self.din("wbr_%d" % l, [8, 128, 10, 128])
        wo_d = self.din("wo_%d" % l, [8, 128, 8, 128])
        w13_d = self.din("w13_%d_1" % l, [NJ, 128, 8, 256])
        w2_d = self.din("w2_%d_1" % l, [NJ, 128, 8, 128])
        xin = self.xdram.ap()
        gk = self.g_k.ap().rearrange("(r p) t -> p r t", p=128)
        vf_d = self.g_vf.ap()
        hm_d, cs_d, mt_d, c256_d, cbsb_d = self.hm_d, self.cs_d, self.mt_d, self.c256_d, self.cbsb_d
        one1 = ar.f32(64)
        self.memset(one1, 1.0, ["const"])
        Xreg = self.xT
        Xflat = Xreg.rearrange("p k t -> p (k t)")
        xoff = [0]

        def xf32(n):
            o = xoff[0]
            xoff[0] += n
            assert xoff[0] <= 8 * NT, ("X region overflow", xoff[0])
            return Xflat[:, o:o + n]

        def xbf(n):
            w = (n + 1) // 2
            return xf32(w).bitcast(BF16)[:, 0:n]

        m0 = ar.mark()
        attnT = ar.bf(4 * NT).rearrange("p (k t) -> p k t", k=4)
        convo = ar.bf(2 * NT).rearrange("p (k t) -> p k t", k=2)
        sco = ar.bf(2 * NT).rearrange("p (k t) -> p k t", k=2)
        Y = ar.bf(2 * NT).rearrange("p (k t) -> p k t", k=2)
        KTc = ar.bf(CT)
        Vc = ar.bf(2 * 2 * 65).rearrange("p (a b c) -> p a b c", a=2, b=2)
        Fc = ar.bf(2 * 256).rearrange("p (a c) -> p a c", a=2)
        mP = ar.mark()
        qT = ar.bf(4 * NT).rearrange("p (k t) -> p k t", k=4)
        NU = NT + NH
        ABLK = BLK + [HBLK]
        MBLK = LBLK if l == DEPTH - 1 else BLK
        xoff[0] = 0
        u = xbf(8 * NU).rearrange("p (k t) -> p k t", k=8)
        m1 = ar.mark()
        ud = self.udram.ap()
        for bi, (t0, n, ty) in enumerate(BLK):
            kb.dma('sp', u[:, :, t0:t0 + n], ud[:, :, t0:t0 + n], reads=["udram"], writes=["u%d" % bi], sem="uin%d" % bi)
        E = ar.bf(8 * 256).rearrange("p (r k t) -> p r k t", r=8, k=8)
        kb.dma('sp', E.rearrange("p r k t -> p r (k t)"), self.g_xe.ap().rearrange("(r p) n -> p r n", p=128),
               reads=["g_xe"], writes=["E"], sem="E")
        hs = ar.f32(16)
        kb.dma('sp', hs, self.hsel_d, writes=["hs"], sem="hs")
        xhal = ar.f32(8 * NH).rearrange("p (k t) -> p k t", k=8)
        for side, (d0, s0) in enumerate(((0, 16), (16, 0))):
            dst = xhal[:, :, d0:d0 + 16]
            for r in range(8):
                sc_ = hs[:, side * 8 + r:side * 8 + r + 1]
                if r == 0:
                    self.ts(dst, E[:, r, :, s0:s0 + 16], sc_, None, ALU.mult, None, ["E", "hs"], ["xhal"])
                else:
                    self.stt(dst, E[:, r, :, s0:s0 + 16], sc_, dst, ALU.mult, ALU.add, ["E", "hs", "xhal"], ["xhal"])
        self.cp(u[:, :, NT:NT + NH], xhal, ["xhal"], ["u5"])
        kb.barrier()
        ar.release(m1)
        self.debug("u", u, ["u%d" % i for i in range(6)], [128, 8, NU], BF16)
        m1 = ar.mark()
        xm = xoff[0]
        KT = None
        wq = ar.bf(5 * 8 * 128).rearrange("p (c k n) -> p c k n", c=5, k=8)
        kb.dma('pool', wq, wfm_d[:, 0:5], writes=["wq"], sem="wq")
        wvf = ar.bf(8 * 384).rearrange("p (k c) -> p k c", k=8)
        kb.dma('pool', wvf, wvf_d, writes=["wvf"], sem="wvf")
        self.load_rope()
        qt = self.qk_tmps("a", 6)
        qt2 = self.qk_tmps("b", 7)
        self.memset(Vc[:, :, :, 64:65], 1.0, ["Vc1"])
        it = 0
        for c in range(4):
            for bi, (t0, n, ty) in enumerate(BLK):
                pb = it % 4
                it += 1
                for k in range(8):
                    self.mm(self.ps(pb, n), wq[:, c, k, :], u[:, k, t0:t0 + n], k == 0, k == 7, ["wq", "u%d" % bi],
                            ["ps%d" % pb], k == 7)
                self.qknorm(pb, n, self.tcol(T, 'qg'), qT[:, c, t0:t0 + n], ["qT%d_%d" % (c, bi)],
                            t0 if ty == 'x' else None, qt if it % 2 == 0 else qt2, T)
        t0, n, ty = BLK[4]
        for k in range(8):
            self.mm(self.ps(0, n), wq[:, 4, k, :], u[:, k, t0:t0 + n], k == 0, k == 7, ["wq", "u4"], ["ps0"], k == 7)
        self.qknorm(0, n, self.tcol(T, 'kg'), KTc[:, 0:n], ["KTc"], None, qt, T)
        for tt_ in range(2):
            pb = 1 + tt_
            for k in range(8):
                self.mm(self.ps(pb, 384), u[:, k, LT + tt_ * 128:LT + (tt_ + 1) * 128], wvf[:, k, :], k == 0, k == 7,
                        ["wvf", "u4"], ["ps%d" % pb], k == 7)
            self.act(Vc[:, tt_, :, 0:64], self.ps(pb, 128).rearrange("p (a d) -> p a d", a=2), AF.Copy,
                     ["ps%d" % pb], ["Vc"])
            self.act(Fc[:, tt_, :], self.pst[:, pb * 512 + 128:pb * 512 + 384], AF.Copy, ["ps%d" % pb], ["Fc"])
        kb.barrier()
        ar.release(m1)
        self.debug("qT", qT, [], [128, 4, NT], BF16)
        m1 = ar.mark()
        wc_raw = ar.f32(3 * 8 * 128)
        wcb = wc_raw.bitcast(BF16).rearrange("p (c k n) -> p c k n", c=6, k=8)
        HB = LT + 30
        PBW = LT + 2
        hb = xbf(2 * HB).rearrange("p (i t) -> p i t", i=2)
        pbf = xf32(2 * PBW).rearrange("p (i t) -> p i t", i=2)
        hbc = xbf(2 * (CT + 30)).rearrange("p (i t) -> p i t", i=2)
        dg = ar.bf(62 * 128).rearrange("p (a c) -> p a c", a=62)
        for a_ in range(62):
            self.ts(dg[:, a_, :], self.c_ident, self.tcol(T, 'cw', a_), None, ALU.mult, None, ["const", T['name']], ["dg"])
        pbc = xf32(2 * (CT + 2)).rearrange("p (i t) -> p i t", i=2)
        gb = ar.bf(2 * NT).rearrange("p (i t) -> p i t", i=2)
        acc = ar.f32(2 * NT).rearrange("p (i t) -> p i t", i=2)
        sg = [xf32(512), xf32(512)]
        hht = ar.f32(2 * NH).rearrange("p (i t) -> p i t", i=2)
        pht = ar.f32(2 * NH).rearrange("p (i t) -> p i t", i=2)
        hm = ar.f32(NH)
        kb.dma('sp', hm, hm_d, writes=["hm"], sem="hm")
        self.memset(hbc, 0.0, ["hbc"])
        self.memset(pbc, 0.0, ["pbc"])
        it = 0
        kb.dma('pool', wcb[:, 0:4], wfm_d[:, 5:9], writes=["wc"], sem="wc")
        for bi, (t0, n, ty) in enumerate(ABLK):
            uk = "u%d" % bi
            for i in range(2):
                for k in range(8):
                    self.mm(self.ps(0, n), wcb[:, i, k, :], u[:, k, t0:t0 + n], k == 0, k == 7, ["wc", uk], ["ps0"], k == 7)
                for k in range(8):
                    self.mm(self.ps(1, n), wcb[:, 2 + i, k, :], u[:, k, t0:t0 + n], k == 0, k == 7, ["wc", uk], ["ps1"], k == 7)
                si = it % 2
                it += 1
                self.act(sg[si][:, 0:n], self.ps(1, n), AF.Sigmoid, ["ps1"], ["sg%d" % si])
                if bi < 4:
                    dst = hb[:, i, 15 + t0:15 + t0 + n]
                    dk = "hb"
                elif bi == 4:
                    dst = hbc[:, i, 15:15 + CT]
                    dk = "hbc"
                else:
                    dst = hht[:, i, :]
                    dk = "hht"
                self.tt(dst, sg[si][:, 0:n], self.ps(0, n), ALU.mult, ["sg%d" % si, "ps0"], [dk])
        kb.dma('pool', wcb[:, 0:6], wfm_d[:, 9:15], writes=["wc"], sem="wc")
        for bi, (t0, n, ty) in enumerate(ABLK):
            uk = "u%d" % bi
            for i in range(2):
                for k in range(8):
                    self.mm(self.ps(2, n), wcb[:, 2 + i, k, :], u[:, k, t0:t0 + n], k == 0, k == 7, ["wc", uk], ["ps2"], k == 7)
                for k in range(8):
                    self.mm(self.ps(3, n), wcb[:, 4 + i, k, :], u[:, k, t0:t0 + n], k == 0, k == 7, ["wc", uk], ["ps3"], k == 7)
                si = it % 2
                it += 1
                self.act(sg[si][:, 0:n], self.ps(2, n), AF.Copy, ["ps2"], ["sg%d" % si])
                if bi < 4:
                    dst = pbf[:, i, 1 + t0:1 + t0 + n]
                    dk = "pb"
                elif bi == 4:
                    dst = pbc[:, i, 1:1 + CT]
                    dk = "pbc"
                else:
                    dst = pht[:, i, :]
                    dk = "pht"
                self.tt(dst, sg[si][:, 0:n], self.ps(3, n), ALU.mult, ["sg%d" % si, "ps3"], [dk])
                if bi < 5:
                    for k in range(8):
                        self.mm(self.ps(5, n), wcb[:, i, k, :], u[:, k, t0:t0 + n], k == 0, k == 7, ["wc", uk], ["ps5"], k == 7)
                    self.act(gb[:, i, t0:t0 + n], self.ps(5, n), AF.Copy, ["ps5"], ["gb"])
        for i in range(2):
            self.tt(hht[:, i, :], hht[:, i, :], hm, ALU.mult, ["hht", "hm"], ["hht"])
            self.tt(pht[:, i, :], pht[:, i, :], hm, ALU.mult, ["pht", "hm"], ["pht"])
            self.cp(hb[:, i, 0:15], hht[:, i, 1:16], ["hht"], ["hb"])
            self.cp(hb[:, i, 15 + LT:30 + LT], hht[:, i, 16:31], ["hht"], ["hb"])
            self.cp(pbf[:, i, 0:1], pht[:, i, 15:16], ["pht"], ["pb"])
            self.cp(pbf[:, i, 1 + LT:2 + LT], pht[:, i, 16:17], ["pht"], ["pb"])
        for i in range(2):
            for (src, skey, n0, d0) in ((pbf, "pb", LT, 0), (pbc, "pbc", CT, LT)):
                a_ = acc[:, i, d0:d0 + n0]
                self.ts(a_, src[:, i, 0:n0], self.tcol(T, 'scw', i * 3), None, ALU.mult, None, [skey, T['name']], ["acc"])
                for tap in range(1, 3):
                    self.stt(a_, src[:, i, tap:tap + n0], self.tcol(T, 'scw', i * 3 + tap), a_, ALU.mult, ALU.add,
                             [skey, "acc", T['name']], ["acc"])
            self.tt(sco[:, i, :], acc[:, i, :], gb[:, i, :], ALU.mult, ["acc", "gb"], ["sco"])
        itc = 0
        for i in range(2):
            for (src, skey, blks, d0) in ((hb, "hb", LBLK, 0), (hbc, "hbc", [(0, CT, 'c')], LT)):
                for (t0, n, ty) in blks:
                    pb = 2 + itc % 2
                    itc += 1
                    for tap in range(31):
                        self.mm(self.ps(pb, n), dg[:, i * 31 + tap, :], src[:, i, t0 + tap:t0 + tap + n], tap == 0, tap == 30,
                                ["dg", skey], ["ps%d" % pb], tap == 30)
                    self.act(acc[:, i, d0 + t0:d0 + t0 + n], self.ps(pb, n), AF.Identity, ["ps%d" % pb, T['name']], ["acc"],
                             bias=self.tcol(T, 'cb', i))
        kb.barrier()
        lt_ = [wc_raw[:, i_ * 512:(i_ + 1) * 512] for i_ in range(4)]
        for bi, (t0, n, ty) in enumerate(BLK):
            for i in range(2):
                self.mm(self.ps(0, n), self.c_onesf, acc[:, i, t0:t0 + n], i == 0, i == 1, ["acc", "const"], ["ps0"], i == 1)
            for i in range(2):
                self.act(lt_[i][:, 0:n], acc[:, i, t0:t0 + n], AF.Square, ["acc"], ["lt%d" % i])
                self.mm(self.ps(1, n), self.c_onesf, lt_[i][:, 0:n], i == 0, i == 1, ["lt%d" % i, "const"], ["ps1"], i == 1)
            self.act(lt_[2][:, 0:n], self.ps(0, n), AF.Square, ["ps0"], ["lt2"])
            self.tt(lt_[3][:, 0:n], self.ps(1, n), lt_[2][:, 0:n], ALU.subtract, ["ps1", "lt2"], ["lt3"])
            self.rsqrt(lt_[3][:, 0:n], lt_[3][:, 0:n], 1.0, ["lt3"], ["lt3"])
            for i in range(2):
                self.tt(lt_[i][:, 0:n], acc[:, i, t0:t0 + n], self.ps(0, n), ALU.subtract, ["acc", "ps0"], ["lt%d" % i])
                self.tt(lt_[i][:, 0:n], lt_[i][:, 0:n], lt_[3][:, 0:n], ALU.mult, ["lt%d" % i, "lt3"], ["lt%d" % i])
                self.act(convo[:, i, t0:t0 + n], lt_[i][:, 0:n], AF.Silu, ["lt%d" % i, T['name']], ["convo"],
                         bias=self.tcol(T, 'lb', i), scale=self.tcol(T, 'lg', i))
        kb.barrier()
        ar.release(m1)
        self.debug("convo", convo, [], [128, 2, NT], BF16)
        self.debug("sco", sco, [], [128, 2, NT], BF16)
        m1 = ar.mark()
        xoff[0] = 0
        NKC = 2 + L // 128
        KT = xbf(128 * NKC)
        V = xbf(NKC * 2 * 65).rearrange("p (n a c) -> p n a c", n=NKC, a=2)
        self.memset(V[:, 2:, :, 64:65], 1.0, ["V"])
        self.cp(KT[:, 0:CT], KTc, ["KTc"], ["KT"])
        self.cp(V[:, 0:2], Vc, ["Vc", "Vc1"], ["V"])
        for r in range(8):
            kb.dma('sp', KT[:, CT + r * LT:CT + (r + 1) * LT], gk[:, r, :], reads=["g_k"],
                   writes=["KT"], sem="KTd", group=True)
        vsrc = vf_d.rearrange("(n p) c -> p n c", p=128)
        for qd in range(4):
            for a in range(2):
                kb.dma('pool', V[:, 2 + qd * 32:2 + (qd + 1) * 32, a, 0:64], vsrc[:, qd * 32:(qd + 1) * 32, a * 64:(a + 1) * 64],
                       reads=["g_vf"], writes=["V"], sem="Vd", group=True)
        pt = [ar.bf(1024).rearrange("p (a n) -> p a n", a=2) for _ in range(3)]
        ou = [ar.f32(512), ar.f32(512)]
        rden = ar.f32(512)
        obB = [ar.bf(512), ar.bf(512)]
        itp = 0
        fin = 0
        for (t0, n, ty) in MBLK:
            chunks = list(range(NKC)) if ty == 'x' else [0, 1]
            bi = t0 // 512
            for c in range(4):
                def qk(kc, sb):
                    self.mm(self.ps(2 * sb, n), KT[0:64, kc * 128:(kc + 1) * 128], qT[0:64, c, t0:t0 + n], True, True,
                            ["KT", "qT%d_%d" % (c, bi)], ["S%d" % sb], False)
                    self.mm(self.ps(2 * sb + 1, n), KT[64:128, kc * 128:(kc + 1) * 128], qT[64:128, c, t0:t0 + n], True, True,
                            ["KT", "qT%d_%d" % (c, bi)], ["S%d" % sb], True)
                LA = 2
                for i_ in range(min(LA, len(chunks))):
                    qk(chunks[i_], i_ % 3)
                for ci, kc in enumerate(chunks):
                    sb = ci % 3
                    if ci + LA < len(chunks):
                        qk(chunks[ci + LA], (ci + LA) % 3)
                    P = pt[itp % 3]
                    pk = "P%d" % (itp % 3)
                    itp += 1
                    sin_ = self.pst[:, 2 * sb * 512:2 * sb * 512 + 1024].rearrange("p (a n) -> p a n", a=2)[:, :, 0:n]
                    self.act(P[:, :, 0:n], sin_, AF.Exp, ["S%d" % sb], [pk], scale=0.125)
                    first, last = ci == 0, ci == len(chunks) - 1
                    self.mm(self.ps(6, n, 65), V[:, kc, 0, :], P[:, 0, 0:n], first, last, ["V", pk], ["acc4"], False)
                    self.mm(self.ps(7, n, 65), V[:, kc, 1, :], P[:, 1, 0:n], first, last, ["V", pk], ["acc4"], True)
                for hh in range(2):
                    pa = 6 + hh
                    f = fin % 2
                    fin += 1
                    self.act(ou[f][0:65, 0:n], self.ps(pa, n, 65), AF.Copy, ["acc4"], ["ou%d" % f])
                    self.act(rden[64:65, 0:n], ou[f][64:65, 0:n], AF.Ln, ["ou%d" % f], ["rden"])
                    self.act(rden[64:65, 0:n], rden[64:65, 0:n], AF.Exp, ["rden"], ["rden"], scale=-1.0)
                    self.mm(self.ps(4, n, 64), one1[64:65, 0:64], rden[64:65, 0:n], True, True, ["rden", "const"], ["S2"], True)
                    if hh == 0:
                        self.tt(attnT[0:64, c, t0:t0 + n], ou[f][0:64, 0:n], self.ps(4, n, 64), ALU.mult,
                                ["ou%d" % f, "S2"], ["attnT"])
                    else:
                        self.tt(obB[f][0:64, 0:n], ou[f][0:64, 0:n], self.ps(4, n, 64), ALU.mult,
                                ["ou%d" % f, "S2"], ["obB%d" % f])
                        kb.dma('sp', attnT[64:128, c, t0:t0 + n], obB[f][0:64, 0:n], reads=["obB%d" % f], writes=["attnT"],
                               sem="obB%d" % f)
        kb.barrier()
        ar.release(mP)
        self.debug("attnT", attnT, [], [128, 4, NT], BF16)
        m1 = ar.mark()
        xoff[0] = 0
        Fg = xbf(128 * 64).rearrange("p (n c) -> p n c", n=128)
        Asb = xbf(2 * 128 * 64).rearrange("p (k c r) -> p k c r", k=128, c=64)
        MT = ar.bf(128 * 2 * 32).rearrange("p (k s n) -> p k s n", k=128, s=2)
        CS = xbf(256)
        Xs = xbf(2 * 2048).rearrange("p (r n) -> p r n", r=2)
        C256 = xbf(2 * 2 * 256).rearrange("p (h r n) -> p h r n", h=2, r=2)
        CBSB = xbf(2 * 64).rearrange("p (r n) -> p r n", r=2)
        kb.dma('pool', MT, mt_d, writes=["ftab"], sem="ftab", group=True)
        kb.dma('pool', CS, cs_d, writes=["ftab"], sem="ftab", group=True)
        kb.dma('pool', C256, c256_d, writes=["ftab"], sem="ftab", group=True)
        kb.dma('pool', CBSB[0:64], cbsb_d, writes=["ftab"], sem="ftab", group=True)
        fsrc = vf_d.rearrange("(a b) c -> a b c", b=128)
        ev = 0
        for g in range(4):
            for qd in range(4):
                kb.dma('pool', Fg[:, qd * 32:(qd + 1) * 32, :], fsrc[:, qd * 32:(qd + 1) * 32, 128 + g * 64:128 + (g + 1) * 64],
                       reads=["g_vf"], writes=["Fg"], sem="Fg", group=True)
            for c0 in range(0, 64, 2):
                pb = ((c0 // 2) % 2) * 2
                k0, k1_ = "ps%d" % pb, "ps%d" % (pb + 1)
                self.mm(self.ps(pb, 256), Fg[:, :, c0], CS, True, True, ["Fg", "ftab"], [k0], True)
                self.mm(self.ps(pb + 1, 256), Fg[:, :, c0 + 1], CS, True, True, ["Fg", "ftab"], [k1_], True)
                for b_ in range(2):
                    src = self.ps(pb + b_, 256).rearrange("p (r k) -> p k r", r=2)
                    dst = Asb[:, :, c0 + b_, :]
                    kk = "ps%d" % (pb + b_)
                    if b_ == 0:
                        self.act(dst, src, AF.Copy, [kk], ["Asb"])
                    else:
                        self.cp(dst, src, [kk], ["Asb"])
            for k1h in range(2):
                for k1l in range(64):
                    k1 = k1h * 64 + k1l
                    col = k1l * 32
                    o_ = self.pst[0:64, col:col + 32]
                    lastk = k1l == 63
                    self.mm(o_, Asb[:, k1, :, 0], MT[:, k1, 0, :], True, False, ["Asb", "ftab"], ["ps0", "ps1", "ps2", "ps3"], False)
                    self.mm(o_, Asb[:, k1, :, 1], MT[:, k1, 1, :], False, True, ["Asb", "ftab"], ["ps0", "ps1", "ps2", "ps3"], lastk)
                srcv = self.pst[0:64, 0:2048].rearrange("p (k r a) -> p k r a", r=2, a=16)
                for r in range(2):
                    src = srcv[:, :, r, :]
                    dst = Xs[0:64, r, :].rearrange("p (a k) -> p k a", k=128)[:, k1h * 64:(k1h + 1) * 64, :]
                    if r == 0:
                        self.act(dst, src, AF.Copy, ["ps0", "ps1", "ps2", "ps3"], ["Xs"])
                    else:
                        self.cp(dst, src, ["ps0", "ps1", "ps2", "ps3"], ["Xs"])
            ro = (g % 2) * 64
            for tb in range(4):
                pb = 4 + tb % 2
                o_ = self.pst[ro:ro + 64, pb * 512:pb * 512 + 512]
                self.mm(o_, CBSB[0:64, 0, :], Xs[0:64, 0, tb * 512:(tb + 1) * 512], True, False, ["Xs", "ftab"], ["ps%d" % pb], False)
                self.mm(o_, CBSB[0:64, 1, :], Xs[0:64, 1, tb * 512:(tb + 1) * 512], False, True, ["Xs", "ftab"], ["ps%d" % pb], True)
                self.act(Y[ro:ro + 64, g // 2, tb * 512:(tb + 1) * 512], o_, AF.Copy, ["ps%d" % pb], ["Y"])
            xc = [self.pst[0:64, 6 * 512:6 * 512 + 256], self.pst[0:64, 7 * 512:7 * 512 + 256]]
            for r in range(2):
                for nh in range(2):
                    self.mm(xc[r], Fc[:, nh, g * 64:(g + 1) * 64], C256[:, nh, r, :], nh == 0, nh == 1, ["Fc", "ftab"],
                            ["ps%d" % (6 + r)], nh == 1)
                self.cp(Xs[0:64, r, 0:256], xc[r], ["ps%d" % (6 + r)], ["Xs"])
            o_ = self.pst[ro:ro + 64, 4 * 512:4 * 512 + 256]
            self.mm(o_, CBSB[0:64, 0, :], Xs[0:64, 0, 0:256], True, False, ["Xs", "ftab"], ["ps4"], False)
            self.mm(o_, CBSB[0:64, 1, :], Xs[0:64, 1, 0:256], False, True, ["Xs", "ftab"], ["ps4"], True)
            self.act(Y[ro:ro + 64, g // 2, LT:NT], o_, AF.Copy, ["ps4"], ["Y"])
        kb.barrier()
        ar.release(m1)
        self.debug("Y", Y, [], [128, 2, NT], BF16)
        m1 = ar.mark()
        xoff[0] = 0
        u = xbf(8 * NT).rearrange("p (k t) -> p k t", k=8)
        wgb = [xbf(4 * 8 * 128).rearrange("p (b k n) -> p b k n", b=4, k=8) for _ in range(2)]
        wbrb = [xbf(10 * 128).rearrange("p (k n) -> p k n", k=10) for _ in range(2)]
        merged = ar.bf(8 * NT).rearrange("p (k t) -> p k t", k=8)
        for bi, (t0, n, ty) in enumerate(MBLK):
            kb.dma('sp', u[:, :, t0:t0 + n], self.udram.ap()[:, :, t0:t0 + n], reads=["udram"], writes=["u%d" % bi],
                   sem="uin%d" % bi)
        sg = [ar.f32(512), ar.f32(512)]
        macc = [ar.f32(512), ar.f32(512)]
        mt2 = [ar.f32(512), ar.f32(512)]
        brs = [(attnT, 4, 0, "attnT"), (Y, 2, 4, "Y"), (convo, 2, 6, "convo"), (sco, 2, 8, "sco")]
        it = 0
        for o in range(8):
            sl = o % 2
            kb.dma('pool', wgb[sl], wg_d[o], writes=["wg%d" % sl], sem="wg%d" % sl)
            kb.dma('pool', wbrb[sl], wbr_d[o], writes=["wbr%d" % sl], sem="wbr%d" % sl)
            for bi, (t0, n, ty) in enumerate(MBLK):
                ma = macc[bi % 2]
                mk = "macc%d" % (bi % 2)
                for br, (src, nk, k0, skey) in enumerate(brs):
                    pg = it % 2
                    py = 2 + it % 2
                    si = it % 2
                    it += 1
                    for k in range(8):
                        self.mm(self.ps(pg, n), wgb[sl][:, br, k, :], u[:, k, t0:t0 + n], k == 0, k == 7,
                                ["wg%d" % sl, "u%d" % bi], ["ps%d" % pg], k == 7)
                    for k in range(nk):
                        self.mm(self.ps(py, n), wbrb[sl][:, k0 + k, :], src[:, k, t0:t0 + n], k == 0, k == nk - 1,
                                ["wbr%d" % sl, skey], ["ps%d" % py], k == nk - 1)
                    self.act(sg[si][:, 0:n], self.ps(pg, n), AF.Sigmoid, ["ps%d" % pg, T['name']], ["sg%d" % si],
                             bias=self.tcol(T, 'bg', br * 8 + o))
                    if br == 0:
                        self.tt(ma[:, 0:n], sg[si][:, 0:n], self.ps(py, n), ALU.mult, ["sg%d" % si, "ps%d" % py], [mk])
                    else:
                        self.tt(mt2[si][:, 0:n], sg[si][:, 0:n], self.ps(py, n), ALU.mult, ["sg%d" % si, "ps%d" % py], ["mt%d" % si])
                        if br < 3:
                            self.tt(ma[:, 0:n], ma[:, 0:n], mt2[si][:, 0:n], ALU.add, [mk, "mt%d" % si], [mk])
                        else:
                            self.tt(merged[:, o, t0:t0 + n], ma[:, 0:n], mt2[si][:, 0:n], ALU.add, [mk, "mt%d" % si], ["merged%d" % bi])
        kb.barrier()
        self.debug("merged", merged, [], [128, 8, NT], BF16)
        xT = self.xT
        for bi, (t0, n, ty) in enumerate(MBLK):
            kb.dma('sp', xT[:, :, t0:t0 + n], xin[:, :, t0:t0 + n], reads=["xdram"], writes=["x%d" % bi], sem="xin%d" % bi)
        wob = [ar.bf(8 * 128).rearrange("p (k n) -> p k n", k=8) for _ in range(2)]
        it = 0
        for o in range(8):
            sl = o % 2
            kb.dma('pool', wob[sl], wo_d[o], writes=["wo%d" % sl], sem="wo%d" % sl)
            for bi, (t0, n, ty) in enumerate(MBLK):
                py = 4 + it % 2
                it += 1
                for k in range(8):
                    self.mm(self.ps(py, n), wob[sl][:, k, :], merged[:, k, t0:t0 + n], k == 0, k == 7,
                            ["wo%d" % sl, "merged%d" % bi], ["ps%d" % py], k == 7)
                self.stt(xT[:, o, t0:t0 + n], self.ps(py, n), self.tG(T, ty, 1, o), xT[:, o, t0:t0 + n],
                         ALU.mult, ALU.add, ["ps%d" % py, "x%d" % bi, T['name'] + "_der"], ["x%d" % bi])
        kb.barrier()
        ar.release(m0)
        self.debug("x_mix", self.xT, ["x%d" % i for i in range(5)], [128, 8, NT])
        self.ffn(T, 2, w13_d, w2_d, "f2", MBLK)
        self.debug("x_ffn2", self.xT, ["x%d" % i for i in range(5)], [128, 8, NT])
        kb.barrier()
        ar.release(mB)

    def final_norm(self):
        kb, ar = self.kb, self.ar
        g_d = self.din("fng", [128, 8])
        gt = ar.f32(8)
        kb.dma('sp', gt, g_d, writes=["fng"], sem="fng")
        y_d = self.dout("yT_out", [128, 8, LT])
        tmp = self.norm_tmps()
        yo = [ar.f32(512), ar.f32(512)]
        xT = self.xT
        it = 0
        for bi, (t0, n, ty) in enumerate(LBLK):
            sq, rstd = tmp['sq'], tmp['rstd']
            for k in range(8):
                i = k % 2
                self.act(sq[i][:, 0:n], xT[:, k, t0:t0 + n], AF.Square, ["x%d" % bi], ["sq%d" % i])
                self.mm(self.ps(7, n), self.c_ones, sq[i][:, 0:n], k == 0, k == 7, ["sq%d" % i, "const"], ["ps7"], True)
            self.rsqrt(rstd[:, 0:n], self.ps(7, n), 1.0 / D, ["ps7"], ["rstd"])
            for k in range(8):
                i = it % 2
                it += 1
                self.stt(yo[i][:, 0:n], xT[:, k, t0:t0 + n], gt[:, k:k + 1], rstd[:, 0:n], ALU.mult, ALU.mult,
                         ["x%d" % bi, "rstd", "fng"], ["yo%d" % i])
                kb.dma('sp', y_d[:, k, t0:t0 + n], yo[i][:, 0:n], reads=["yo%d" % i], sem="yout%d" % i)

    def build(self):
        kb, ar, nc = self.kb, self.ar, self.nc
        self.load_consts()
        self.rope_d = self.din("rope", [2, 128, LT])
        self.hm_d = self.din("hmask", [128, NH])
        self.hsel_d = self.din("hsel", [128, 16])
        self.cs_d = self.din("cs128", [128, 256])
        self.mt_d = self.din("mtab", [128, 128, 2, 32])
        self.c256_d = self.din("c256", [128, 2, 2, 256])
        self.cbsb_d = self.din("cbsb", [64, 2, 64])
        self.xdram = nc.dram_tensor("xdram", [128, 8, NT], F32)
        self.b_k = nc.dram_tensor("b_k", [128, LT], BF16)
        self.g_k = nc.dram_tensor("g_k", [NCORE * 128, LT], BF16)
        self.b_vf = nc.dram_tensor("b_vf", [LT, 384], BF16)
        self.g_vf = nc.dram_tensor("g_vf", [NCORE * LT, 384], BF16)
        self.b_xe = nc.dram_tensor("b_xe", [128, 8 * NH], BF16)
        self.g_xe = nc.dram_tensor("g_xe", [NCORE * 128, 8 * NH], BF16)
        self.udram = nc.dram_tensor("udram", [128, 8, NT], BF16)
        self.b_mod = nc.dram_tensor("b_mod", [128, 72], F32)
        self.g_mod = nc.dram_tensor("g_mod", [NCORE * 128, 72], F32)
        self.xT = ar.f32(8 * NT).rearrange("p (k t) -> p k t", k=8)
        xin = self.din("xT_in", [128, 8, NT])
        self.mod_phase()
        for bi, (t0, n, ty) in enumerate(BLK):
            kb.dma('sp', self.xT[:, :, t0:t0 + n], xin[:, :, t0:t0 + n], writes=["x%d" % bi], sem="xin%d" % bi)
        self.part_a(0)
        for l in range(DEPTH):
            self.part_b(l)
            if l < DEPTH - 1:
                self.part_a(l + 1)
            else:
                self.final_norm()


def _bf_consts():
    ones = np.ones((128, 128), np.float32)
    bd = np.zeros((128, 128), np.float32)
    bd[:64, :64] = 1
    bd[64:, 64:] = 1
    perm = np.zeros((128, 128), np.float32)
    for d in range(128):
        hd = d % 64
        half = (hd % 32) // 16
        partner = d + 16 if half == 0 else d - 16
        perm[partner, d] = 1.0
    return np.ascontiguousarray(np.stack([ones, bd, perm, np.eye(128, dtype=np.float32)], 1))


def _rope_tables(core):
    pos = np.arange(core * LT, (core + 1) * LT)
    row = (pos // 64).astype(np.float64)
    col = (pos % 64).astype(np.float64)
    inv = 10000.0 ** (-np.arange(0, 32, 2, dtype=np.float64) / 32)
    cos = np.zeros((128, LT), np.float32)
    sin = np.zeros((128, LT), np.float32)
    for d in range(128):
        hd = d % 64
        axis = hd // 32
        half = (hd % 32) // 16
        f = hd % 16
        ang = (row if axis == 0 else col) * inv[f]
        cos[d] = np.cos(ang)
        sin[d] = np.sin(ang) * (-1.0 if half == 0 else 1.0)
    return np.stack([cos, sin], 0)


def _fm(w, cols):
    return np.ascontiguousarray(w.reshape(8, 128, w.shape[1])[:, :, cols].transpose(1, 0, 2))


def _layer_tab(l, modx, modc, I):
    tab = np.zeros((128, NTAB), np.float32)
    tab[:, TO['modx']:TO['modx'] + 72] = modx
    tab[:, TO['modc']:TO['modc'] + 72] = modc
    tab[:, TO['normg']:TO['normg'] + 24] = I['norm_g'][l].reshape(3, 8, 128).transpose(2, 0, 1).reshape(128, 24)
    tab[:, TO['bg']:TO['bg'] + 32] = I['b_gate'][l].reshape(4, 8, 128).transpose(2, 0, 1).reshape(128, 32)
    tab[:, TO['cw']:TO['cw'] + 62] = I['conv_dw_w'][l].reshape(31, 2, 128).transpose(2, 1, 0).reshape(128, 62)
    tab[:, TO['cb']:TO['cb'] + 2] = I['conv_dw_b'][l].reshape(2, 128).T
    tab[:, TO['lg']:TO['lg'] + 2] = I['conv_ln_g'][l].reshape(2, 128).T
    tab[:, TO['lb']:TO['lb'] + 2] = I['conv_ln_b'][l].reshape(2, 128).T
    tab[:, TO['scw']:TO['scw'] + 6] = I['sc_conv_w'][l].reshape(3, 2, 128).transpose(2, 1, 0).reshape(128, 6)
    tab[:, TO['qg']] = np.tile(I['q_norm_g'][l], 2)
    tab[:, TO['kg']] = np.tile(I['k_norm_g'][l], 2)
    return tab


def _w13p(w13):
    a = w13[:, :DFF].reshape(8, 128, NJ, 128)
    b = w13[:, DFF:].reshape(8, 128, NJ, 128)
    ab = np.concatenate([a, b], axis=3)
    return np.ascontiguousarray(ab.transpose(2, 1, 0, 3))


def _a_inputs(l, I):
    w_in = I['w_in'][l]
    return dict(
        a_w13=_w13p(I['ffn_w13'][l, 0]),
        a_w2=np.ascontiguousarray(I['ffn_w2'][l, 0].reshape(NJ, 128, 8, 128)),
        a_wk=_fm(w_in, np.arange(512, 640)),
        a_wvf=_fm(w_in, np.arange(640, 1024)),
    )


_CACHE = {}


def _get_prog(kind, dbg=()):
    key = (kind, tuple(dbg))
    if key not in _CACHE:
        _CACHE[key] = Prog(kind, dbg)
    return _CACHE[key]


def _xT_from(x_core, ctx):
    a = np.concatenate([x_core, ctx], axis=0)
    return np.ascontiguousarray(a.T.reshape(8, 128, NT).transpose(1, 0, 2))


def _b_inputs(l, I):
    w_in = I['w_in'][l]
    pair = lambda c: np.concatenate([np.arange(c * 64, c * 64 + 64), np.arange((4 + c) * 64, (4 + c) * 64 + 64)])
    chunks = [pair(c) for c in range(4)] + [np.arange(512, 640)]
    chunks += [np.arange(1024 + i * 128, 1024 + (i + 1) * 128) for i in range(4)]
    chunks += [np.arange(1536 + i * 128, 1536 + (i + 1) * 128) for i in range(6)]
    wfm = np.ascontiguousarray(np.stack([_fm(w_in, c) for c in chunks], axis=1))
    g = w_in[:, 2304:].reshape(8, 128, 4, 8, 128)
    wg = np.ascontiguousarray(g.transpose(3, 1, 2, 0, 4))
    rows = np.stack([pair(c) for c in range(4)])
    a = I['w_attn_out'][l][rows]
    allw = np.concatenate([a, I['w_fnet'][l].reshape(2, 128, 1024), I['w_conv_out'][l].reshape(2, 128, 1024),
                           I['w_sc_out'][l].reshape(2, 128, 1024)], 0)
    wbr = np.ascontiguousarray(allw.reshape(10, 128, 8, 128).transpose(2, 1, 0, 3))
    wo = np.ascontiguousarray(I['w_o'][l].reshape(8, 128, 8, 128).transpose(2, 1, 0, 3))
    return dict(b_wfm=wfm, b_wvf=_fm(w_in, np.arange(640, 1024)), b_wg=wg, b_wbr=wbr, b_wo=wo,
                b_w13=_w13p(I['ffn_w13'][l, 1]),
                b_w2=np.ascontiguousarray(I['ffn_w2'][l, 1].reshape(NJ, 128, 8, 128)))


def _fnet_tables(core):
    n = np.arange(128, dtype=np.float64)
    a1 = 2 * np.pi * np.outer(n, n) / 128
    cs128 = np.concatenate([np.cos(a1), -np.sin(a1)], 1).astype(np.float32)
    k1 = np.arange(128)[:, None]
    k2 = np.arange(16)[None, :] + 16 * core
    kk = (k1 + 128 * k2).reshape(-1).astype(np.float64)
    a2 = 2 * np.pi * np.outer(n, kk) / L
    s = 1.0 / 128.0
    mc = (np.cos(a2) * s).reshape(128, 128, 16)
    ms = (np.sin(a2) * s).reshape(128, 128, 16)
    mtab = np.zeros((128, 128, 2, 32), np.float32)
    mtab[:, :, 0, 0:16] = mc
    mtab[:, :, 0, 16:32] = -ms
    mtab[:, :, 1, 0:16] = ms
    mtab[:, :, 1, 16:32] = mc
    nn = (np.arange(2)[None, :] * 128 + np.arange(128)[:, None]).astype(np.float64)
    a3 = 2 * np.pi * nn[:, :, None] * np.arange(256)[None, None, :] / 256
    c256 = (np.stack([np.cos(a3), -np.sin(a3)], 2) / 16.0).astype(np.float32)
    j = np.arange(64, dtype=np.float64)
    a4 = 2 * np.pi * np.outer(j, j) / 64
    cbsb = (np.stack([np.cos(a4), np.sin(a4)], 1) / 8.0).astype(np.float32)
    return dict(cs128=cs128, mtab=np.ascontiguousarray(mtab), c256=np.ascontiguousarray(c256),
                cbsb=np.ascontiguousarray(cbsb))


def make_in_maps(I):
    I = {k: np.asarray(v) for k, v in I.items()}
    cbf = _bf_consts()
    x = I['x'][0]
    ctx = I['ctx'][0]
    cT = np.ascontiguousarray(np.stack([I['c'].reshape(8, 128).T, I['c_ctx'].reshape(8, 128).T], axis=2).astype(np.float32))
    zero = np.zeros((128, 72), np.float32)
    shared = dict(cbf=cbf, cT=cT, fng=np.ascontiguousarray(I['final_norm_g'].reshape(8, 128).T))
    for l in range(DEPTH):
        shared["tab%d" % l] = _layer_tab(l, zero, zero, I)
        a = _a_inputs(l, I)
        bw = _b_inputs(l, I)
        shared["w13_%d_0" % l] = a['a_w13']
        shared["w2_%d_0" % l] = a['a_w2']
        shared["wk_%d" % l] = a['a_wk']
        shared["wvf_%d" % l] = a['a_wvf']
        shared["w13_%d_1" % l] = bw['b_w13']
        shared["w2_%d_1" % l] = bw['b_w2']
        shared["wfm_%d" % l] = bw['b_wfm']
        shared["wg_%d" % l] = bw['b_wg']
        shared["wbr_%d" % l] = bw['b_wbr']
        shared["wo_%d" % l] = bw['b_wo']
    in_maps = []
    for i in range(NCORE):
        m = dict(shared)
        ft = _fnet_tables(i)
        m.update(ft)
        m["rope"] = _rope_tables(i)
        m["xT_in"] = _xT_from(x[i * LT:(i + 1) * LT], ctx)
        cols = np.arange(i * 1152, (i + 1) * 1152)
        w = I['w_ada'][:, :, cols].reshape(4, 8, 128, 9, 128).transpose(0, 3, 2, 1, 4).reshape(36, 128, 8, 128)
        m["wada"] = np.ascontiguousarray(w)
        m["bada"] = np.ascontiguousarray(I['b_ada'][:, cols].reshape(4, 9, 128).transpose(2, 0, 1).reshape(128, 36))
        hm = np.zeros((128, NH), np.float32)
        hs = np.zeros((128, 16), np.float32)
        if i > 0:
            hm[:, 0:16] = 1.0
            hs[:, i - 1] = 1.0
        if i < NCORE - 1:
            hm[:, 16:32] = 1.0
            hs[:, 8 + i + 1] = 1.0
        m["hmask"] = hm
        m["hsel"] = hs
        in_maps.append(m)
    return in_maps


def kernel(**I):
    prog = _get_prog('F')
    in_maps = make_in_maps(I)
    res = run_bass_kernel_spmd(prog.nc, in_maps, core_ids=list(range(NCORE)))
    outs = res.results
    y = np.concatenate([o["yT_out"].transpose(2, 1, 0).reshape(LT, D) for o in outs], axis=0)
    return np.ascontiguousarray(y[None].astype(np.float32))
```
